# Optimizing a Trainium2 kernel written in Bass

```python
import jax, jax.numpy as jnp
from jax import lax
import numpy as np

D_MODEL = 2048
BATCH = 1
SEQ = 8192
DEPTH = 1

NORM_EPS = 1e-5
D_SSM = 2048
SSD_HEAD_DIM = 64
SSD_HEADS = D_SSM // SSD_HEAD_DIM
SSD_GROUPS = 4
SSD_HEADS_PER_GROUP = SSD_HEADS // SSD_GROUPS
SSD_STATE = 128
CONV_WIDTH = 4
CHUNK = 256
D_CONV_CH = D_SSM + 2 * SSD_GROUPS * SSD_STATE
D_POOL = 2048
POOL_WINDOWS = (2, 4, 8, 16)
POOL_GROUPS = len(POOL_WINDOWS)
POOL_GROUP_DIM = D_POOL // POOL_GROUPS
D_MIX = D_SSM + D_POOL
D_IN_PROJ = D_SSM + D_CONV_CH + SSD_HEADS + D_POOL
D_FF = -(-8 * D_MODEL // (3 * 256)) * 256

kernel_name = "hybrid_ssd_multiscale_pool_block"


def rms_norm(x, w):
    xf = x.astype(jnp.float32)
    y = xf * lax.rsqrt(jnp.mean(xf * xf, axis=-1, keepdims=True) + NORM_EPS)
    return (y * w.astype(jnp.float32)).astype(x.dtype)


def causal_depthwise_conv(u, w, b):
    ch = u.shape[-1]
    out = lax.conv_general_dilated(
        u, w.astype(u.dtype).reshape(CONV_WIDTH, 1, ch),
        window_strides=(1,), padding=[(CONV_WIDTH - 1, 0)],
        dimension_numbers=("NWC", "WIO", "NWC"), feature_group_count=ch)
    return out + b.astype(u.dtype)


def ssd_chunked_scan(xh, dt, a, b_mat, c_mat):
    bsz, seqlen = xh.shape[:2]
    pad = (-seqlen) % CHUNK
    if pad:
        padw = lambda t: jnp.pad(t, [(0, 0), (0, pad)] + [(0, 0)] * (t.ndim - 2))
        xh, dt, b_mat, c_mat = padw(xh), padw(dt), padw(b_mat), padw(c_mat)
    nc = (seqlen + pad) // CHUNK
    rs = lambda t: t.reshape((bsz, nc, CHUNK) + t.shape[2:])
    xh, dt, b_mat, c_mat = rs(xh), rs(dt), rs(b_mat), rs(c_mat)

    a_cum = jnp.cumsum(dt * a, axis=2)
    xdt = xh * dt[..., None]

    seg = a_cum[:, :, :, None] - a_cum[:, :, None, :]
    causal = jnp.tril(jnp.ones((CHUNK, CHUNK), dtype=bool))[:, :, None, None]
    decay = jnp.exp(jnp.where(causal, seg, -jnp.inf))
    cb = jnp.einsum("bclgn,bcsgn->bclsg", c_mat, b_mat)
    y_diag = jnp.einsum("bclsg,bclsgr,bcsgrp->bclgrp", cb, decay, xdt)

    decay_to_end = jnp.exp(a_cum[:, :, -1:] - a_cum)
    states = jnp.einsum("bclgn,bclgr,bclgrp->bcgrpn", b_mat, decay_to_end, xdt)
    chunk_decay = jnp.exp(a_cum[:, :, -1])

    def step(h, inp):
        s, dcy = inp
        return h * dcy[..., None, None] + s, h
    h0 = jnp.zeros(states.shape[:1] + states.shape[2:], jnp.float32)
    _, prev = lax.scan(step, h0, (jnp.moveaxis(states, 1, 0), jnp.moveaxis(chunk_decay, 1, 0)))
    prev = jnp.moveaxis(prev, 0, 1)

    y_off = jnp.einsum("bclgn,bcgrpn,bclgr->bclgrp", c_mat, prev, jnp.exp(a_cum))
    y = (y_diag + y_off).reshape((bsz, nc * CHUNK) + xh.shape[3:])
    return y[:, :seqlen]


def gated_group_rmsnorm(y, z, w):
    g = y * jax.nn.silu(z.astype(jnp.float32))
    shp = g.shape
    g = g.reshape(shp[:-1] + (SSD_GROUPS, shp[-1] // SSD_GROUPS))
    g = g * lax.rsqrt(jnp.mean(g * g, axis=-1, keepdims=True) + NORM_EPS)
    return g.reshape(shp) * w.astype(jnp.float32)


def multiscale_causal_pool(u):
    uf = u.astype(jnp.float32)
    seqlen = u.shape[1]
    cs = jnp.pad(jnp.cumsum(uf, axis=1), ((0, 0), (1, 0), (0, 0)))
    t = jnp.arange(seqlen)
    outs = []
    for gi, w in enumerate(POOL_WINDOWS):
        sl = slice(gi * POOL_GROUP_DIM, (gi + 1) * POOL_GROUP_DIM)
        csg = cs[..., sl]
        start = jnp.maximum(t + 1 - w, 0)
        win_sum = csg[:, 1:] - csg[:, start]
        count = jnp.minimum(t + 1, w).astype(jnp.float32)
        outs.append(win_sum / count[None, :, None] - uf[..., sl])
    return jnp.stack(outs, axis=2)


def hybrid_mixer(h, w_in, conv_w, conv_b, dt_bias, a_log, d_skip, ssd_norm_w,
                 pool_w, pool_scale, w_out):
    bsz, seqlen, _ = h.shape
    proj = h @ w_in.astype(h.dtype)
    z, xbc, dt_raw, u = jnp.split(
        proj, [D_SSM, D_SSM + D_CONV_CH, D_SSM + D_CONV_CH + SSD_HEADS], axis=-1)

    xbc = jax.nn.silu(causal_depthwise_conv(xbc, conv_w, conv_b)).astype(jnp.float32)
    xs, bm, cm = jnp.split(xbc, [D_SSM, D_SSM + SSD_GROUPS * SSD_STATE], axis=-1)
    dt = jax.nn.softplus(dt_raw.astype(jnp.float32) + dt_bias.astype(jnp.float32))
    a = -jnp.exp(a_log.astype(jnp.float32))
    xh = xs.reshape(bsz, seqlen, SSD_GROUPS, SSD_HEADS_PER_GROUP, SSD_HEAD_DIM)
    y = ssd_chunked_scan(
        xh,
        dt.reshape(bsz, seqlen, SSD_GROUPS, SSD_HEADS_PER_GROUP),
        a.reshape(SSD_GROUPS, SSD_HEADS_PER_GROUP),
        bm.reshape(bsz, seqlen, SSD_GROUPS, SSD_STATE),
        cm.reshape(bsz, seqlen, SSD_GROUPS, SSD_STATE))
    y = y + d_skip.astype(jnp.float32).reshape(SSD_GROUPS, SSD_HEADS_PER_GROUP)[..., None] * xh
    y_ssd = gated_group_rmsnorm(y.reshape(bsz, seqlen, D_SSM), z, ssd_norm_w)

    pooled = multiscale_causal_pool(u)
    y_pool = jnp.einsum("blgc,gcd->blgd", pooled, pool_w.astype(jnp.float32))
    y_pool = y_pool.reshape(bsz, seqlen, D_POOL) * pool_scale.astype(jnp.float32)

    mixed = jnp.concatenate([y_ssd, y_pool], axis=-1).astype(h.dtype)
    return mixed @ w_out.astype(h.dtype)


def swiglu(h, w_gate, w_up, w_down):
    return (jax.nn.silu(h @ w_gate.astype(h.dtype)) * (h @ w_up.astype(h.dtype))) @ w_down.astype(h.dtype)


def setup_inputs(seed: int = 0) -> dict:
    key = jax.random.key(seed)
    ks = jax.random.split(key, 20)
    f32 = jnp.float32
    nrm = lambda k, shp, s: jax.random.normal(k, shp, f32) * s
    dt_init = jnp.exp(jax.random.uniform(ks[5], (DEPTH, SSD_HEADS), f32,
                                         np.log(1e-3), np.log(1e-1)))
    return {
        "x": jax.random.normal(ks[0], (BATCH, SEQ, D_MODEL), f32),
        "attn_norm_w": 1.0 + nrm(ks[1], (DEPTH, D_MODEL), 0.02),
        "w_in": nrm(ks[2], (DEPTH, D_MODEL, D_IN_PROJ), D_MODEL ** -0.5),
        "conv_w": nrm(ks[3], (DEPTH, CONV_WIDTH, D_CONV_CH), CONV_WIDTH ** -0.5),
        "conv_b": nrm(ks[4], (DEPTH, D_CONV_CH), 0.02),
        "dt_bias": dt_init + jnp.log(-jnp.expm1(-dt_init)),
        "a_log": jnp.log(jax.random.uniform(ks[6], (DEPTH, SSD_HEADS), f32, 1.0, 16.0)),
        "d_skip": 1.0 + nrm(ks[7], (DEPTH, SSD_HEADS), 0.02),
        "ssd_norm_w": 1.0 + nrm(ks[8], (DEPTH, D_SSM), 0.02),
        "pool_w": nrm(ks[9], (DEPTH, POOL_GROUPS, POOL_GROUP_DIM, POOL_GROUP_DIM), POOL_GROUP_DIM ** -0.5),
        "pool_scale": 1.0 + nrm(ks[10], (DEPTH, D_POOL), 0.02),
        "w_out": nrm(ks[11], (DEPTH, D_MIX, D_MODEL), D_MIX ** -0.5),
        "ffn_norm_w": 1.0 + nrm(ks[12], (DEPTH, D_MODEL), 0.02),
        "w_gate": nrm(ks[13], (DEPTH, D_MODEL, D_FF), D_MODEL ** -0.5),
        "w_up": nrm(ks[14], (DEPTH, D_MODEL, D_FF), D_MODEL ** -0.5),
        "w_down": nrm(ks[15], (DEPTH, D_FF, D_MODEL), D_FF ** -0.5),
        "final_norm_w": 1.0 + nrm(ks[16], (D_MODEL,), 0.02),
    }


def reference(x, attn_norm_w, w_in, conv_w, conv_b, dt_bias, a_log, d_skip,
              ssd_norm_w, pool_w, pool_scale, w_out, ffn_norm_w, w_gate, w_up,
              w_down, final_norm_w):
    h = x
    for i in range(DEPTH):
        h = h + hybrid_mixer(rms_norm(h, attn_norm_w[i]), w_in[i], conv_w[i], conv_b[i],
                             dt_bias[i], a_log[i], d_skip[i], ssd_norm_w[i],
                             pool_w[i], pool_scale[i], w_out[i])
        h = h + swiglu(rms_norm(h, ffn_norm_w[i]), w_gate[i], w_up[i], w_down[i])
    return rms_norm(h, final_norm_w)
```

```python
from contextlib import ExitStack
import numpy as np
import concourse.bass as bass
import concourse.mybir as mybir
from concourse.bass_utils import run_bass_kernel_spmd

F32 = mybir.dt.float32
BF16 = mybir.dt.bfloat16
AF = mybir.ActivationFunctionType
ALU = mybir.AluOpType

NCORES = 8
D = 2048
SEQ = 8192
TOK = SEQ // NCORES
HT = 512
NPRE = 14
NHB = NPRE + 2
DFF = 5632
NFF = DFF // 128
EPS = 1e-5
C_Z, C_X, C_B, C_C, C_DT, C_U = 0, 2048, 4096, 4608, 5120, 5152

ENGS = ("pe", "act", "dve", "pool", "sp")

K_ONES, K_LT, K_T0, K_T1, K_CAUS, K_ID = 0, 128, 256, 512, 768, 1152
NCONST = 1280
P_NW1, P_NW2, P_NW3, P_SSDW, P_PSC, P_DSK = 0, 16, 32, 48, 64, 80
P_CW, P_CB, P_DTB, P_ALOG, P_VALID, P_ICNT = 96, 192, 216, 248, 280, 296
NPAR = 296 + 128


class Sched:
    def __init__(self, nc):
        self.nc = nc
        self.ops = []
        self.tok_w = {}
        self.tok_r = {}
        self.last = {e: None for e in ENGS}
        self.pending_barrier = {e: set() for e in ENGS}

    def op(self, eng, fn, r=(), w=(), dma=False):
        if getattr(self, "cap", None) is not None:
            self.cap[-1].append((eng, fn, tuple(r), tuple(w), dma))
            return None
        oid = len(self.ops)
        deps = {}
        for t in r:
            if t in self.tok_w:
                deps[self.tok_w[t]] = True
        for t in w:
            if t in self.tok_w:
                deps.setdefault(self.tok_w[t], False)
            for x in self.tok_r.get(t, ()):
                deps.setdefault(x, False)
        for x in self.pending_barrier[eng]:
            deps[x] = True
        self.pending_barrier[eng] = set()
        deps.pop(oid, None)
        best = {}
        out = {}
        for d, raw in deps.items():
            p = self.ops[d]
            if p["dma"]:
                out[d] = raw
                continue
            if p["eng"] == eng and eng == "pe" and not dma and not raw:
                continue
            pe_ = p["eng"]
            if pe_ not in best or d > best[pe_]:
                best[pe_] = d
        for pe_, d in best.items():
            out[d] = True
        self.ops.append(dict(eng=eng, fn=fn, deps=out, dma=dma, id=oid))
        for t in r:
            lst = self.tok_r.setdefault(t, [])
            if not dma:
                lst[:] = [x for x in lst if self.ops[x]["dma"] or self.ops[x]["eng"] != eng]
            lst.append(oid)
        for t in w:
            self.tok_w[t] = oid
            self.tok_r[t] = []
        self.last[eng] = oid
        return oid

    def begin_capture(self):
        self.cap = [[]]

    def new_block(self):
        if getattr(self, "cap", None) is not None and self.cap[-1]:
            self.cap.append([])

    def end_capture(self):
        blocks = [b for b in self.cap if b]
        self.cap = None
        return blocks

    def replay(self, blocks):
        for b in blocks:
            for (eng, fn, r, w, dma) in b:
                self.op(eng, fn, r=r, w=w, dma=dma)

    def interleave(self, A, B):
        na = sum(len(b) for b in A)
        nb = sum(len(b) for b in B)
        ia = ib = 0
        da = db = 0
        while ia < len(A) or ib < len(B):
            fa = da / na if na else 1.0
            fb = db / nb if nb else 1.0
            if ib >= len(B) or (ia < len(A) and fa <= fb):
                self.replay([A[ia]])
                da += len(A[ia])
                ia += 1
            else:
                self.replay([B[ib]])
                db += len(B[ib])
                ib += 1

    def barrier(self):
        lasts = {self.last[e] for e in ENGS if self.last[e] is not None}
        for e in ENGS:
            self.pending_barrier[e] |= lasts

    def emit(self, stack, final_wait_eng="sp", nds=32):
        nc = self.nc
        ops = self.ops
        esem = {e: stack.enter_context(nc.semaphore("s_" + e)) for e in ENGS}
        dsem = [stack.enter_context(nc.semaphore("d_%d" % i)) for i in range(nds)]
        dcount = [0] * nds
        dma_prev = [None] * nds
        npool = (nds * 2) // 3
        kk = {"pool": 0, "sp": 0}
        for o in ops:
            if o["dma"]:
                if o["eng"] == "pool":
                    s = kk["pool"] % npool
                    kk["pool"] += 1
                else:
                    s = npool + kk["sp"] % (nds - npool)
                    kk["sp"] += 1
                if dma_prev[s] is not None:
                    o["deps"].setdefault(dma_prev[s], True)
                o["dsem"] = s
                dma_prev[s] = o["id"]
        needed = set()
        for o in ops:
            needed |= set(o["deps"])
        ecount = {e: 0 for e in ENGS}
        ref = {}
        for o in ops:
            if o["dma"]:
                s = o["dsem"]
                dcount[s] += 16
                ref[o["id"]] = (dsem[s], dcount[s])
            elif o["id"] in needed:
                ecount[o["eng"]] += 1
                ref[o["id"]] = (esem[o["eng"]], ecount[o["eng"]])
        dma_ids = [o["id"] for o in ops if o["dma"]]
        per = {e: [o for o in ops if o["eng"] == e] for e in ENGS}
        block = stack.enter_context(nc.Block())
        self.nwaits = 0

        def mk(e):
            def body(engine):
                waited = {}
                for o in per[e]:
                    for d in sorted(o["deps"]):
                        sem, val = ref[d]
                        if waited.get(id(sem), 0) >= val:
                            continue
                        engine.wait_ge(sem, val)
                        self.nwaits += 1
                        waited[id(sem)] = val
                    meth, a, kw = o["fn"]
                    ins = getattr(engine, meth)(*a, **kw)
                    if o["id"] in ref:
                        sem, val = ref[o["id"]]
                        ins.then_inc(sem, 16 if o["dma"] else 1)
                if e == final_wait_eng:
                    for d in dma_ids:
                        sem, val = ref[d]
                        if waited.get(id(sem), 0) < val:
                            engine.wait_ge(sem, val)
                            waited[id(sem)] = val
            return body

        block.tensor(mk("pe"))
        block.scalar(mk("act"))
        block.vector(mk("dve"))
        block.gpsimd(mk("pool"))
        block.sync(mk("sp"))


class Arena:
    def __init__(self, tensor, nbytes):
        self.t = tensor
        self.nbytes = nbytes
        self.off = 0
        self.peak = 0

    def alloc(self, shape, dt):
        esz = 4 if dt == F32 else 2
        n = 1
        for s in shape:
            n *= s
        nb = n * esz
        off = (self.off + 3) // 4 * 4
        assert off + nb <= self.nbytes, ("SBUF arena overflow", off + nb, self.nbytes)
        self.off = off + nb
        self.peak = max(self.peak, self.off)
        v = self.t[:, off // 2: (off + nb) // 2]
        if dt == F32:
            v = v.bitcast(F32)
        if len(shape) == 2:
            v = v.rearrange("p (a b) -> p a b", b=shape[1])
        elif len(shape) == 3:
            v = v.rearrange("p (a b c) -> p a b c", b=shape[1], c=shape[2])
        return v

    def mark(self):
        return self.off

    def reset(self, m):
        self.off = m


def build_nc(npre=NPRE, do_own=True, do_ffn=True, dbg=False, lvl=9):
    nc = bass.Bass("TRN2", target_bir_lowering=False)
    dt_in = lambda name, shape: nc.dram_tensor(name, shape, F32, kind="ExternalInput").ap()
    xall = dt_in("xall", [NHB * D, HT])
    w_in = dt_in("w_in", [D, 7200])
    pool_w = dt_in("pool_w", [4 * 512, 512])
    w_out = dt_in("w_out", [4096, D])
    w_gate = dt_in("w_gate", [D, DFF])
    w_up = dt_in("w_up", [D, DFF])
    w_down = dt_in("w_down", [DFF, D])
    consts_d = dt_in("consts", [128, NCONST])
    params_d = dt_in("params", [128, NPAR])
    outT = nc.dram_tensor("outT", [D, TOK], F32, kind="ExternalOutput").ap()
    if dbg:
        dbg_S = nc.dram_tensor("dbg_S", [128, 2048], F32, kind="ExternalOutput").ap()
        dbg_R = nc.dram_tensor("dbg_R", [128, 2 * 16 * HT], F32, kind="ExternalOutput").ap()
        dbg_H = nc.dram_tensor("dbg_H", [128, 16 * HT], F32, kind="ExternalOutput").ap()

    xall_v = xall.rearrange("(h c p) t -> h p c t", c=16, p=128)
    w_in_v = w_in.rearrange("(kc p) n -> p kc n", p=128)
    w_out_v = w_out.rearrange("(kc p) n -> p kc n", p=128)
    pool_w_v = pool_w.rearrange("(g kc p) n -> g p kc n", g=4, p=128)
    w_gate_v = w_gate.rearrange("(kc p) n -> p kc n", p=128)
    w_up_v = w_up.rearrange("(kc p) n -> p kc n", p=128)
    w_down_v = w_down.rearrange("(kc p) n -> p kc n", p=128)
    outT_v = outT.rearrange("(c p) t -> p c t", p=128)

    with ExitStack() as st:
        TOTAL = 212800
        arena_t = st.enter_context(nc.sbuf_tensor("arena", [128, TOTAL // 2], BF16))
        pbank = [st.enter_context(nc.psum_tensor("pb%d" % i, [128, 512], F32)) for i in range(8)]
        A = Arena(arena_t, TOTAL)
        S = Sched(nc)

        def I(eng, meth, r=(), w=(), dma=False, a=(), **kw):
            S.op(eng, (meth, tuple(a), kw), r=r, w=w, dma=dma)

        cst = A.alloc([NCONST], F32)
        prm = A.alloc([NPAR], F32)
        ones_bf = A.alloc([128], BF16)
        ident_bf = A.alloc([128], BF16)
        LT_bf = A.alloc([128], BF16)
        a_b = A.alloc([32], F32)
        carry = A.alloc([24, 3], F32)
        M_S = A.mark()
        Sst = A.alloc([4, 512], F32)
        hn = A.alloc([16, 16 + HT], BF16)
        M_R = A.mark()
        R = [A.alloc([16, HT], F32), A.alloc([16, HT], F32)]
        PH = A.mark()

        onesf = cst[:, K_ONES:K_ONES + 128]
        LTf = cst[:, K_LT:K_LT + 128]
        T0 = cst[:, K_T0:K_T0 + 256]
        T1h = cst[:, K_T1 + 128:K_T1 + 256]
        caus = cst[:, K_CAUS:K_CAUS + 384]
        identf = cst[:, K_ID:K_ID + 128]

        def pcol(off, i=0, n=1):
            return prm[:, off + i: off + i + n]

        def dma_w(dst, src, tok, eng="pool"):
            I(eng, "dma_start", w=[tok], dma=True, out=dst, in_=src)

        mm_banks = [[0, 1, 2, 7]]
        mm_rr = [0]

        def next_mm():
            b = mm_banks[0]
            i = b[mm_rr[0] % len(b)]
            mm_rr[0] += 1
            return i

        pCB = pbank[3][:, 0:384]
        pdt = pbank[3][:, 384:512].rearrange("p (a b) -> p a b", b=32)
        pss = pbank[4][:, :]
        ptr = pbank[5][:, 0:256].bitcast(BF16).rearrange("p (a b) -> p a b", b=128)
        pY = pbank[5][:, 256:512]
        pYo = pbank[6][:, 0:256]
        pra = [pbank[6][:, 256 + q * 96: 256 + (q + 1) * 96].rearrange("p (a b) -> p a b", b=32) for q in range(2)]
        ph = pbank[6][:, 448:464]
        pcc = pbank[6][:, 464:468]
        pAbs = [(pbank[7][:, 0:256], "pb7"), (pbank[2][:, 0:256], "pb2")]
        pYs = [(pbank[5][:, 256:512], "pb5"), (pbank[3][:, 0:256], "pb3")]
        pYos = [(pbank[6][:, 0:256], "pb6"), (pbank[4][:, 256:512], "pb4")]

        I("sp", "dma_start", w=["cst"], dma=True, out=cst, in_=consts_d[:, :])
        I("sp", "dma_start", w=["prm"], dma=True, out=prm, in_=params_d[:, :])
        I("dve", "tensor_copy", r=["cst"], w=["ones_bf"], a=(ones_bf, onesf))
        I("dve", "tensor_copy", r=["cst"], w=["ident_bf"], a=(ident_bf, identf))
        I("dve", "tensor_copy", r=["cst"], w=["LT_bf"], a=(LT_bf, LTf))
        I("act", "activation", r=["prm"], w=["a_b"], out=a_b, in_=prm[:, P_ALOG:P_ALOG + 32], func=AF.Exp)
        I("dve", "tensor_scalar", r=["a_b"], w=["a_b"], out=a_b, in0=a_b, scalar1=-1.0, scalar2=None, op0=ALU.mult)
        I("pool", "memset", w=["carry%d" % m_ for m_ in range(24)], a=(carry, 0.0))
        I("pool", "memset", w=["S0", "S1", "S2", "S3"], a=(Sst, 0.0))
        I("pool", "memset", w=["hn"], a=(hn, 0.0))

        def norm_half(src, nw_off, dst_fn, tmp, dst_tok="hn", src_tok="R"):
            for c in range(16):
                sqb = tmp["sq"][c % 2]
                I("act", "activation", r=[src_tok], w=["sq%d" % (c % 2)], out=sqb, in_=src[:, c, :], func=AF.Square)
                I("pe", "matmul", r=["ones_bf", "sq%d" % (c % 2)], w=["pb4"], a=(pss, ones_bf, sqb),
                  start=(c == 0), stop=(c == 15))
            I("act", "activation", r=["pb4"], w=["sd"], out=tmp["sd"], in_=pss, func=AF.Sqrt, bias=EPS, scale=1.0 / D)
            I("dve", "reciprocal", r=["sd"], w=["rstd"], a=(tmp["rstd"], tmp["sd"]))
            for c in range(16):
                I("dve", "scalar_tensor_tensor", r=[src_tok, "prm", "rstd"], w=[dst_tok], out=dst_fn(c),
                  in0=src[:, c, :], scalar=pcol(nw_off, c), in1=tmp["rstd"], op0=ALU.mult, op1=ALU.mult)

        def mixer_phase_alloc(own):
            t = {}
            t["sq"] = [A.alloc([HT], BF16), A.alloc([HT], BF16)]
            t["sd"] = A.alloc([HT], F32)
            t["rstd"] = A.alloc([HT], F32)
            t["pre"] = [A.alloc([3 + HT + 1], F32) for _ in range(2 if own else 3)]
            t["acc"] = [A.alloc([HT], F32) for _ in range(2 if own else 3)]
            t["xc"] = [A.alloc([HT], BF16) for _ in range(0 if own else 3)]
            t["xtok"] = [A.alloc([4, 640], BF16) for _ in range(1 if own else 2)]
            t["xw"] = [A.alloc([2, 512], BF16) for _ in range(1 if own else 2)]
            for k in ("dtr", "dtabs", "dtl", "dt", "dtA", "lndt"):
                t[k] = A.alloc([4, 32], F32)
            t["RAraw"] = [A.alloc([3, 32], F32) for _ in range(2)]
            t["eR"] = [A.alloc([3, 32], F32) for _ in range(2)]
            t["wgt"] = [A.alloc([2, 32], F32) for _ in range(2)]
            t["negA"] = [A.alloc([2, 32], F32) for _ in range(2)]
            t["stmp"] = A.alloc([512], F32)
            t["d3"] = [A.alloc([4, 32], BF16) for _ in range(3)]
            t["rr"] = [A.alloc([4, 32], F32) for _ in range(2)]
            return t

        def dt_block(t, own, hb, WDT, hsrc=None, htok="hn"):
            hsrc = (lambda kc, tt: hn[:, kc, 16 + tt * 128: 16 + (tt + 1) * 128]) if hsrc is None else hsrc
            for tt in range(4):
                for kc in range(16):
                    I("pe", "matmul", r=[htok, "wdt"], w=["pb3"],
                      a=(pdt[:, tt, :], hsrc(kc, tt), WDT[:, kc, :]),
                      start=(kc == 0), stop=(kc == 15))
            bias_b = prm[:, P_DTB:P_DTB + 32].unsqueeze(1).to_broadcast([128, 4, 32])
            I("dve", "tensor_tensor", r=["pb3", "prm"], w=["dtr"], out=t["dtr"], in0=pdt, in1=bias_b, op=ALU.add)
            I("act", "activation", r=["dtr"], w=["dtabs"], out=t["dtabs"], in_=t["dtr"], func=AF.Abs)
            I("act", "activation", r=["dtabs"], w=["dtabs"], out=t["dtabs"], in_=t["dtabs"], func=AF.Exp, scale=-1.0)
            I("act", "activation", r=["dtabs"], w=["dtl"], out=t["dtl"], in_=t["dtabs"], func=AF.Ln, bias=1.0, scale=1.0)
            I("dve", "scalar_tensor_tensor", r=["dtr", "dtl"], w=["dt"], out=t["dt"], in0=t["dtr"], scalar=0.0,
              in1=t["dtl"], op0=ALU.max, op1=ALU.add)
            if not own:
                I("dve", "tensor_scalar", r=["dt", "prm"], w=["dt"], out=t["dt"], in0=t["dt"],
                  scalar1=pcol(P_VALID, hb), scalar2=None, op0=ALU.mult)
            a_bb = a_b.unsqueeze(1).to_broadcast([128, 4, 32])
            I("dve", "tensor_tensor", r=["dt", "a_b"], w=["dtA"], out=t["dtA"], in0=t["dt"], in1=a_bb, op=ALU.mult)
            if own:
                I("act", "activation", r=["dt"], w=["lndt"], out=t["lndt"], in_=t["dt"], func=AF.Ln)
            d3, rr = t["d3"], t["rr"]
            I("dve", "tensor_copy", r=["dtA"], w=["d3_0"], a=(d3[0], t["dtA"]))
            I("dve", "tensor_tensor", r=["dtA", "d3_0"], w=["rr0"], out=rr[0], in0=t["dtA"], in1=d3[0], op=ALU.subtract)
            I("dve", "tensor_copy", r=["rr0"], w=["d3_1"], a=(d3[1], rr[0]))
            I("dve", "tensor_tensor", r=["rr0", "d3_1"], w=["rr1"], out=rr[1], in0=rr[0], in1=d3[1], op=ALU.subtract)
            I("dve", "tensor_copy", r=["rr1"], w=["d3_2"], a=(d3[2], rr[1]))
            rd = ["LT_bf", "ones_bf", "d3_0", "d3_1", "d3_2"]
            for q in range(2):
                ta, tb = 2 * q, 2 * q + 1
                p_ = pra[q]
                tk = "pb6"
                for k in range(3):
                    I("pe", "matmul", r=rd, w=[tk], a=(p_[:, 0, :], LT_bf, d3[k][:, ta, :]), start=(k == 0), stop=False)
                    I("pe", "matmul", r=rd, w=[tk], a=(p_[:, 0, :], ones_bf, d3[k][:, tb, :]), start=False, stop=(k == 2))
                for k in range(3):
                    I("pe", "matmul", r=rd, w=[tk], a=(p_[:, 1, :], LT_bf, d3[k][:, tb, :]), start=(k == 0), stop=(k == 2))
                for k in range(3):
                    I("pe", "matmul", r=rd, w=[tk], a=(p_[:, 2, :], ones_bf, d3[k][:, ta, :]), start=(k == 0), stop=False)
                    I("pe", "matmul", r=rd, w=[tk], a=(p_[:, 2, :], ones_bf, d3[k][:, tb, :]), start=False, stop=(k == 2))
                I("act", "activation", r=[tk], w=["eR%d" % q], out=t["eR"][q], in_=p_, func=AF.Exp)
                I("dve", "tensor_tensor", r=["eR%d" % q, "dt"], w=["wgt%d" % q], out=t["wgt"][q],
                  in0=t["eR"][q][:, 0:2, :], in1=t["dt"][:, ta:ta + 2, :], op=ALU.mult)
                if own:
                    I("act", "copy", r=[tk], w=["RAraw%d" % q], a=(t["RAraw"][q], p_))
                    aend_b = t["RAraw"][q][:, 2:3, :].to_broadcast([128, 2, 32])
                    I("dve", "tensor_tensor", r=["RAraw%d" % q], w=["negA%d" % q], out=t["negA"][q],
                      in0=aend_b, in1=t["RAraw"][q][:, 0:2, :], op=ALU.subtract)

        def proj_chunk(wsrc, wtok, col0, rhs_fn, n, mmi, out=None, otok=None, htok="hn"):
            pm = pbank[mmi][:, 0:n] if out is None else out
            otok = otok or ("pb%d" % mmi)
            for kc in range(16):
                I("pe", "matmul", r=[wtok, htok], w=[otok], a=(pm, wsrc[:, kc, col0:col0 + 128], rhs_fn(kc)),
                  start=(kc == 0), stop=(kc == 15))
            return pm

        def conv_s1(t, pm, mmi, m, i):
            pre = t["pre"][i % len(t["pre"])]
            acc = t["acc"][i % len(t["acc"])]
            ptk, atk = "pre%d" % (i % len(t["pre"])), "acc%d" % (i % len(t["acc"]))
            I("act", "copy", r=["pb%d" % mmi], w=[ptk], a=(pre[:, 3:3 + HT], pm))
            I("act", "copy", r=["carry%d" % m], w=[ptk], a=(pre[:, 0:3], carry[:, m, :]))
            I("act", "activation", r=[ptk, "prm"], w=[atk], out=acc, in_=pre[:, 0:HT], func=AF.Identity,
              bias=pcol(P_CB, m), scale=pcol(P_CW, m * 4 + 0))

        def conv_s2(t, m, i):
            pre = t["pre"][i % len(t["pre"])]
            acc = t["acc"][i % len(t["acc"])]
            ptk, atk = "pre%d" % (i % len(t["pre"])), "acc%d" % (i % len(t["acc"]))
            for k in (1, 2, 3):
                I("dve", "scalar_tensor_tensor", r=[ptk, "prm", atk], w=[atk], out=acc, in0=pre[:, k:k + HT],
                  scalar=pcol(P_CW, m * 4 + k), in1=acc, op0=ALU.mult, op1=ALU.add)

        def conv_s3(t, m, dst, dst_tok, i):
            pre = t["pre"][i % len(t["pre"])]
            acc = t["acc"][i % len(t["acc"])]
            ptk, atk = "pre%d" % (i % len(t["pre"])), "acc%d" % (i % len(t["acc"]))
            I("act", "copy", r=[ptk], w=["carry%d" % m], a=(carry[:, m, :], pre[:, HT:HT + 3]))
            I("act", "activation", r=[atk], w=[dst_tok], out=dst, in_=acc, func=AF.Silu)

        def conv_silu(t, pm, mmi, m, dst, dst_tok, i):
            conv_s1(t, pm, mmi, m, i)
            conv_s2(t, m, i)
            conv_s3(t, m, dst, dst_tok, i)

        def transpose_pe(src, src_tok):
            for tt in range(4):
                I("pe", "transpose", r=[src_tok, "ident_bf"], w=["pb5"],
                  a=(ptr[:, tt, :], src[:, tt * 128:(tt + 1) * 128], ident_bf))

        def transpose_evac(xtok, xtok_tok, col0):
            I("act", "copy", r=["pb5"], w=[xtok_tok], a=(xtok[:, :, col0:col0 + 128], ptr))

        def transpose_to_tok(src, src_tok, xtok, xtok_tok, col0):
            for tt in range(4):
                I("pe", "transpose", r=[src_tok, "ident_bf"], w=["pb5"],
                  a=(ptr[:, tt, :], src[:, tt * 128:(tt + 1) * 128], ident_bf))
            I("act", "copy", r=["pb5"], w=[xtok_tok], a=(xtok[:, :, col0:col0 + 128], ptr))

        def state_update_a(t, g, q, xtok, xtok_tok, gi, eng="dve"):
            ta, tb = 2 * q, 2 * q + 1
            xw = t["xw"][gi % len(t["xw"])]
            xwt = "xw%d" % (gi % len(t["xw"]))
            for j, tt in enumerate((ta, tb)):
                wb = t["wgt"][q][:, j, 8 * g:8 * g + 8].unsqueeze(2).to_broadcast([128, 8, 64])
                I(eng, "tensor_tensor", r=[xtok_tok, "wgt%d" % q], w=[xwt],
                  out=xw[:, j, :].rearrange("p (a b) -> p a b", b=64),
                  in0=xtok[:, tt, 0:512].rearrange("p (a b) -> p a b", b=64), in1=wb, op=ALU.mult)

        def state_update_b(t, g, q, xtok, xtok_tok, gi, pS=None, pStok="pb4"):
            pS = pss if pS is None else pS
            ta, tb = 2 * q, 2 * q + 1
            xw = t["xw"][gi % len(t["xw"])]
            xwt = "xw%d" % (gi % len(t["xw"]))
            for j, tt in enumerate((ta, tb)):
                I("pe", "matmul", r=[xtok_tok, xwt], w=[pStok], a=(pS, xtok[:, tt, 512:640], xw[:, j, :]),
                  start=(j == 0), stop=(j == 1))
            decb = t["eR"][q][:, 2, 8 * g:8 * g + 8].unsqueeze(2).to_broadcast([128, 8, 64])
            Sg = Sst[:, g, :]
            I("dve", "tensor_tensor", r=["S%d" % g, "eR%d" % q], w=["stmp"],
              out=t["stmp"].rearrange("p (a b) -> p a b", b=64),
              in0=Sg.rearrange("p (a b) -> p a b", b=64), in1=decb, op=ALU.mult)
            I("dve", "tensor_tensor", r=["stmp", pStok], w=["S%d" % g], out=Sg, in0=t["stmp"], in1=pS, op=ALU.add)

        def state_update(t, g, q, xtok, xtok_tok, gi, pS=None, pStok="pb4"):
            state_update_a(t, g, q, xtok, xtok_tok, gi)
            state_update_b(t, g, q, xtok, xtok_tok, gi, pS=pS, pStok=pStok)

        def run_pipeline(items, stages, hooks=None):
            nst = len(stages)
            for n in range(len(items) + nst - 1):
                for s in reversed(range(nst)):
                    k = n - s
                    if 0 <= k < len(items):
                        stages[s](items[k])
                if hooks and n in hooks:
                    fs = hooks[n]
                    for f in (fs if isinstance(fs, list) else [fs]):
                        f()

        A.off = PH - 16 * HT * 4
        WRES = A.alloc([16, 2560], BF16)
        WDT_P = A.alloc([16, 32], BF16)
        hnB = A.alloc([16, HT], BF16)
        tP = mixer_phase_alloc(False)
        tP["nt"] = [A.alloc([HT], F32)]
        dma_w(WDT_P, w_in_v[:, :, C_DT:C_DT + 32], "wdt")
        for j in (0, 4, 1, 2, 3):
            dma_w(WRES[:, :, j * 512:(j + 1) * 512], w_in_v[:, :, C_X + j * 512: C_X + (j + 1) * 512], "wres%d" % j)

        ci = [0]
        gi = [0]
        rhs_main = lambda kc: hn[:, kc, 16:16 + HT]

        def norm_stats(src, tmp, src_tok):
            for c in range(16):
                sqb = tmp["sq"][c % 2]
                I("act", "activation", r=[src_tok], w=["sq%d" % (c % 2)], out=sqb, in_=src[:, c, :], func=AF.Square)
                I("pe", "matmul", r=["ones_bf", "sq%d" % (c % 2)], w=["pb4"], a=(pss, ones_bf, sqb),
                  start=(c == 0), stop=(c == 15))
            I("act", "activation", r=["pb4"], w=["sd"], out=tmp["sd"], in_=pss, func=AF.Sqrt, bias=EPS, scale=1.0 / D)
            I("dve", "reciprocal", r=["sd"], w=["rstd"], a=(tmp["rstd"], tmp["sd"]))

        def norm_apply(src, nw_off, dst_fn, tmp, dst_tok, src_tok, eng="dve"):
            if eng == "pool":
                for c in range(16):
                    nt = tmp["nt"][0]
                    ntk = "nt0"
                    I("pool", "tensor_scalar", r=[src_tok, "prm"], w=[ntk], out=nt, in0=src[:, c, :],
                      scalar1=pcol(nw_off, c), scalar2=None, op0=ALU.mult)
                    I("pool", "tensor_tensor", r=[ntk, "rstd"], w=[dst_tok], out=dst_fn(c), in0=nt, in1=tmp["rstd"], op=ALU.mult)
                return
            for c in range(16):
                I(eng, "scalar_tensor_tensor", r=[src_tok, "prm", "rstd"], w=[dst_tok], out=dst_fn(c),
                  in0=src[:, c, :], scalar=pcol(nw_off, c), in1=tmp["rstd"], op0=ALU.mult, op1=ALU.mult)

        pS_P = pbank[3][:, :]
        hbs = list(range(NPRE - npre, NPRE))
        NH = len(hbs)

        def hbuf(h):
            if (NH - 1 - h) % 2 == 0:
                return (lambda kc: hn[:, kc, 16:16 + HT]), "hn", (lambda c: hn[:, c, 16:16 + HT]), \
                       (lambda kc, tt: hn[:, kc, 16 + tt * 128: 16 + (tt + 1) * 128])
            return (lambda kc: hnB[:, kc, :]), "hnB", (lambda c: hnB[:, c, :]), \
                   (lambda kc, tt: hnB[:, kc, tt * 128:(tt + 1) * 128])

        info = []
        for h, hb in enumerate(hbs):
            rhs_fn, htok, _, _ = hbuf(h)
            for g in range(4):
                for i in range(5):
                    if i == 0:
                        cur_x = (tP["xtok"][gi[0] % 2], "xtok%d" % (gi[0] % 2), gi[0])
                        gi[0] += 1
                    if i < 4:
                        col0, m, dcol = g * 512 + i * 128, g * 4 + i, i * 128
                    else:
                        col0, m, dcol = 2048 + g * 128, 16 + g, 512
                    info.append(dict(h=h, g=g, i=i, col0=col0, m=m, dcol=dcol, xt=cur_x, ci=ci[0], xc=tP["xc"][ci[0] % 3],
                                     xct="xc%d" % (ci[0] % 3), rhs=rhs_fn, htok=htok))
                    ci[0] += 1

        def st0(it):
            it["mmi"] = next_mm()
            it["pm"] = proj_chunk(WRES, "wres%d" % (it["col0"] // 512), it["col0"], it["rhs"], HT, it["mmi"], htok=it["htok"])

        def st1(it):
            conv_s1(tP, it["pm"], it["mmi"], it["m"], it["ci"])

        def st2(it):
            conv_s2(tP, it["m"], it["ci"])

        def st3(it):
            conv_s3(tP, it["m"], it["xc"], it["xct"], it["ci"])

        def st4(it):
            transpose_pe(it["xc"], it["xct"])

        def st5(it):
            transpose_evac(it["xt"][0], it["xt"][1], it["dcol"])

        def st6(it):
            if it["i"] == 4 and lvl >= 4:
                xt, xk, gx = it["xt"]
                for q in range(2):
                    state_update_a(tP, it["g"], q, xt, xk, gx * 2 + q, eng="pool")

        def st7(it):
            pass

        def st8(it):
            if it["i"] == 4 and lvl >= 4:
                xt, xk, gx = it["xt"]
                for q in range(2):
                    state_update_b(tP, it["g"], q, xt, xk, gx * 2 + q, pS=pS_P, pStok="pb3")

        def load_x(h):
            I("sp", "dma_start", w=["R"], dma=True, out=R[0], in_=xall_v[hbs[h]])

        def do_stats(h):
            norm_stats(R[0], tP, "R")

        def do_apply(h):
            _, htok, dstf, _ = hbuf(h)
            norm_apply(R[0], P_NW1, dstf, tP, htok, "R", eng=("pool" if h > 0 else "dve"))
            if h + 1 < NH:
                load_x(h + 1)

        def do_dt(h):
            _, htok, _, dsrc = hbuf(h)
            dt_block(tP, False, hbs[h], WDT_P, hsrc=dsrc, htok=htok)

        hooks = {}
        if NH:
            load_x(0)
            do_stats(0)
            do_apply(0)
            do_dt(0)
            for h in range(NH - 1):
                bstep = 20 * h
                hooks.setdefault(bstep + 3, []).append(lambda h=h: do_stats(h + 1))
                hooks.setdefault(bstep + 8, []).append(lambda h=h: do_apply(h + 1))
                hooks.setdefault(bstep + 28, []).append(lambda h=h: do_dt(h + 1))
            run_pipeline(info, [st0, st1, st2, st3, st4, st5, st6, st7, st8], hooks=hooks)
        peakP = A.peak
        if dbg:
            I("sp", "dma_start", r=["S0", "S1", "S2", "S3"], dma=True, out=dbg_S[:, :], in_=Sst.rearrange("p a b -> p (a b)"))
            for c_ in range(16):
                I("act", "copy", r=["hn"], w=["R"], a=(R[0][:, c_, :], hn[:, c_, 16:16 + HT]))
            I("sp", "dma_start", r=["R"], dma=True, out=dbg_H[:, :], in_=R[0].rearrange("p a b -> p (a b)"))

        S.barrier()
        mm_banks[0] = [0, 1]
        A.reset(PH)
        WDT_M = A.alloc([16, 32], BF16)
        tM = mixer_phase_alloc(True)
        WS = [A.alloc([16, 512], BF16) for _ in range(2)]
        bt = A.alloc([HT], BF16)
        ctt = A.alloc([HT], BF16)
        xTg = A.alloc([4, HT], BF16)
        szg = A.alloc([4, HT], BF16)
        Dx0 = [A.alloc([256], F32) for _ in range(2)]
        Dx1 = [A.alloc([128], F32) for _ in range(2)]
        arg0 = [A.alloc([256], F32) for _ in range(2)]
        arg1 = [A.alloc([128], F32) for _ in range(2)]
        E0 = [tM["sd"][:, 0:256], tM["sd"][:, 256:512]]
        E1 = [tM["rstd"][:, 0:128], tM["rstd"][:, 128:256]]
        M1 = [tM["rstd"][:, 256:384].bitcast(BF16)[:, 0:128], tM["rstd"][:, 256:384].bitcast(BF16)[:, 128:256]]
        M0 = [tM["sq"][0][:, 0:256], tM["sq"][0][:, 256:512]]
        CBm0_ = A.alloc([384], F32)
        CBms = [(CBm0_, ["CBm"]), (tM["acc"][0][:, 0:384], ["acc0"])]
        eA = [A.alloc([256], F32) for _ in range(2)]
        Sbf0_ = A.alloc([512], BF16)
        Sbfs = [(Sbf0_, ["Sbf"]), (tM["acc"][1].bitcast(BF16)[:, 0:512], ["acc1"])]
        gbuf0_ = A.alloc([4, 256], F32)
        gb1a = tM["pre"][0][:, 0:512].rearrange("p (a b) -> p a b", b=256)
        gb1b = tM["pre"][1][:, 0:512].rearrange("p (a b) -> p a b", b=256)
        gbufs = [([gbuf0_[:, i, :] for i in range(4)], ["gbuf"]),
                 ([gb1a[:, 0, :], gb1a[:, 1, :], gb1b[:, 0, :], gb1b[:, 1, :]], ["pre0", "pre1"])]
        ytmp = [A.alloc([256], F32) for _ in range(2)]
        gsq = [A.alloc([256], BF16) for _ in range(2)]
        rstd_g = A.alloc([256], F32)
        sd_g = A.alloc([256], F32)
        yss = A.alloc([4, HT], BF16)
        ubuf = A.alloc([16 + HT], F32)
        pp = [A.alloc([16 + HT], F32) for _ in range(2)]
        pooled = A.alloc([4, HT], BF16)
        ypool = A.alloc([4, HT], BF16)
        dma_w(WDT_M, w_in_v[:, :, C_DT:C_DT + 32], "wdt")

        ws_rr = [0]

        def load_ws(srcs):
            i = ws_rr[0] % 2
            ws_rr[0] += 1
            ws = WS[i]
            tok = "ws%d" % i
            for dfn, src in srcs:
                dma_w(dfn(ws), src, tok)
            return ws, tok

        T0b = T0.unsqueeze(1).to_broadcast([128, 2, 256])
        T1b = T1h.unsqueeze(1).to_broadcast([128, 2, 128])
        pG = pbank[4][:, 0:256]

        for hh in (range(2) if do_own else ()):
            hb = NPRE + hh
            if hh == 1:
                S.barrier()
            Rh = R[hh]
            rt = "R%d" % hh
            I("sp", "dma_start", w=[rt], dma=True, out=Rh, in_=xall_v[hb])
            I("dve", "tensor_copy", r=["hn"], w=["hn"], a=(hn[:, :, 0:16], hn[:, :, HT:HT + 16]))
            norm_half(Rh, P_NW1, lambda c: hn[:, c, 16:16 + HT], tM, src_tok=rt)
            dt_block(tM, True, hb, WDT_M)
            xtok = tM["xtok"][0]
            xtk = "xtok0"
            hcnt = [0]

            def o_st0(it):
                it["mmi"] = next_mm()
                it["pm"] = proj_chunk(it["ws"], it["wt"], it["wcol"], rhs_main, HT, it["mmi"])
                if it["kind"] == "u":
                    proj_chunk(it["ws"], it["wt"], it["wcol"], lambda kc: hn[:, kc, 0:16], 16, 6, out=ph, otok="pb6")

            def o_st1(it):
                k_ = it["kind"]
                if k_ in ("x", "B", "C"):
                    conv_s1(tM, it["pm"], it["mmi"], it["m"], it["ci"])
                    conv_s2(tM, it["m"], it["ci"])
                elif k_ == "z":
                    I("act", "activation", r=["pb%d" % it["mmi"]], w=["szg"], out=szg[:, it["i"], :], in_=it["pm"], func=AF.Silu)
                else:
                    I("act", "copy", r=["pb%d" % it["mmi"]], w=["ubuf"], a=(ubuf[:, 16:16 + HT], it["pm"]))
                    I("act", "copy", r=["pb6"], w=["ubuf"], a=(ubuf[:, 0:16], ph))

            def o_st2(it):
                k_ = it["kind"]
                if k_ in ("x", "B", "C"):
                    conv_s3(tM, it["m"], it["dst"], it["dtok"], it["ci"])
                elif k_ == "u":
                    g_, i_ = it["g"], it["i"]
                    wwin = float(1 << (g_ + 1))
                    src, stok, lo = ubuf, "ubuf", 0
                    for k in range(g_ + 1):
                        sh = 1 << k
                        dst, dtok = pp[k % 2], "pp%d" % (k % 2)
                        lo2 = lo + sh
                        I("dve", "tensor_tensor", r=[stok], w=[dtok], out=dst[:, lo2:16 + HT], in0=src[:, lo2:16 + HT],
                          in1=src[:, lo2 - sh:16 + HT - sh], op=ALU.add)
                        src, stok, lo = dst, dtok, lo2
                    I("dve", "scalar_tensor_tensor", r=[stok, "ubuf"], w=["pooled"], out=pooled[:, i_, 16:HT],
                      in0=src[:, 32:16 + HT], scalar=1.0 / wwin, in1=ubuf[:, 32:16 + HT], op0=ALU.mult, op1=ALU.subtract)
                    oth = pp[(g_ + 1) % 2]
                    otk = "pp%d" % ((g_ + 1) % 2)
                    io = P_ICNT + it["hh"] * 64 + g_ * 16
                    I("dve", "tensor_tensor", r=[stok, "prm"], w=[otk], out=oth[:, 0:16], in0=src[:, 16:32],
                      in1=prm[:, io:io + 16], op=ALU.mult)
                    I("dve", "tensor_tensor", r=[otk, "ubuf"], w=["pooled"], out=pooled[:, i_, 0:16], in0=oth[:, 0:16],
                      in1=ubuf[:, 16:32], op=ALU.subtract)

            def o_st3(it):
                if it["kind"] in ("x", "B"):
                    transpose_pe(it["dst"], it["dtok"])

            def o_st4(it):
                if it["kind"] in ("x", "B"):
                    transpose_evac(xtok, xtk, it["dcol"])
                S.new_block()

            o_stages = [o_st0, o_st1, o_st2, o_st3, o_st4]

            def out_proj_part(kc0, src_t, src_tok):
                for dh in range(2):
                    ws, wt = load_ws([(lambda w_: w_[:, 0:8, :].rearrange("p (a b) c -> p a (b c)", b=2),
                                       w_out_v[:, kc0:kc0 + 4, dh * 1024:(dh + 1) * 1024])])
                    wv = ws[:, 0:8, :].rearrange("p (a b) c -> p a (b c)", b=2)
                    for dd in range(8):
                        d = dh * 8 + dd
                        mmi = next_mm()
                        pm = pbank[mmi][:, :]
                        for i in range(4):
                            I("pe", "matmul", r=[wt, src_tok], w=["pb%d" % mmi],
                              a=(pm, wv[:, i, dd * 128:(dd + 1) * 128], src_t[:, i, :]), start=(i == 0), stop=(i == 3))
                        I("dve", "tensor_tensor", r=[rt, "pb%d" % mmi], w=[rt], out=Rh[:, d, :], in0=Rh[:, d, :], in1=pm, op=ALU.add)
                        S.new_block()

            def ssd_group(g):
                for q in range(2):
                    ta, tb = 2 * q, 2 * q + 1
                    Sb, Sbt = Sbfs[q]
                    I("act", "copy", r=["S%d" % g], w=Sbt, a=(Sb, Sst[:, g, :]))
                    state_update(tM, g, q, xtok, xtk, gi[0])
                    gi[0] += 1
                    I("pe", "matmul", r=["bt", "ct"], w=["pb3"],
                      a=(pCB[:, 0:256], bt[:, ta * 128:(ta + 1) * 128], ctt[:, ta * 128: ta * 128 + 256]), start=True, stop=True)
                    I("pe", "matmul", r=["bt", "ct"], w=["pb3"],
                      a=(pCB[:, 256:384], bt[:, tb * 128:(tb + 1) * 128], ctt[:, tb * 128:(tb + 1) * 128]), start=True, stop=True)
                    cb, cbt = CBms[q]
                    I("dve", "tensor_tensor", r=["pb3", "cst"], w=cbt, out=cb, in0=pCB, in1=caus, op=ALU.mult)
                S.new_block()
                heads = []
                for q in range(2):
                    for h8 in range(8):
                        k = hcnt[0] % 2
                        pk = (hcnt[0] // 2) % 2
                        heads.append(dict(q=q, h=8 * g + h8, hp=h8 // 2, hx=h8 % 2, hc=h8 * 64, k=k, pk=pk))
                        hcnt[0] += 1

                def s0(it):
                    k, h, q = it["k"], it["h"], it["q"]
                    I("dve", "tensor_scalar", r=["cst", "dtA"], w=["Dx0_%d" % k], out=Dx0[k], in0=T0,
                      scalar1=tM["dtA"][:, 2 * q, h:h + 1], scalar2=None, op0=ALU.mult)
                    I("dve", "tensor_scalar", r=["cst", "dtA"], w=["Dx1_%d" % k], out=Dx1[k], in0=T1h,
                      scalar1=tM["dtA"][:, 2 * q + 1, h:h + 1], scalar2=None, op0=ALU.mult)

                def s1(it):
                    k = it["k"]
                    pA, pAt = pAbs[k]
                    I("pe", "matmul", r=["cst", "Dx0_%d" % k], w=[pAt], a=(pA, onesf, Dx0[k]), start=True, stop=False)
                    I("pe", "matmul", r=["cst", "Dx1_%d" % k], w=[pAt], a=(pA[:, 128:256], onesf, Dx1[k]), start=False, stop=True)

                def s2(it):
                    k, h, q = it["k"], it["h"], it["q"]
                    pA, pAt = pAbs[k]
                    negA = tM["negA"][q]
                    nq = "negA%d" % q
                    I("act", "activation", r=[pAt, nq], w=["arg0_%d" % k], out=arg0[k], in_=pA, func=AF.Relu,
                      bias=negA[:, 0, h:h + 1], scale=-1.0)
                    I("act", "activation", r=[pAt, nq], w=["arg1_%d" % k], out=arg1[k], in_=pA[:, 128:256], func=AF.Relu,
                      bias=negA[:, 1, h:h + 1], scale=-1.0)

                def s3(it):
                    k, h, hx, pk, q = it["k"], it["h"], it["hx"], it["pk"], it["q"]
                    pA, pAt = pAbs[k]
                    I("act", "activation", r=["arg0_%d" % k, "lndt"], w=["E0_%d" % k], out=E0[k], in_=arg0[k], func=AF.Exp,
                      bias=tM["lndt"][:, 2 * q, h:h + 1], scale=-1.0)
                    I("act", "activation", r=["arg1_%d" % k, "lndt"], w=["E1_%d" % k], out=E1[k], in_=arg1[k], func=AF.Exp,
                      bias=tM["lndt"][:, 2 * q + 1, h:h + 1], scale=-1.0)
                    I("act", "activation", r=[pAt], w=["eA%d" % pk], out=eA[pk][hx * 64:(hx + 1) * 64, :],
                      in_=pA[hx * 64:(hx + 1) * 64, :], func=AF.Exp)

                def s4(it):
                    k, q = it["k"], it["q"]
                    cb, cbt = CBms[q]
                    I("dve", "tensor_tensor", r=["E0_%d" % k] + cbt, w=["M0_%d" % k], out=M0[k], in0=E0[k], in1=cb[:, 0:256], op=ALU.mult)
                    I("dve", "tensor_tensor", r=["E1_%d" % k] + cbt, w=["M1_%d" % k], out=M1[k], in0=E1[k], in1=cb[:, 256:384], op=ALU.mult)

                def s5(it):
                    k, hx, hc, pk, hp, q = it["k"], it["hx"], it["hc"], it["pk"], it["hp"], it["q"]
                    pYv, pYt = pYs[pk]
                    I("pe", "matmul", r=[xtk, "M0_%d" % k], w=[pYt],
                      a=(pYv[hx * 64:(hx + 1) * 64, :], xtok[:, 2 * q, hc:hc + 64], M0[k]), start=True, stop=False)
                    I("pe", "matmul", r=[xtk, "M1_%d" % k], w=[pYt],
                      a=(pYv[hx * 64:(hx + 1) * 64, 128:256], xtok[:, 2 * q + 1, hc:hc + 64], M1[k]), start=False, stop=True)
                    if hx == 1:
                        pYov, pYot = pYos[pk]
                        Sb, Sbt = Sbfs[q]
                        I("pe", "matmul", r=Sbt + ["ct"], w=[pYot],
                          a=(pYov, Sb[:, hp * 128:(hp + 1) * 128], ctt[:, q * 256:(q + 1) * 256]), start=True, stop=True)

                def s6(it):
                    if it["hx"] != 1:
                        return
                    pk, hp, q = it["pk"], it["hp"], it["q"]
                    tks = slice(q * 256, (q + 1) * 256)
                    pYv, pYt = pYs[pk]
                    pYov, pYot = pYos[pk]
                    yt, ytk = ytmp[pk], "ytmp%d" % pk
                    gb, gbt = gbufs[q]
                    I("dve", "tensor_tensor", r=[pYot, "eA%d" % pk], w=[ytk], out=yt, in0=pYov, in1=eA[pk], op=ALU.mult)
                    I("dve", "tensor_tensor", r=[ytk, pYt], w=[ytk], out=yt, in0=yt, in1=pYv, op=ALU.add)
                    I("dve", "scalar_tensor_tensor", r=["xTg", "prm", ytk], w=[ytk], out=yt, in0=xTg[:, hp, tks],
                      scalar=pcol(P_DSK, g * 4 + hp), in1=yt, op0=ALU.mult, op1=ALU.add)
                    I("dve", "tensor_tensor", r=[ytk, "szg"], w=gbt, out=gb[hp], in0=yt, in1=szg[:, hp, tks], op=ALU.mult)

                def s7(it):
                    S.new_block()

                def epilogue(q):
                    tks = slice(q * 256, (q + 1) * 256)
                    gb, gbt = gbufs[q]
                    for hp in range(4):
                        gs = gsq[hp % 2]
                        I("act", "activation", r=gbt, w=["gsq%d" % (hp % 2)], out=gs, in_=gb[hp], func=AF.Square)
                        I("pe", "matmul", r=["ones_bf", "gsq%d" % (hp % 2)], w=["pb4"], a=(pG, ones_bf, gs),
                          start=(hp == 0), stop=(hp == 3))
                    I("act", "activation", r=["pb4"], w=["sd_g"], out=sd_g, in_=pG, func=AF.Sqrt, bias=EPS, scale=1.0 / 512)
                    I("dve", "reciprocal", r=["sd_g"], w=["rstd_g"], a=(rstd_g, sd_g))
                    for hp in range(4):
                        I("dve", "scalar_tensor_tensor", r=gbt + ["prm", "rstd_g"], w=["yss"], out=yss[:, hp, tks],
                          in0=gb[hp], scalar=pcol(P_SSDW, g * 4 + hp), in1=rstd_g, op0=ALU.mult, op1=ALU.mult)
                    S.new_block()

                run_pipeline(heads, [s0, s1, s2, s3, s4, s5, s6, s7], hooks={13: (lambda: epilogue(0))})
                epilogue(1)

            def pool_branch(g):
                ws, wt = load_ws([(lambda w_: w_[:, :, :], w_in_v[:, :, C_U + g * 512: C_U + (g + 1) * 512])])
                ws2, wt2 = load_ws([(lambda w_: w_[:, 0:4, :], pool_w_v[g])])
                its = [dict(kind="u", g=g, i=i, hh=hh, ws=ws, wt=wt, wcol=i * 128) for i in range(4)]
                run_pipeline(its, o_stages)
                for j in range(4):
                    mmi = next_mm()
                    pm = pbank[mmi][:, :]
                    for i in range(4):
                        I("pe", "matmul", r=[wt2, "pooled"], w=["pb%d" % mmi],
                          a=(pm, ws2[:, i, j * 128:(j + 1) * 128], pooled[:, i, :]), start=(i == 0), stop=(i == 3))
                    I("act", "mul", r=["pb%d" % mmi, "prm"], w=["ypool"], a=(ypool[:, j, :], pm, pcol(P_PSC, g * 4 + j)))
                    S.new_block()

            for g in range(4):
                wsx, wtx = load_ws([(lambda w_: w_[:, :, :], w_in_v[:, :, C_X + g * 512: C_X + (g + 1) * 512])])
                wsb, wtb = load_ws([(lambda w_: w_[:, :, 0:128], w_in_v[:, :, C_B + g * 128: C_B + (g + 1) * 128]),
                                    (lambda w_: w_[:, :, 128:256], w_in_v[:, :, C_C + g * 128: C_C + (g + 1) * 128])])
                its = []
                for i in range(4):
                    its.append(dict(kind="x", g=g, i=i, ws=wsx, wt=wtx, wcol=i * 128, m=g * 4 + i, dst=xTg[:, i, :], dtok="xTg",
                                    dcol=i * 128, ci=ci[0]))
                    ci[0] += 1
                its.append(dict(kind="B", g=g, i=0, ws=wsb, wt=wtb, wcol=0, m=16 + g, dst=bt, dtok="bt", dcol=512, ci=ci[0]))
                ci[0] += 1
                if hh == 0:
                    proj_chunk(wsb, wtb, 128, lambda kc: hn[:, kc, 12:16], 4, 6, out=pcc, otok="pb6")
                    I("act", "copy", r=["pb6"], w=["carry%d" % (20 + g)], a=(carry[:, 20 + g, :], pcc[:, 1:4]))
                its.append(dict(kind="C", g=g, i=0, ws=wsb, wt=wtb, wcol=128, m=20 + g, dst=ctt, dtok="ct", dcol=0, ci=ci[0]))
                ci[0] += 1
                run_pipeline(its, o_stages)
                wsz, wtz = load_ws([(lambda w_: w_[:, :, :], w_in_v[:, :, C_Z + g * 512: C_Z + (g + 1) * 512])])
                its = [dict(kind="z", g=g, i=i, ws=wsz, wt=wtz, wcol=i * 128) for i in range(4)]
                run_pipeline(its, o_stages)
                S.begin_capture()
                ssd_group(g)
                blkA = S.end_capture()
                S.begin_capture()
                if g > 0:
                    out_proj_part((g - 1) * 4, yss_prev[0], yss_prev[1])
                pool_branch(g)
                out_proj_part(16 + g * 4, ypool, "ypool")
                blkB = S.end_capture()
                nA0 = len(blkA) // 2
                nB1 = 16 if g > 0 else 0
                rest = blkB[nB1:]
                S.interleave(blkA[:nA0], blkB[:nB1] + rest[:len(rest) // 2])
                S.interleave(blkA[nA0:], rest[len(rest) // 2:])
                yss_prev = (yss, "yss")
                if g == 3:
                    out_proj_part(g * 4, yss, "yss")
        peakM = A.peak
        if dbg:
            for hh in range(2):
                I("sp", "dma_start", r=["R%d" % hh], dma=True, out=dbg_R[:, hh * 16 * HT:(hh + 1) * 16 * HT],
                  in_=R[hh].rearrange("p a b -> p (a b)"))

        S.barrier()
        A.reset(PH)
        mm_banks[0] = [0, 1, 2, 3, 5, 6, 7]
        A.off = M_S
        tF = {"sq": [A.alloc([HT], BF16), A.alloc([HT], BF16)], "sd": A.alloc([HT], F32), "rstd": A.alloc([HT], F32)}
        WD = [A.alloc([8, 512], BF16) for _ in range(2)]
        assert A.off <= M_R
        A.reset(PH)
        hn2 = A.alloc([16, TOK], BF16)
        act = [A.alloc([8, TOK], BF16) for _ in range(2)]
        WGU = [A.alloc([16, 512], BF16) for _ in range(2)]
        sg = [A.alloc([HT], F32) for _ in range(2)]
        ost = [A.alloc([HT], F32) for _ in range(2)]

        for hh in (range(2) if do_ffn else ()):
            norm_half(R[hh], P_NW2, lambda c, hh=hh: hn2[:, c, hh * HT:(hh + 1) * HT], tF, dst_tok="hn2", src_tok="R%d" % hh)

        groups = [(0, 8), (8, 8), (16, 8), (24, 8), (32, 8), (40, 4)] if do_ffn else []
        wgu_rr, wd_rr, sg_rr = [0], [0], [0]
        for G, (fc0, nfc) in enumerate(groups):
            ab = act[G % 2]
            at = "act%d" % (G % 2)
            for pr in range(nfc // 2):
                c0 = (fc0 + pr * 2) * 128
                i = wgu_rr[0] % 2
                wgu_rr[0] += 1
                wgu, wgt_ = WGU[i], "wgu%d" % i
                dma_w(wgu[:, :, 0:256], w_gate_v[:, :, c0:c0 + 256], wgt_)
                dma_w(wgu[:, :, 256:512], w_up_v[:, :, c0:c0 + 256], wgt_)
                for cc in range(2):
                    fl = pr * 2 + cc
                    for hh in range(2):
                        mg = next_mm()
                        pg = pbank[mg][:, :]
                        for kc in range(16):
                            I("pe", "matmul", r=[wgt_, "hn2"], w=["pb%d" % mg],
                              a=(pg, wgu[:, kc, cc * 128:(cc + 1) * 128], hn2[:, kc, hh * HT:(hh + 1) * HT]),
                              start=(kc == 0), stop=(kc == 15))
                        mu = next_mm()
                        pu = pbank[mu][:, :]
                        for kc in range(16):
                            I("pe", "matmul", r=[wgt_, "hn2"], w=["pb%d" % mu],
                              a=(pu, wgu[:, kc, 256 + cc * 128:256 + (cc + 1) * 128], hn2[:, kc, hh * HT:(hh + 1) * HT]),
                              start=(kc == 0), stop=(kc == 15))
                        si = sg_rr[0] % 2
                        sg_rr[0] += 1
                        I("act", "activation", r=["pb%d" % mg], w=["sg%d" % si], out=sg[si], in_=pg, func=AF.Silu)
                        I("dve", "tensor_tensor", r=["sg%d" % si, "pb%d" % mu], w=[at], out=ab[:, fl, hh * HT:(hh + 1) * HT],
                          in0=sg[si], in1=pu, op=ALU.mult)
            for db in range(4):
                i = wd_rr[0] % 2
                wd_rr[0] += 1
                wd, wdt_ = WD[i], "wd%d" % i
                dma_w(wd[:, 0:nfc, :], w_down_v[:, fc0:fc0 + nfc, db * 512:(db + 1) * 512], wdt_)
                for dd in range(4):
                    d = db * 4 + dd
                    for hh in range(2):
                        mmi = next_mm()
                        pm = pbank[mmi][:, :]
                        for f in range(nfc):
                            I("pe", "matmul", r=[wdt_, at], w=["pb%d" % mmi],
                              a=(pm, wd[:, f, dd * 128:(dd + 1) * 128], ab[:, f, hh * HT:(hh + 1) * HT]),
                              start=(f == 0), stop=(f == nfc - 1))
                        I("dve", "tensor_tensor", r=["R%d" % hh, "pb%d" % mmi], w=["R%d" % hh], out=R[hh][:, d, :],
                          in0=R[hh][:, d, :], in1=pm, op=ALU.add)
        oi = 0
        for hh in (range(2) if do_ffn else ()):
            src = R[hh]
            rt = "R%d" % hh
            for c in range(16):
                sqb = tF["sq"][c % 2]
                I("act", "activation", r=[rt], w=["sq%d" % (c % 2)], out=sqb, in_=src[:, c, :], func=AF.Square)
                I("pe", "matmul", r=["ones_bf", "sq%d" % (c % 2)], w=["pb4"], a=(pss, ones_bf, sqb),
                  start=(c == 0), stop=(c == 15))
            I("act", "activation", r=["pb4"], w=["sd"], out=tF["sd"], in_=pss, func=AF.Sqrt, bias=EPS, scale=1.0 / D)
            I("dve", "reciprocal", r=["sd"], w=["rstd"], a=(tF["rstd"], tF["sd"]))
            for c in range(16):
                o = ost[oi % 2]
                ot = "ost%d" % (oi % 2)
                oi += 1
                I("dve", "scalar_tensor_tensor", r=[rt, "prm", "rstd"], w=[ot], out=o, in0=src[:, c, :],
                  scalar=pcol(P_NW3, c), in1=tF["rstd"], op0=ALU.mult, op1=ALU.mult)
                I("sp", "dma_start", r=[ot], dma=True, out=outT_v[:, c, hh * HT:(hh + 1) * HT], in_=o)
        peakF = A.peak
        print("SBUF peaks P/M/F:", peakP, peakM, peakF, " ops:", len(S.ops))
        with ExitStack() as st2:
            S.emit(st2)
        print("waits:", S.nwaits)
    return nc


def _pc(v):
    return np.ascontiguousarray(np.asarray(v, np.float32).reshape(16, 128).T)


def _consts():
    c = np.zeros((128, NCONST), np.float32)
    j = np.arange(128)[:, None]
    l = np.arange(128)[None, :]
    tri = (j <= l).astype(np.float32)
    c[:, K_ONES:K_ONES + 128] = 1.0
    c[:, K_LT:K_LT + 128] = (j > l).astype(np.float32)
    c[:, K_T0:K_T0 + 128] = tri
    c[:, K_T0 + 128:K_T0 + 256] = 1.0
    c[:, K_T1 + 128:K_T1 + 256] = tri
    c[:, K_CAUS:K_CAUS + 128] = tri
    c[:, K_CAUS + 128:K_CAUS + 256] = 1.0
    c[:, K_CAUS + 256:K_CAUS + 384] = tri
    c[:, K_ID:K_ID + 128] = np.eye(128, dtype=np.float32)
    return c


def make_in_maps(x, attn_norm_w, w_in, conv_w, conv_b, dt_bias, a_log, d_skip, ssd_norm_w, pool_w,
                 pool_scale, w_out, ffn_norm_w, w_gate, w_up, w_down, final_norm_w):
    x = np.asarray(x, np.float32)
    xs = x.reshape(SEQ, D)
    consts = _consts()
    base = np.zeros((128, NPAR), np.float32)
    base[:, P_NW1:P_NW1 + 16] = _pc(np.asarray(attn_norm_w)[0])
    base[:, P_NW2:P_NW2 + 16] = _pc(np.asarray(ffn_norm_w)[0])
    base[:, P_NW3:P_NW3 + 16] = _pc(np.asarray(final_norm_w))
    base[:, P_SSDW:P_SSDW + 16] = _pc(np.asarray(ssd_norm_w)[0])
    base[:, P_PSC:P_PSC + 16] = _pc(np.asarray(pool_scale)[0])
    base[:, P_DSK:P_DSK + 16] = _pc(np.repeat(np.asarray(d_skip, np.float32)[0], 64))
    cw = np.asarray(conv_w, np.float32)[0]
    base[:, P_CW:P_CW + 96] = cw.reshape(4, 24, 128).transpose(2, 1, 0).reshape(128, 96)
    base[:, P_CB:P_CB + 24] = np.asarray(conv_b, np.float32)[0].reshape(24, 128).T
    base[:, P_DTB:P_DTB + 32] = np.asarray(dt_bias, np.float32)[0][None, :]
    base[:, P_ALOG:P_ALOG + 32] = np.asarray(a_log, np.float32)[0][None, :]

    w_in2 = np.ascontiguousarray(np.asarray(w_in, np.float32)[0])
    pool_w2 = np.ascontiguousarray(np.asarray(pool_w, np.float32)[0].reshape(4 * 512, 512))
    w_out2 = np.ascontiguousarray(np.asarray(w_out, np.float32)[0])
    w_gate2 = np.ascontiguousarray(np.asarray(w_gate, np.float32)[0])
    w_up2 = np.ascontiguousarray(np.asarray(w_up, np.float32)[0])
    w_down2 = np.ascontiguousarray(np.asarray(w_down, np.float32)[0])

    in_maps = []
    for c in range(NCORES):
        xall = np.zeros((NHB, D, HT), np.float32)
        prm = base.copy()
        for j in range(NHB):
            gh = 2 * c - NPRE + j
            if gh >= 0:
                xall[j] = xs[gh * HT:(gh + 1) * HT].T
                prm[:, P_VALID + j] = 1.0
        for hh in range(2):
            tg = c * TOK + hh * HT + np.arange(16)
            for g, wwin in enumerate((2, 4, 8, 16)):
                o = P_ICNT + hh * 64 + g * 16
                prm[:, o:o + 16] = (1.0 / np.minimum(tg + 1, wwin))[None, :]
        in_maps.append({
            "xall": xall.reshape(NHB * D, HT), "w_in": w_in2, "pool_w": pool_w2, "w_out": w_out2,
            "w_gate": w_gate2, "w_up": w_up2, "w_down": w_down2, "consts": consts, "params": prm,
        })
    return in_maps


_NC_CACHE = {}


def kernel(**inputs):
    in_maps = make_in_maps(**inputs)
    if "nc" not in _NC_CACHE:
        _NC_CACHE["nc"] = build_nc()
    nc = _NC_CACHE["nc"]
    res = run_bass_kernel_spmd(nc, in_maps, core_ids=list(range(NCORES)))
    out = np.empty((SEQ, D), np.float32)
    for c in range(NCORES):
        out[c * TOK:(c + 1) * TOK] = res.results[c]["outT"].T
    return out.reshape(1, SEQ, D)
```

```python
from contextlib import ExitStack
import numpy as np
import concourse.bass as bass
import concourse.mybir as mybir
from concourse.bass_utils import run_bass_kernel_spmd

F32 = mybir.dt.float32
BF16 = mybir.dt.bfloat16
AF = mybir.ActivationFunctionType
ALU = mybir.AluOpType

NCORES = 8
D = 2048
SEQ = 8192
TOK = SEQ // NCORES
HT = 512
NPRE = 14
NHB = NPRE + 2
DFF = 5632
NFF = DFF // 128
EPS = 1e-5
C_Z, C_X, C_B, C_C, C_DT, C_U = 0, 2048, 4096, 4608, 5120, 5152

ENGS = ("pe", "act", "dve", "pool", "sp")

K_ONES, K_LT, K_T0, K_T1, K_CAUS, K_ID = 0, 128, 256, 512, 768, 1152
NCONST = 1280
P_NW1, P_NW2, P_NW3, P_SSDW, P_PSC, P_DSK = 0, 16, 32, 48, 64, 80
P_CW, P_CB, P_DTB, P_ALOG, P_VALID, P_ICNT = 96, 192, 216, 248, 280, 296
NPAR = 296 + 128


class Sched:
    def __init__(self, nc):
        self.nc = nc
        self.ops = []
        self.tok_w = {}
        self.tok_r = {}
        self.last = {e: None for e in ENGS}
        self.pending_barrier = {e: set() for e in ENGS}

    def op(self, eng, fn, r=(), w=(), dma=False):
        if getattr(self, "cap", None) is not None:
            self.cap[-1].append((eng, fn, tuple(r), tuple(w), dma))
            return None
        oid = len(self.ops)
        deps = {}
        for t in r:
            if t in self.tok_w:
                deps[self.tok_w[t]] = True
        for t in w:
            if t in self.tok_w:
                deps.setdefault(self.tok_w[t], False)
            for x in self.tok_r.get(t, ()):
                deps.setdefault(x, False)
        for x in self.pending_barrier[eng]:
            deps[x] = True
        self.pending_barrier[eng] = set()
        deps.pop(oid, None)
        best = {}
        out = {}
        for d, raw in deps.items():
            p = self.ops[d]
            if p["dma"]:
                out[d] = raw
                continue
            if p["eng"] == eng and eng == "pe" and not dma and not raw:
                continue
            pe_ = p["eng"]
            if pe_ not in best or d > best[pe_]:
                best[pe_] = d
        for pe_, d in best.items():
            out[d] = True
        self.ops.append(dict(eng=eng, fn=fn, deps=out, dma=dma, id=oid))
        for t in r:
            lst = self.tok_r.setdefault(t, [])
            if not dma:
                lst[:] = [x for x in lst if self.ops[x]["dma"] or self.ops[x]["eng"] != eng]
            lst.append(oid)
        for t in w:
            self.tok_w[t] = oid
            self.tok_r[t] = []
        self.last[eng] = oid
        return oid

    def begin_capture(self):
        self.cap = [[]]

    def new_block(self):
        if getattr(self, "cap", None) is not None and self.cap[-1]:
            self.cap.append([])

    def end_capture(self):
        blocks = [b for b in self.cap if b]
        self.cap = None
        return blocks

    def replay(self, blocks):
        for b in blocks:
            for (eng, fn, r, w, dma) in b:
                self.op(eng, fn, r=r, w=w, dma=dma)

    def interleave(self, A, B):
        na = sum(len(b) for b in A)
        nb = sum(len(b) for b in B)
        ia = ib = 0
        da = db = 0
        while ia < len(A) or ib < len(B):
            fa = da / na if na else 1.0
            fb = db / nb if nb else 1.0
            if ib >= len(B) or (ia < len(A) and fa <= fb):
                self.replay([A[ia]])
                da += len(A[ia])
                ia += 1
            else:
                self.replay([B[ib]])
                db += len(B[ib])
                ib += 1

    def barrier(self):
        lasts = {self.last[e] for e in ENGS if self.last[e] is not None}
        for e in ENGS:
            self.pending_barrier[e] |= lasts

    def emit(self, stack, final_wait_eng="sp", nds=32):
        nc = self.nc
        ops = self.ops
        esem = {e: stack.enter_context(nc.semaphore("s_" + e)) for e in ENGS}
        dsem = [stack.enter_context(nc.semaphore("d_%d" % i)) for i in range(nds)]
        dcount = [0] * nds
        dma_prev = [None] * nds
        npool = (nds * 2) // 3
        kk = {"pool": 0, "sp": 0}
        for o in ops:
            if o["dma"]:
                if o["eng"] == "pool":
                    s = kk["pool"] % npool
                    kk["pool"] += 1
                else:
                    s = npool + kk["sp"] % (nds - npool)
                    kk["sp"] += 1
                if dma_prev[s] is not None:
                    o["deps"].setdefault(dma_prev[s], True)
                o["dsem"] = s
                dma_prev[s] = o["id"]
        needed = set()
        for o in ops:
            needed |= set(o["deps"])
        ecount = {e: 0 for e in ENGS}
        ref = {}
        for o in ops:
            if o["dma"]:
                s = o["dsem"]
                dcount[s] += 16
                ref[o["id"]] = (dsem[s], dcount[s])
            elif o["id"] in needed:
                ecount[o["eng"]] += 1
                ref[o["id"]] = (esem[o["eng"]], ecount[o["eng"]])
        dma_ids = [o["id"] for o in ops if o["dma"]]
        per = {e: [o for o in ops if o["eng"] == e] for e in ENGS}
        block = stack.enter_context(nc.Block())
        self.nwaits = 0

        def mk(e):
            def body(engine):
                waited = {}
                for o in per[e]:
                    for d in sorted(o["deps"]):
                        sem, val = ref[d]
                        if waited.get(id(sem), 0) >= val:
                            continue
                        engine.wait_ge(sem, val)
                        self.nwaits += 1
                        waited[id(sem)] = val
                    meth, a, kw = o["fn"]
                    ins = getattr(engine, meth)(*a, **kw)
                    if o["id"] in ref:
                        sem, val = ref[o["id"]]
                        ins.then_inc(sem, 16 if o["dma"] else 1)
                if e == final_wait_eng:
                    for d in dma_ids:
                        sem, val = ref[d]
                        if waited.get(id(sem), 0) < val:
                            engine.wait_ge(sem, val)
                            waited[id(sem)] = val
            return body

        block.tensor(mk("pe"))
        block.scalar(mk("act"))
        block.vector(mk("dve"))
        block.gpsimd(mk("pool"))
        block.sync(mk("sp"))


class Arena:
    def __init__(self, tensor, nbytes):
        self.t = tensor
        self.nbytes = nbytes
        self.off = 0
        self.peak = 0

    def alloc(self, shape, dt):
        esz = 4 if dt == F32 else 2
        n = 1
        for s in shape:
            n *= s
        nb = n * esz
        off = (self.off + 3) // 4 * 4
        assert off + nb <= self.nbytes, ("SBUF arena overflow", off + nb, self.nbytes)
        self.off = off + nb
        self.peak = max(self.peak, self.off)
        v = self.t[:, off // 2: (off + nb) // 2]
        if dt == F32:
            v = v.bitcast(F32)
        if len(shape) == 2:
            v = v.rearrange("p (a b) -> p a b", b=shape[1])
        elif len(shape) == 3:
            v = v.rearrange("p (a b c) -> p a b c", b=shape[1], c=shape[2])
        return v

    def mark(self):
        return self.off

    def reset(self, m):
        self.off = m


def build_nc(npre=NPRE, do_own=True, do_ffn=True, dbg=False, lvl=9):
    nc = bass.Bass("TRN2", target_bir_lowering=False)
    dt_in = lambda name, shape: nc.dram_tensor(name, shape, F32, kind="ExternalInput").ap()
    xall = dt_in("xall", [NHB * D, HT])
    w_in = dt_in("w_in", [D, 7200])
    pool_w = dt_in("pool_w", [4 * 512, 512])
    w_out = dt_in("w_out", [4096, D])
    w_gate = dt_in("w_gate", [D, DFF])
    w_up = dt_in("w_up", [D, DFF])
    w_down = dt_in("w_down", [DFF, D])
    consts_d = dt_in("consts", [128, NCONST])
    params_d = dt_in("params", [128, NPAR])
    outT = nc.dram_tensor("outT", [D, TOK], F32, kind="ExternalOutput").ap()
    if dbg:
        dbg_S = nc.dram_tensor("dbg_S", [128, 2048], F32, kind="ExternalOutput").ap()
        dbg_R = nc.dram_tensor("dbg_R", [128, 2 * 16 * HT], F32, kind="ExternalOutput").ap()
        dbg_H = nc.dram_tensor("dbg_H", [128, 16 * HT], F32, kind="ExternalOutput").ap()

    xall_v = xall.rearrange("(h c p) t -> h p c t", c=16, p=128)
    w_in_v = w_in.rearrange("(kc p) n -> p kc n", p=128)
    w_out_v = w_out.rearrange("(kc p) n -> p kc n", p=128)
    pool_w_v = pool_w.rearrange("(g kc p) n -> g p kc n", g=4, p=128)
    w_gate_v = w_gate.rearrange("(kc p) n -> p kc n", p=128)
    w_up_v = w_up.rearrange("(kc p) n -> p kc n", p=128)
    w_down_v = w_down.rearrange("(kc p) n -> p kc n", p=128)
    outT_v = outT.rearrange("(c p) t -> p c t", p=128)

    with ExitStack() as st:
        TOTAL = 212800
        arena_t = st.enter_context(nc.sbuf_tensor("arena", [128, TOTAL // 2], BF16))
        pbank = [st.enter_context(nc.psum_tensor("pb%d" % i, [128, 512], F32)) for i in range(8)]
        A = Arena(arena_t, TOTAL)
        S = Sched(nc)

        def I(eng, meth, r=(), w=(), dma=False, a=(), **kw):
            S.op(eng, (meth, tuple(a), kw), r=r, w=w, dma=dma)

        cst = A.alloc([NCONST], F32)
        prm = A.alloc([NPAR], F32)
        ones_bf = A.alloc([128], BF16)
        ident_bf = A.alloc([128], BF16)
        LT_bf = A.alloc([128], BF16)
        a_b = A.alloc([32], F32)
        carry = A.alloc([24, 3], F32)
        M_S = A.mark()
        Sst = A.alloc([4, 512], F32)
        hn = A.alloc([16, 16 + HT], BF16)
        M_R = A.mark()
        R = [A.alloc([16, HT], F32), A.alloc([16, HT], F32)]
        PH = A.mark()

        onesf = cst[:, K_ONES:K_ONES + 128]
        LTf = cst[:, K_LT:K_LT + 128]
        T0 = cst[:, K_T0:K_T0 + 256]
        T1h = cst[:, K_T1 + 128:K_T1 + 256]
        caus = cst[:, K_CAUS:K_CAUS + 384]
        identf = cst[:, K_ID:K_ID + 128]

        def pcol(off, i=0, n=1):
            return prm[:, off + i: off + i + n]

        def dma_w(dst, src, tok, eng="pool"):
            I(eng, "dma_start", w=[tok], dma=True, out=dst, in_=src)

        mm_banks = [[0, 1, 2, 7]]
        mm_rr = [0]

        def next_mm():
            b = mm_banks[0]
            i = b[mm_rr[0] % len(b)]
            mm_rr[0] += 1
            return i

        pCB = pbank[3][:, 0:384]
        pdt = pbank[3][:, 384:512].rearrange("p (a b) -> p a b", b=32)
        pss = pbank[4][:, :]
        ptr = pbank[5][:, 0:256].bitcast(BF16).rearrange("p (a b) -> p a b", b=128)
        pY = pbank[5][:, 256:512]
        pYo = pbank[6][:, 0:256]
        pra = [pbank[6][:, 256 + q * 96: 256 + (q + 1) * 96].rearrange("p (a b) -> p a b", b=32) for q in range(2)]
        ph = pbank[6][:, 448:464]
        pcc = pbank[6][:, 464:468]
        pAbs = [(pbank[7][:, 0:256], "pb7"), (pbank[2][:, 0:256], "pb2")]
        pYs = [(pbank[5][:, 256:512], "pb5"), (pbank[3][:, 0:256], "pb3")]
        pYos = [(pbank[6][:, 0:256], "pb6"), (pbank[4][:, 256:512], "pb4")]

        I("sp", "dma_start", w=["cst"], dma=True, out=cst, in_=consts_d[:, :])
        I("sp", "dma_start", w=["prm"], dma=True, out=prm, in_=params_d[:, :])
        I("dve", "tensor_copy", r=["cst"], w=["ones_bf"], a=(ones_bf, onesf))
        I("dve", "tensor_copy", r=["cst"], w=["ident_bf"], a=(ident_bf, identf))
        I("dve", "tensor_copy", r=["cst"], w=["LT_bf"], a=(LT_bf, LTf))
        I("act", "activation", r=["prm"], w=["a_b"], out=a_b, in_=prm[:, P_ALOG:P_ALOG + 32], func=AF.Exp)
        I("dve", "tensor_scalar", r=["a_b"], w=["a_b"], out=a_b, in0=a_b, scalar1=-1.0, scalar2=None, op0=ALU.mult)
        I("pool", "memset", w=["carry%d" % m_ for m_ in range(24)], a=(carry, 0.0))
        I("pool", "memset", w=["S0", "S1", "S2", "S3"], a=(Sst, 0.0))
        I("pool", "memset", w=["hn"], a=(hn, 0.0))

        def norm_half(src, nw_off, dst_fn, tmp, dst_tok="hn", src_tok="R"):
            for c in range(16):
                sqb = tmp["sq"][c % 2]
                I("act", "activation", r=[src_tok], w=["sq%d" % (c % 2)], out=sqb, in_=src[:, c, :], func=AF.Square)
                I("pe", "matmul", r=["ones_bf", "sq%d" % (c % 2)], w=["pb4"], a=(pss, ones_bf, sqb),
                  start=(c == 0), stop=(c == 15))
            I("act", "activation", r=["pb4"], w=["sd"], out=tmp["sd"], in_=pss, func=AF.Sqrt, bias=EPS, scale=1.0 / D)
            I("dve", "reciprocal", r=["sd"], w=["rstd"], a=(tmp["rstd"], tmp["sd"]))
            for c in range(16):
                I("dve", "scalar_tensor_tensor", r=[src_tok, "prm", "rstd"], w=[dst_tok], out=dst_fn(c),
                  in0=src[:, c, :], scalar=pcol(nw_off, c), in1=tmp["rstd"], op0=ALU.mult, op1=ALU.mult)

        def mixer_phase_alloc(own):
            t = {}
            t["sq"] = [A.alloc([HT], BF16), A.alloc([HT], BF16)]
            t["sd"] = A.alloc([HT], F32)
            t["rstd"] = A.alloc([HT], F32)
            t["pre"] = [A.alloc([3 + HT + 1], F32) for _ in range(2 if own else 3)]
            t["acc"] = [A.alloc([HT], F32) for _ in range(2 if own else 3)]
            t["xc"] = [A.alloc([HT], BF16) for _ in range(0 if own else 3)]
            t["xtok"] = [A.alloc([4, 640], BF16) for _ in range(1 if own else 2)]
            t["xw"] = [A.alloc([2, 512], BF16) for _ in range(1 if own else 2)]
            for k in ("dtr", "dtabs", "dtl", "dt", "dtA", "lndt"):
                t[k] = A.alloc([4, 32], F32)
            t["RAraw"] = [A.alloc([3, 32], F32) for _ in range(2)]
            t["eR"] = [A.alloc([3, 32], F32) for _ in range(2)]
            t["wgt"] = [A.alloc([2, 32], F32) for _ in range(2)]
            t["negA"] = [A.alloc([2, 32], F32) for _ in range(2)]
            t["stmp"] = A.alloc([512], F32)
            t["d3"] = [A.alloc([4, 32], BF16) for _ in range(3)]
            t["rr"] = [A.alloc([4, 32], F32) for _ in range(2)]
            return t

        def dt_block(t, own, hb, WDT, hsrc=None, htok="hn"):
            hsrc = (lambda kc, tt: hn[:, kc, 16 + tt * 128: 16 + (tt + 1) * 128]) if hsrc is None else hsrc
            for tt in range(4):
                for kc in range(16):
                    I("pe", "matmul", r=[htok, "wdt"], w=["pb3"],
                      a=(pdt[:, tt, :], hsrc(kc, tt), WDT[:, kc, :]),
                      start=(kc == 0), stop=(kc == 15))
            bias_b = prm[:, P_DTB:P_DTB + 32].unsqueeze(1).to_broadcast([128, 4, 32])
            I("dve", "tensor_tensor", r=["pb3", "prm"], w=["dtr"], out=t["dtr"], in0=pdt, in1=bias_b, op=ALU.add)
            I("act", "activation", r=["dtr"], w=["dtabs"], out=t["dtabs"], in_=t["dtr"], func=AF.Abs)
            I("act", "activation", r=["dtabs"], w=["dtabs"], out=t["dtabs"], in_=t["dtabs"], func=AF.Exp, scale=-1.0)
            I("act", "activation", r=["dtabs"], w=["dtl"], out=t["dtl"], in_=t["dtabs"], func=AF.Ln, bias=1.0, scale=1.0)
            I("dve", "scalar_tensor_tensor", r=["dtr", "dtl"], w=["dt"], out=t["dt"], in0=t["dtr"], scalar=0.0,
              in1=t["dtl"], op0=ALU.max, op1=ALU.add)
            if not own:
                I("dve", "tensor_scalar", r=["dt", "prm"], w=["dt"], out=t["dt"], in0=t["dt"],
                  scalar1=pcol(P_VALID, hb), scalar2=None, op0=ALU.mult)
            a_bb = a_b.unsqueeze(1).to_broadcast([128, 4, 32])
            I("dve", "tensor_tensor", r=["dt", "a_b"], w=["dtA"], out=t["dtA"], in0=t["dt"], in1=a_bb, op=ALU.mult)
            if own:
                I("act", "activation", r=["dt"], w=["lndt"], out=t["lndt"], in_=t["dt"], func=AF.Ln)
            d3, rr = t["d3"], t["rr"]
            I("dve", "tensor_copy", r=["dtA"], w=["d3_0"], a=(d3[0], t["dtA"]))
            I("dve", "tensor_tensor", r=["dtA", "d3_0"], w=["rr0"], out=rr[0], in0=t["dtA"], in1=d3[0], op=ALU.subtract)
            I("dve", "tensor_copy", r=["rr0"], w=["d3_1"], a=(d3[1], rr[0]))
            I("dve", "tensor_tensor", r=["rr0", "d3_1"], w=["rr1"], out=rr[1], in0=rr[0], in1=d3[1], op=ALU.subtract)
            I("dve", "tensor_copy", r=["rr1"], w=["d3_2"], a=(d3[2], rr[1]))
            rd = ["LT_bf", "ones_bf", "d3_0", "d3_1", "d3_2"]
            for q in range(2):
                ta, tb = 2 * q, 2 * q + 1
                p_ = pra[q]
                tk = "pb6"
                for k in range(3):
                    I("pe", "matmul", r=rd, w=[tk], a=(p_[:, 0, :], LT_bf, d3[k][:, ta, :]), start=(k == 0), stop=False)
                    I("pe", "matmul", r=rd, w=[tk], a=(p_[:, 0, :], ones_bf, d3[k][:, tb, :]), start=False, stop=(k == 2))
                for k in range(3):
                    I("pe", "matmul", r=rd, w=[tk], a=(p_[:, 1, :], LT_bf, d3[k][:, tb, :]), start=(k == 0), stop=(k == 2))
                for k in range(3):
                    I("pe", "matmul", r=rd, w=[tk], a=(p_[:, 2, :], ones_bf, d3[k][:, ta, :]), start=(k == 0), stop=False)
                    I("pe", "matmul", r=rd, w=[tk], a=(p_[:, 2, :], ones_bf, d3[k][:, tb, :]), start=False, stop=(k == 2))
                I("act", "activation", r=[tk], w=["eR%d" % q], out=t["eR"][q], in_=p_, func=AF.Exp)
                I("dve", "tensor_tensor", r=["eR%d" % q, "dt"], w=["wgt%d" % q], out=t["wgt"][q],
                  in0=t["eR"][q][:, 0:2, :], in1=t["dt"][:, ta:ta + 2, :], op=ALU.mult)
                if own:
                    I("act", "copy", r=[tk], w=["RAraw%d" % q], a=(t["RAraw"][q], p_))
                    aend_b = t["RAraw"][q][:, 2:3, :].to_broadcast([128, 2, 32])
                    I("dve", "tensor_tensor", r=["RAraw%d" % q], w=["negA%d" % q], out=t["negA"][q],
                      in0=aend_b, in1=t["RAraw"][q][:, 0:2, :], op=ALU.subtract)

        def proj_chunk(wsrc, wtok, col0, rhs_fn, n, mmi, out=None, otok=None, htok="hn"):
            pm = pbank[mmi][:, 0:n] if out is None else out
            otok = otok or ("pb%d" % mmi)
            for kc in range(16):
                I("pe", "matmul", r=[wtok, htok], w=[otok], a=(pm, wsrc[:, kc, col0:col0 + 128], rhs_fn(kc)),
                  start=(kc == 0), stop=(kc == 15))
            return pm

        def conv_s1(t, pm, mmi, m, i):
            pre = t["pre"][i % len(t["pre"])]
            acc = t["acc"][i % len(t["acc"])]
            ptk, atk = "pre%d" % (i % len(t["pre"])), "acc%d" % (i % len(t["acc"]))
            I("act", "copy", r=["pb%d" % mmi], w=[ptk], a=(pre[:, 3:3 + HT], pm))
            I("act", "copy", r=["carry%d" % m], w=[ptk], a=(pre[:, 0:3], carry[:, m, :]))
            I("act", "activation", r=[ptk, "prm"], w=[atk], out=acc, in_=pre[:, 0:HT], func=AF.Identity,
              bias=pcol(P_CB, m), scale=pcol(P_CW, m * 4 + 0))

        def conv_s2(t, m, i):
            pre = t["pre"][i % len(t["pre"])]
            acc = t["acc"][i % len(t["acc"])]
            ptk, atk = "pre%d" % (i % len(t["pre"])), "acc%d" % (i % len(t["acc"]))
            for k in (1, 2, 3):
                I("dve", "scalar_tensor_tensor", r=[ptk, "prm", atk], w=[atk], out=acc, in0=pre[:, k:k + HT],
                  scalar=pcol(P_CW, m * 4 + k), in1=acc, op0=ALU.mult, op1=ALU.add)

        def conv_s3(t, m, dst, dst_tok, i):
            pre = t["pre"][i % len(t["pre"])]
            acc = t["acc"][i % len(t["acc"])]
            ptk, atk = "pre%d" % (i % len(t["pre"])), "acc%d" % (i % len(t["acc"]))
            I("act", "copy", r=[ptk], w=["carry%d" % m], a=(carry[:, m, :], pre[:, HT:HT + 3]))
            I("act", "activation", r=[atk], w=[dst_tok], out=dst, in_=acc, func=AF.Silu)

        def conv_silu(t, pm, mmi, m, dst, dst_tok, i):
            conv_s1(t, pm, mmi, m, i)
            conv_s2(t, m, i)
            conv_s3(t, m, dst, dst_tok, i)

        def transpose_pe(src, src_tok):
            for tt in range(4):
                I("pe", "transpose", r=[src_tok, "ident_bf"], w=["pb5"],
                  a=(ptr[:, tt, :], src[:, tt * 128:(tt + 1) * 128], ident_bf))

        def transpose_evac(xtok, xtok_tok, col0):
            I("act", "copy", r=["pb5"], w=[xtok_tok], a=(xtok[:, :, col0:col0 + 128], ptr))

        def transpose_to_tok(src, src_tok, xtok, xtok_tok, col0):
            for tt in range(4):
                I("pe", "transpose", r=[src_tok, "ident_bf"], w=["pb5"],
                  a=(ptr[:, tt, :], src[:, tt * 128:(tt + 1) * 128], ident_bf))
            I("act", "copy", r=["pb5"], w=[xtok_tok], a=(xtok[:, :, col0:col0 + 128], ptr))

        def state_update_a(t, g, q, xtok, xtok_tok, gi):
            ta, tb = 2 * q, 2 * q + 1
            xw = t["xw"][gi % len(t["xw"])]
            xwt = "xw%d" % (gi % len(t["xw"]))
            for j, tt in enumerate((ta, tb)):
                wb = t["wgt"][q][:, j, 8 * g:8 * g + 8].unsqueeze(2).to_broadcast([128, 8, 64])
                I("dve", "tensor_tensor", r=[xtok_tok, "wgt%d" % q], w=[xwt],
                  out=xw[:, j, :].rearrange("p (a b) -> p a b", b=64),
                  in0=xtok[:, tt, 0:512].rearrange("p (a b) -> p a b", b=64), in1=wb, op=ALU.mult)

        def state_update_b(t, g, q, xtok, xtok_tok, gi, pS=None, pStok="pb4"):
            pS = pss if pS is None else pS
            ta, tb = 2 * q, 2 * q + 1
            xw = t["xw"][gi % len(t["xw"])]
            xwt = "xw%d" % (gi % len(t["xw"]))
            for j, tt in enumerate((ta, tb)):
                I("pe", "matmul", r=[xtok_tok, xwt], w=[pStok], a=(pS, xtok[:, tt, 512:640], xw[:, j, :]),
                  start=(j == 0), stop=(j == 1))
            decb = t["eR"][q][:, 2, 8 * g:8 * g + 8].unsqueeze(2).to_broadcast([128, 8, 64])
            Sg = Sst[:, g, :]
            I("dve", "tensor_tensor", r=["S%d" % g, "eR%d" % q], w=["stmp"],
              out=t["stmp"].rearrange("p (a b) -> p a b", b=64),
              in0=Sg.rearrange("p (a b) -> p a b", b=64), in1=decb, op=ALU.mult)
            I("dve", "tensor_tensor", r=["stmp", pStok], w=["S%d" % g], out=Sg, in0=t["stmp"], in1=pS, op=ALU.add)

        def state_update(t, g, q, xtok, xtok_tok, gi, pS=None, pStok="pb4"):
            state_update_a(t, g, q, xtok, xtok_tok, gi)
            state_update_b(t, g, q, xtok, xtok_tok, gi, pS=pS, pStok=pStok)

        def run_pipeline(items, stages, hooks=None):
            nst = len(stages)
            for n in range(len(items) + nst - 1):
                for s in reversed(range(nst)):
                    k = n - s
                    if 0 <= k < len(items):
                        stages[s](items[k])
                if hooks and n in hooks:
                    fs = hooks[n]
                    for f in (fs if isinstance(fs, list) else [fs]):
                        f()

        A.off = PH - 16 * HT * 4
        WRES = A.alloc([16, 2560], BF16)
        WDT_P = A.alloc([16, 32], BF16)
        hnB = A.alloc([16, HT], BF16)
        tP = mixer_phase_alloc(False)
        dma_w(WDT_P, w_in_v[:, :, C_DT:C_DT + 32], "wdt")
        for j in (0, 4, 1, 2, 3):
            dma_w(WRES[:, :, j * 512:(j + 1) * 512], w_in_v[:, :, C_X + j * 512: C_X + (j + 1) * 512], "wres%d" % j)

        ci = [0]
        gi = [0]
        rhs_main = lambda kc: hn[:, kc, 16:16 + HT]

        def norm_stats(src, tmp, src_tok):
            for c in range(16):
                sqb = tmp["sq"][c % 2]
                I("act", "activation", r=[src_tok], w=["sq%d" % (c % 2)], out=sqb, in_=src[:, c, :], func=AF.Square)
                I("pe", "matmul", r=["ones_bf", "sq%d" % (c % 2)], w=["pb4"], a=(pss, ones_bf, sqb),
                  start=(c == 0), stop=(c == 15))
            I("act", "activation", r=["pb4"], w=["sd"], out=tmp["sd"], in_=pss, func=AF.Sqrt, bias=EPS, scale=1.0 / D)
            I("dve", "reciprocal", r=["sd"], w=["rstd"], a=(tmp["rstd"], tmp["sd"]))

        def norm_apply(src, nw_off, dst_fn, tmp, dst_tok, src_tok):
            for c in range(16):
                I("dve", "scalar_tensor_tensor", r=[src_tok, "prm", "rstd"], w=[dst_tok], out=dst_fn(c),
                  in0=src[:, c, :], scalar=pcol(nw_off, c), in1=tmp["rstd"], op0=ALU.mult, op1=ALU.mult)

        pS_P = pbank[3][:, :]
        hbs = list(range(NPRE - npre, NPRE))
        NH = len(hbs)

        def hbuf(h):
            if (NH - 1 - h) % 2 == 0:
                return (lambda kc: hn[:, kc, 16:16 + HT]), "hn", (lambda c: hn[:, c, 16:16 + HT]), \
                       (lambda kc, tt: hn[:, kc, 16 + tt * 128: 16 + (tt + 1) * 128])
            return (lambda kc: hnB[:, kc, :]), "hnB", (lambda c: hnB[:, c, :]), \
                   (lambda kc, tt: hnB[:, kc, tt * 128:(tt + 1) * 128])

        info = []
        for h, hb in enumerate(hbs):
            rhs_fn, htok, _, _ = hbuf(h)
            for g in range(4):
                for i in range(5):
                    if i == 0:
                        cur_x = (tP["xtok"][gi[0] % 2], "xtok%d" % (gi[0] % 2), gi[0])
                        gi[0] += 1
                    if i < 4:
                        col0, m, dcol = g * 512 + i * 128, g * 4 + i, i * 128
                    else:
                        col0, m, dcol = 2048 + g * 128, 16 + g, 512
                    info.append(dict(h=h, g=g, i=i, col0=col0, m=m, dcol=dcol, xt=cur_x, ci=ci[0], xc=tP["xc"][ci[0] % 3],
                                     xct="xc%d" % (ci[0] % 3), rhs=rhs_fn, htok=htok))
                    ci[0] += 1

        def st0(it):
            it["mmi"] = next_mm()
            it["pm"] = proj_chunk(WRES, "wres%d" % (it["col0"] // 512), it["col0"], it["rhs"], HT, it["mmi"], htok=it["htok"])

        def st1(it):
            conv_s1(tP, it["pm"], it["mmi"], it["m"], it["ci"])

        def st2(it):
            conv_s2(tP, it["m"], it["ci"])

        def st3(it):
            conv_s3(tP, it["m"], it["xc"], it["xct"], it["ci"])

        def st4(it):
            transpose_pe(it["xc"], it["xct"])

        def st5(it):
            transpose_evac(it["xt"][0], it["xt"][1], it["dcol"])

        def st6(it):
            if it["i"] == 4 and lvl >= 4:
                xt, xk, gx = it["xt"]
                for q in range(2):
                    state_update_a(tP, it["g"], q, xt, xk, gx * 2 + q)

        def st7(it):
            pass

        def st8(it):
            if it["i"] == 4 and lvl >= 4:
                xt, xk, gx = it["xt"]
                for q in range(2):
                    state_update_b(tP, it["g"], q, xt, xk, gx * 2 + q, pS=pS_P, pStok="pb3")

        def load_x(h):
            I("sp", "dma_start", w=["R"], dma=True, out=R[0], in_=xall_v[hbs[h]])

        def do_stats(h):
            norm_stats(R[0], tP, "R")

        def do_apply(h):
            _, htok, dstf, _ = hbuf(h)
            norm_apply(R[0], P_NW1, dstf, tP, htok, "R")
            if h + 1 < NH:
                load_x(h + 1)

        def do_dt(h):
            _, htok, _, dsrc = hbuf(h)
            dt_block(tP, False, hbs[h], WDT_P, hsrc=dsrc, htok=htok)

        hooks = {}
        if NH:
            load_x(0)
            do_stats(0)
            do_apply(0)
            do_dt(0)
            for h in range(NH - 1):
                bstep = 20 * h
                hooks.setdefault(bstep + 3, []).append(lambda h=h: do_stats(h + 1))
                hooks.setdefault(bstep + 8, []).append(lambda h=h: do_apply(h + 1))
                hooks.setdefault(bstep + 28, []).append(lambda h=h: do_dt(h + 1))
            run_pipeline(info, [st0, st1, st2, st3, st4, st5, st6, st7, st8], hooks=hooks)
        peakP = A.peak
        if dbg:
            I("sp", "dma_start", r=["S0", "S1", "S2", "S3"], dma=True, out=dbg_S[:, :], in_=Sst.rearrange("p a b -> p (a b)"))
            for c_ in range(16):
                I("act", "copy", r=["hn"], w=["R"], a=(R[0][:, c_, :], hn[:, c_, 16:16 + HT]))
            I("sp", "dma_start", r=["R"], dma=True, out=dbg_H[:, :], in_=R[0].rearrange("p a b -> p (a b)"))

        S.barrier()
        mm_banks[0] = [0, 1]
        A.reset(PH)
        WDT_M = A.alloc([16, 32], BF16)
        tM = mixer_phase_alloc(True)
        WS = [A.alloc([16, 512], BF16) for _ in range(2)]
        bt = A.alloc([HT], BF16)
        ctt = A.alloc([HT], BF16)
        xTg = A.alloc([4, HT], BF16)
        szg = A.alloc([4, HT], BF16)
        Dx0 = [A.alloc([256], F32) for _ in range(2)]
        Dx1 = [A.alloc([128], F32) for _ in range(2)]
        arg0 = [A.alloc([256], F32) for _ in range(2)]
        arg1 = [A.alloc([128], F32) for _ in range(2)]
        E0 = [tM["sd"][:, 0:256], tM["sd"][:, 256:512]]
        E1 = [tM["rstd"][:, 0:128], tM["rstd"][:, 128:256]]
        M1 = [tM["rstd"][:, 256:384].bitcast(BF16)[:, 0:128], tM["rstd"][:, 256:384].bitcast(BF16)[:, 128:256]]
        M0 = [tM["sq"][0][:, 0:256], tM["sq"][0][:, 256:512]]
        CBm0_ = A.alloc([384], F32)
        CBms = [(CBm0_, ["CBm"]), (tM["acc"][0][:, 0:384], ["acc0"])]
        eA = [A.alloc([256], F32) for _ in range(2)]
        Sbf0_ = A.alloc([512], BF16)
        Sbfs = [(Sbf0_, ["Sbf"]), (tM["acc"][1].bitcast(BF16)[:, 0:512], ["acc1"])]
        gbuf0_ = A.alloc([4, 256], F32)
        gb1a = tM["pre"][0][:, 0:512].rearrange("p (a b) -> p a b", b=256)
        gb1b = tM["pre"][1][:, 0:512].rearrange("p (a b) -> p a b", b=256)
        gbufs = [([gbuf0_[:, i, :] for i in range(4)], ["gbuf"]),
                 ([gb1a[:, 0, :], gb1a[:, 1, :], gb1b[:, 0, :], gb1b[:, 1, :]], ["pre0", "pre1"])]
        ytmp = [A.alloc([256], F32) for _ in range(2)]
        gsq = [A.alloc([256], BF16) for _ in range(2)]
        rstd_g = A.alloc([256], F32)
        sd_g = A.alloc([256], F32)
        yss = A.alloc([4, HT], BF16)
        ubuf = A.alloc([16 + HT], F32)
        pp = [A.alloc([16 + HT], F32) for _ in range(2)]
        pooled = A.alloc([4, HT], BF16)
        ypool = A.alloc([4, HT], BF16)
        dma_w(WDT_M, w_in_v[:, :, C_DT:C_DT + 32], "wdt")

        ws_rr = [0]

        def load_ws(srcs):
            i = ws_rr[0] % 2
            ws_rr[0] += 1
            ws = WS[i]
            tok = "ws%d" % i
            for dfn, src in srcs:
                dma_w(dfn(ws), src, tok)
            return ws, tok

        T0b = T0.unsqueeze(1).to_broadcast([128, 2, 256])
        T1b = T1h.unsqueeze(1).to_broadcast([128, 2, 128])
        pG = pbank[4][:, 0:256]

        for hh in (range(2) if do_own else ()):
            hb = NPRE + hh
            if hh == 1:
                S.barrier()
            Rh = R[hh]
            rt = "R%d" % hh
            I("sp", "dma_start", w=[rt], dma=True, out=Rh, in_=xall_v[hb])
            I("dve", "tensor_copy", r=["hn"], w=["hn"], a=(hn[:, :, 0:16], hn[:, :, HT:HT + 16]))
            norm_half(Rh, P_NW1, lambda c: hn[:, c, 16:16 + HT], tM, src_tok=rt)
            dt_block(tM, True, hb, WDT_M)
            xtok = tM["xtok"][0]
            xtk = "xtok0"
            hcnt = [0]

            def o_st0(it):
                it["mmi"] = next_mm()
                it["pm"] = proj_chunk(it["ws"], it["wt"], it["wcol"], rhs_main, HT, it["mmi"])
                if it["kind"] == "u":
                    proj_chunk(it["ws"], it["wt"], it["wcol"], lambda kc: hn[:, kc, 0:16], 16, 6, out=ph, otok="pb6")

            def o_st1(it):
                k_ = it["kind"]
                if k_ in ("x", "B", "C"):
                    conv_s1(tM, it["pm"], it["mmi"], it["m"], it["ci"])
                    conv_s2(tM, it["m"], it["ci"])
                elif k_ == "z":
                    I("act", "activation", r=["pb%d" % it["mmi"]], w=["szg"], out=szg[:, it["i"], :], in_=it["pm"], func=AF.Silu)
                else:
                    I("act", "copy", r=["pb%d" % it["mmi"]], w=["ubuf"], a=(ubuf[:, 16:16 + HT], it["pm"]))
                    I("act", "copy", r=["pb6"], w=["ubuf"], a=(ubuf[:, 0:16], ph))

            def o_st2(it):
                k_ = it["kind"]
                if k_ in ("x", "B", "C"):
                    conv_s3(tM, it["m"], it["dst"], it["dtok"], it["ci"])
                elif k_ == "u":
                    g_, i_ = it["g"], it["i"]
                    wwin = float(1 << (g_ + 1))
                    src, stok, lo = ubuf, "ubuf", 0
                    for k in range(g_ + 1):
                        sh = 1 << k
                        dst, dtok = pp[k % 2], "pp%d" % (k % 2)
                        lo2 = lo + sh
                        I("dve", "tensor_tensor", r=[stok], w=[dtok], out=dst[:, lo2:16 + HT], in0=src[:, lo2:16 + HT],
                          in1=src[:, lo2 - sh:16 + HT - sh], op=ALU.add)
                        src, stok, lo = dst, dtok, lo2
                    I("dve", "scalar_tensor_tensor", r=[stok, "ubuf"], w=["pooled"], out=pooled[:, i_, 16:HT],
                      in0=src[:, 32:16 + HT], scalar=1.0 / wwin, in1=ubuf[:, 32:16 + HT], op0=ALU.mult, op1=ALU.subtract)
                    oth = pp[(g_ + 1) % 2]
                    otk = "pp%d" % ((g_ + 1) % 2)
                    io = P_ICNT + it["hh"] * 64 + g_ * 16
                    I("dve", "tensor_tensor", r=[stok, "prm"], w=[otk], out=oth[:, 0:16], in0=src[:, 16:32],
                      in1=prm[:, io:io + 16], op=ALU.mult)
                    I("dve", "tensor_tensor", r=[otk, "ubuf"], w=["pooled"], out=pooled[:, i_, 0:16], in0=oth[:, 0:16],
                      in1=ubuf[:, 16:32], op=ALU.subtract)

            def o_st3(it):
                if it["kind"] in ("x", "B"):
                    transpose_pe(it["dst"], it["dtok"])

            def o_st4(it):
                if it["kind"] in ("x", "B"):
                    transpose_evac(xtok, xtk, it["dcol"])
                S.new_block()

            o_stages = [o_st0, o_st1, o_st2, o_st3, o_st4]

            def out_proj_part(kc0, src_t, src_tok):
                for dh in range(2):
                    ws, wt = load_ws([(lambda w_: w_[:, 0:8, :].rearrange("p (a b) c -> p a (b c)", b=2),
                                       w_out_v[:, kc0:kc0 + 4, dh * 1024:(dh + 1) * 1024])])
                    wv = ws[:, 0:8, :].rearrange("p (a b) c -> p a (b c)", b=2)
                    for dd in range(8):
                        d = dh * 8 + dd
                        mmi = next_mm()
                        pm = pbank[mmi][:, :]
                        for i in range(4):
                            I("pe", "matmul", r=[wt, src_tok], w=["pb%d" % mmi],
                              a=(pm, wv[:, i, dd * 128:(dd + 1) * 128], src_t[:, i, :]), start=(i == 0), stop=(i == 3))
                        I("dve", "tensor_tensor", r=[rt, "pb%d" % mmi], w=[rt], out=Rh[:, d, :], in0=Rh[:, d, :], in1=pm, op=ALU.add)
                        S.new_block()

            def ssd_group(g):
                for q in range(2):
                    ta, tb = 2 * q, 2 * q + 1
                    Sb, Sbt = Sbfs[q]
                    I("act", "copy", r=["S%d" % g], w=Sbt, a=(Sb, Sst[:, g, :]))
                    state_update(tM, g, q, xtok, xtk, gi[0])
                    gi[0] += 1
                    I("pe", "matmul", r=["bt", "ct"], w=["pb3"],
                      a=(pCB[:, 0:256], bt[:, ta * 128:(ta + 1) * 128], ctt[:, ta * 128: ta * 128 + 256]), start=True, stop=True)
                    I("pe", "matmul", r=["bt", "ct"], w=["pb3"],
                      a=(pCB[:, 256:384], bt[:, tb * 128:(tb + 1) * 128], ctt[:, tb * 128:(tb + 1) * 128]), start=True, stop=True)
                    cb, cbt = CBms[q]
                    I("dve", "tensor_tensor", r=["pb3", "cst"], w=cbt, out=cb, in0=pCB, in1=caus, op=ALU.mult)
                S.new_block()
                heads = []
                for q in range(2):
                    for h8 in range(8):
                        k = hcnt[0] % 2
                        pk = (hcnt[0] // 2) % 2
                        heads.append(dict(q=q, h=8 * g + h8, hp=h8 // 2, hx=h8 % 2, hc=h8 * 64, k=k, pk=pk))
                        hcnt[0] += 1

                def s0(it):
                    pass

                def s1(it):
                    k, h, q = it["k"], it["h"], it["q"]
                    pA, pAt = pAbs[k]
                    la = tM["dtA"][:, 2 * q, h:h + 1].to_broadcast([128, 128])
                    lb = tM["dtA"][:, 2 * q + 1, h:h + 1].to_broadcast([128, 128])
                    I("pe", "matmul", r=["cst", "dtA"], w=[pAt], a=(pA, la, T0), start=True, stop=False)
                    I("pe", "matmul", r=["cst", "dtA"], w=[pAt], a=(pA[:, 128:256], lb, T1h), start=False, stop=True)

                def s2(it):
                    k, h, q = it["k"], it["h"], it["q"]
                    pA, pAt = pAbs[k]
                    negA = tM["negA"][q]
                    nq = "negA%d" % q
                    I("act", "activation", r=[pAt, nq], w=["arg0_%d" % k], out=arg0[k], in_=pA, func=AF.Relu,
                      bias=negA[:, 0, h:h + 1], scale=-1.0)
                    I("act", "activation", r=[pAt, nq], w=["arg1_%d" % k], out=arg1[k], in_=pA[:, 128:256], func=AF.Relu,
                      bias=negA[:, 1, h:h + 1], scale=-1.0)

                def s3(it):
                    k, h, hx, pk, q = it["k"], it["h"], it["hx"], it["pk"], it["q"]
                    pA, pAt = pAbs[k]
                    I("act", "activation", r=["arg0_%d" % k, "lndt"], w=["E0_%d" % k], out=E0[k], in_=arg0[k], func=AF.Exp,
                      bias=tM["lndt"][:, 2 * q, h:h + 1], scale=-1.0)
                    I("act", "activation", r=["arg1_%d" % k, "lndt"], w=["E1_%d" % k], out=E1[k], in_=arg1[k], func=AF.Exp,
                      bias=tM["lndt"][:, 2 * q + 1, h:h + 1], scale=-1.0)
                    I("act", "activation", r=[pAt], w=["eA%d" % pk], out=eA[pk][hx * 64:(hx + 1) * 64, :],
                      in_=pA[hx * 64:(hx + 1) * 64, :], func=AF.Exp)

                def s4(it):
                    k, q = it["k"], it["q"]
                    cb, cbt = CBms[q]
                    I("dve", "tensor_tensor", r=["E0_%d" % k] + cbt, w=["M0_%d" % k], out=M0[k], in0=E0[k], in1=cb[:, 0:256], op=ALU.mult)
                    I("dve", "tensor_tensor", r=["E1_%d" % k] + cbt, w=["M1_%d" % k], out=M1[k], in0=E1[k], in1=cb[:, 256:384], op=ALU.mult)

                def s5(it):
                    k, hx, hc, pk, hp, q = it["k"], it["hx"], it["hc"], it["pk"], it["hp"], it["q"]
                    pYv, pYt = pYs[pk]
                    I("pe", "matmul", r=[xtk, "M0_%d" % k], w=[pYt],
                      a=(pYv[hx * 64:(hx + 1) * 64, :], xtok[:, 2 * q, hc:hc + 64], M0[k]), start=True, stop=False)
                    I("pe", "matmul", r=[xtk, "M1_%d" % k], w=[pYt],
                      a=(pYv[hx * 64:(hx + 1) * 64, 128:256], xtok[:, 2 * q + 1, hc:hc + 64], M1[k]), start=False, stop=True)
                    if hx == 1:
                        pYov, pYot = pYos[pk]
                        Sb, Sbt = Sbfs[q]
                        I("pe", "matmul", r=Sbt + ["ct"], w=[pYot],
                          a=(pYov, Sb[:, hp * 128:(hp + 1) * 128], ctt[:, q * 256:(q + 1) * 256]), start=True, stop=True)

                def s6(it):
                    if it["hx"] != 1:
                        return
                    pk, hp, q = it["pk"], it["hp"], it["q"]
                    tks = slice(q * 256, (q + 1) * 256)
                    pYv, pYt = pYs[pk]
                    pYov, pYot = pYos[pk]
                    yt, ytk = ytmp[pk], "ytmp%d" % pk
                    gb, gbt = gbufs[q]
                    I("dve", "tensor_tensor", r=[pYot, "eA%d" % pk], w=[ytk], out=yt, in0=pYov, in1=eA[pk], op=ALU.mult)
                    I("dve", "tensor_tensor", r=[ytk, pYt], w=[ytk], out=yt, in0=yt, in1=pYv, op=ALU.add)
                    I("dve", "scalar_tensor_tensor", r=["xTg", "prm", ytk], w=[ytk], out=yt, in0=xTg[:, hp, tks],
                      scalar=pcol(P_DSK, g * 4 + hp), in1=yt, op0=ALU.mult, op1=ALU.add)
                    I("dve", "tensor_tensor", r=[ytk, "szg"], w=gbt, out=gb[hp], in0=yt, in1=szg[:, hp, tks], op=ALU.mult)

                def s7(it):
                    S.new_block()

                def epilogue(q):
                    tks = slice(q * 256, (q + 1) * 256)
                    gb, gbt = gbufs[q]
                    for hp in range(4):
                        gs = gsq[hp % 2]
                        I("act", "activation", r=gbt, w=["gsq%d" % (hp % 2)], out=gs, in_=gb[hp], func=AF.Square)
                        I("pe", "matmul", r=["ones_bf", "gsq%d" % (hp % 2)], w=["pb4"], a=(pG, ones_bf, gs),
                          start=(hp == 0), stop=(hp == 3))
                    I("act", "activation", r=["pb4"], w=["sd_g"], out=sd_g, in_=pG, func=AF.Sqrt, bias=EPS, scale=1.0 / 512)
                    I("dve", "reciprocal", r=["sd_g"], w=["rstd_g"], a=(rstd_g, sd_g))
                    for hp in range(4):
                        I("dve", "scalar_tensor_tensor", r=gbt + ["prm", "rstd_g"], w=["yss"], out=yss[:, hp, tks],
                          in0=gb[hp], scalar=pcol(P_SSDW, g * 4 + hp), in1=rstd_g, op0=ALU.mult, op1=ALU.mult)
                    S.new_block()

                run_pipeline(heads, [s0, s1, s2, s3, s4, s5, s6, s7], hooks={13: (lambda: epilogue(0))})
                epilogue(1)

            def pool_branch(g):
                ws, wt = load_ws([(lambda w_: w_[:, :, :], w_in_v[:, :, C_U + g * 512: C_U + (g + 1) * 512])])
                ws2, wt2 = load_ws([(lambda w_: w_[:, 0:4, :], pool_w_v[g])])
                its = [dict(kind="u", g=g, i=i, hh=hh, ws=ws, wt=wt, wcol=i * 128) for i in range(4)]
                run_pipeline(its, o_stages)
                for j in range(4):
                    mmi = next_mm()
                    pm = pbank[mmi][:, :]
                    for i in range(4):
                        I("pe", "matmul", r=[wt2, "pooled"], w=["pb%d" % mmi],
                          a=(pm, ws2[:, i, j * 128:(j + 1) * 128], pooled[:, i, :]), start=(i == 0), stop=(i == 3))
                    I("act", "mul", r=["pb%d" % mmi, "prm"], w=["ypool"], a=(ypool[:, j, :], pm, pcol(P_PSC, g * 4 + j)))
                    S.new_block()

            for g in range(4):
                wsx, wtx = load_ws([(lambda w_: w_[:, :, :], w_in_v[:, :, C_X + g * 512: C_X + (g + 1) * 512])])
                wsb, wtb = load_ws([(lambda w_: w_[:, :, 0:128], w_in_v[:, :, C_B + g * 128: C_B + (g + 1) * 128]),
                                    (lambda w_: w_[:, :, 128:256], w_in_v[:, :, C_C + g * 128: C_C + (g + 1) * 128])])
                its = []
                for i in range(4):
                    its.append(dict(kind="x", g=g, i=i, ws=wsx, wt=wtx, wcol=i * 128, m=g * 4 + i, dst=xTg[:, i, :], dtok="xTg",
                                    dcol=i * 128, ci=ci[0]))
                    ci[0] += 1
                its.append(dict(kind="B", g=g, i=0, ws=wsb, wt=wtb, wcol=0, m=16 + g, dst=bt, dtok="bt", dcol=512, ci=ci[0]))
                ci[0] += 1
                if hh == 0:
                    proj_chunk(wsb, wtb, 128, lambda kc: hn[:, kc, 12:16], 4, 6, out=pcc, otok="pb6")
                    I("act", "copy", r=["pb6"], w=["carry%d" % (20 + g)], a=(carry[:, 20 + g, :], pcc[:, 1:4]))
                its.append(dict(kind="C", g=g, i=0, ws=wsb, wt=wtb, wcol=128, m=20 + g, dst=ctt, dtok="ct", dcol=0, ci=ci[0]))
                ci[0] += 1
                run_pipeline(its, o_stages)
                wsz, wtz = load_ws([(lambda w_: w_[:, :, :], w_in_v[:, :, C_Z + g * 512: C_Z + (g + 1) * 512])])
                its = [dict(kind="z", g=g, i=i, ws=wsz, wt=wtz, wcol=i * 128) for i in range(4)]
                run_pipeline(its, o_stages)
                S.begin_capture()
                ssd_group(g)
                blkA = S.end_capture()
                S.begin_capture()
                if g > 0:
                    out_proj_part((g - 1) * 4, yss_prev[0], yss_prev[1])
                pool_branch(g)
                out_proj_part(16 + g * 4, ypool, "ypool")
                blkB = S.end_capture()
                nA0 = len(blkA) // 2
                nB1 = 16 if g > 0 else 0
                rest = blkB[nB1:]
                S.interleave(blkA[:nA0], blkB[:nB1] + rest[:len(rest) // 2])
                S.interleave(blkA[nA0:], rest[len(rest) // 2:])
                yss_prev = (yss, "yss")
                if g == 3:
                    out_proj_part(g * 4, yss, "yss")
        peakM = A.peak
        if dbg:
            for hh in range(2):
                I("sp", "dma_start", r=["R%d" % hh], dma=True, out=dbg_R[:, hh * 16 * HT:(hh + 1) * 16 * HT],
                  in_=R[hh].rearrange("p a b -> p (a b)"))

        S.barrier()
        A.reset(PH)
        mm_banks[0] = [0, 1, 2, 3, 5, 6, 7]
        A.off = M_S
        tF = {"sq": [A.alloc([HT], BF16), A.alloc([HT], BF16)], "sd": A.alloc([HT], F32), "rstd": A.alloc([HT], F32)}
        WD = [A.alloc([8, 512], BF16) for _ in range(2)]
        assert A.off <= M_R
        A.reset(PH)
        hn2 = A.alloc([16, TOK], BF16)
        act = [A.alloc([8, TOK], BF16) for _ in range(2)]
        WGU = [A.alloc([16, 512], BF16) for _ in range(2)]
        sg = [A.alloc([HT], F32) for _ in range(2)]
        ost = [A.alloc([HT], F32) for _ in range(2)]

        for hh in (range(2) if do_ffn else ()):
            norm_half(R[hh], P_NW2, lambda c, hh=hh: hn2[:, c, hh * HT:(hh + 1) * HT], tF, dst_tok="hn2", src_tok="R%d" % hh)

        groups = [(0, 8), (8, 8), (16, 8), (24, 8), (32, 8), (40, 4)] if do_ffn else []
        wgu_rr, wd_rr, sg_rr = [0], [0], [0]
        for G, (fc0, nfc) in enumerate(groups):
            ab = act[G % 2]
            at = "act%d" % (G % 2)
            for pr in range(nfc // 2):
                c0 = (fc0 + pr * 2) * 128
                i = wgu_rr[0] % 2
                wgu_rr[0] += 1
                wgu, wgt_ = WGU[i], "wgu%d" % i
                dma_w(wgu[:, :, 0:256], w_gate_v[:, :, c0:c0 + 256], wgt_)
                dma_w(wgu[:, :, 256:512], w_up_v[:, :, c0:c0 + 256], wgt_)
                for cc in range(2):
                    fl = pr * 2 + cc
                    for hh in range(2):
                        mg = next_mm()
                        pg = pbank[mg][:, :]
                        for kc in range(16):
                            I("pe", "matmul", r=[wgt_, "hn2"], w=["pb%d" % mg],
                              a=(pg, wgu[:, kc, cc * 128:(cc + 1) * 128], hn2[:, kc, hh * HT:(hh + 1) * HT]),
                              start=(kc == 0), stop=(kc == 15))
                        mu = next_mm()
                        pu = pbank[mu][:, :]
                        for kc in range(16):
                            I("pe", "matmul", r=[wgt_, "hn2"], w=["pb%d" % mu],
                              a=(pu, wgu[:, kc, 256 + cc * 128:256 + (cc + 1) * 128], hn2[:, kc, hh * HT:(hh + 1) * HT]),
                              start=(kc == 0), stop=(kc == 15))
                        si = sg_rr[0] % 2
                        sg_rr[0] += 1
                        I("act", "activation", r=["pb%d" % mg], w=["sg%d" % si], out=sg[si], in_=pg, func=AF.Silu)
                        I("dve", "tensor_tensor", r=["sg%d" % si, "pb%d" % mu], w=[at], out=ab[:, fl, hh * HT:(hh + 1) * HT],
                          in0=sg[si], in1=pu, op=ALU.mult)
            for db in range(4):
                i = wd_rr[0] % 2
                wd_rr[0] += 1
                wd, wdt_ = WD[i], "wd%d" % i
                dma_w(wd[:, 0:nfc, :], w_down_v[:, fc0:fc0 + nfc, db * 512:(db + 1) * 512], wdt_)
                for dd in range(4):
                    d = db * 4 + dd
                    for hh in range(2):
                        mmi = next_mm()
                        pm = pbank[mmi][:, :]
                        for f in range(nfc):
                            I("pe", "matmul", r=[wdt_, at], w=["pb%d" % mmi],
                              a=(pm, wd[:, f, dd * 128:(dd + 1) * 128], ab[:, f, hh * HT:(hh + 1) * HT]),
                              start=(f == 0), stop=(f == nfc - 1))
                        I("dve", "tensor_tensor", r=["R%d" % hh, "pb%d" % mmi], w=["R%d" % hh], out=R[hh][:, d, :],
                          in0=R[hh][:, d, :], in1=pm, op=ALU.add)
        oi = 0
        for hh in (range(2) if do_ffn else ()):
            src = R[hh]
            rt = "R%d" % hh
            for c in range(16):
                sqb = tF["sq"][c % 2]
                I("act", "activation", r=[rt], w=["sq%d" % (c % 2)], out=sqb, in_=src[:, c, :], func=AF.Square)
                I("pe", "matmul", r=["ones_bf", "sq%d" % (c % 2)], w=["pb4"], a=(pss, ones_bf, sqb),
                  start=(c == 0), stop=(c == 15))
            I("act", "activation", r=["pb4"], w=["sd"], out=tF["sd"], in_=pss, func=AF.Sqrt, bias=EPS, scale=1.0 / D)
            I("dve", "reciprocal", r=["sd"], w=["rstd"], a=(tF["rstd"], tF["sd"]))
            for c in range(16):
                o = ost[oi % 2]
                ot = "ost%d" % (oi % 2)
                oi += 1
                I("dve", "scalar_tensor_tensor", r=[rt, "prm", "rstd"], w=[ot], out=o, in0=src[:, c, :],
                  scalar=pcol(P_NW3, c), in1=tF["rstd"], op0=ALU.mult, op1=ALU.mult)
                I("sp", "dma_start", r=[ot], dma=True, out=outT_v[:, c, hh * HT:(hh + 1) * HT], in_=o)
        peakF = A.peak
        print("SBUF peaks P/M/F:", peakP, peakM, peakF, " ops:", len(S.ops))
        with ExitStack() as st2:
            S.emit(st2)
        print("waits:", S.nwaits)
    return nc


def _pc(v):
    return np.ascontiguousarray(np.asarray(v, np.float32).reshape(16, 128).T)


def _consts():
    c = np.zeros((128, NCONST), np.float32)
    j = np.arange(128)[:, None]
    l = np.arange(128)[None, :]
    tri = (j <= l).astype(np.float32)
    c[:, K_ONES:K_ONES + 128] = 1.0
    c[:, K_LT:K_LT + 128] = (j > l).astype(np.float32)
    c[:, K_T0:K_T0 + 128] = tri
    c[:, K_T0 + 128:K_T0 + 256] = 1.0
    c[:, K_T1 + 128:K_T1 + 256] = tri
    c[:, K_CAUS:K_CAUS + 128] = tri
    c[:, K_CAUS + 128:K_CAUS + 256] = 1.0
    c[:, K_CAUS + 256:K_CAUS + 384] = tri
    c[:, K_ID:K_ID + 128] = np.eye(128, dtype=np.float32)
    return c


def make_in_maps(x, attn_norm_w, w_in, conv_w, conv_b, dt_bias, a_log, d_skip, ssd_norm_w, pool_w,
                 pool_scale, w_out, ffn_norm_w, w_gate, w_up, w_down, final_norm_w):
    x = np.asarray(x, np.float32)
    xs = x.reshape(SEQ, D)
    consts = _consts()
    base = np.zeros((128, NPAR), np.float32)
    base[:, P_NW1:P_NW1 + 16] = _pc(np.asarray(attn_norm_w)[0])
    base[:, P_NW2:P_NW2 + 16] = _pc(np.asarray(ffn_norm_w)[0])
    base[:, P_NW3:P_NW3 + 16] = _pc(np.asarray(final_norm_w))
    base[:, P_SSDW:P_SSDW + 16] = _pc(np.asarray(ssd_norm_w)[0])
    base[:, P_PSC:P_PSC + 16] = _pc(np.asarray(pool_scale)[0])
    base[:, P_DSK:P_DSK + 16] = _pc(np.repeat(np.asarray(d_skip, np.float32)[0], 64))
    cw = np.asarray(conv_w, np.float32)[0]
    base[:, P_CW:P_CW + 96] = cw.reshape(4, 24, 128).transpose(2, 1, 0).reshape(128, 96)
    base[:, P_CB:P_CB + 24] = np.asarray(conv_b, np.float32)[0].reshape(24, 128).T
    base[:, P_DTB:P_DTB + 32] = np.asarray(dt_bias, np.float32)[0][None, :]
    base[:, P_ALOG:P_ALOG + 32] = np.asarray(a_log, np.float32)[0][None, :]

    w_in2 = np.ascontiguousarray(np.asarray(w_in, np.float32)[0])
    pool_w2 = np.ascontiguousarray(np.asarray(pool_w, np.float32)[0].reshape(4 * 512, 512))
    w_out2 = np.ascontiguousarray(np.asarray(w_out, np.float32)[0])
    w_gate2 = np.ascontiguousarray(np.asarray(w_gate, np.float32)[0])
    w_up2 = np.ascontiguousarray(np.asarray(w_up, np.float32)[0])
    w_down2 = np.ascontiguousarray(np.asarray(w_down, np.float32)[0])

    in_maps = []
    for c in range(NCORES):
        xall = np.zeros((NHB, D, HT), np.float32)
        prm = base.copy()
        for j in range(NHB):
            gh = 2 * c - NPRE + j
            if gh >= 0:
                xall[j] = xs[gh * HT:(gh + 1) * HT].T
                prm[:, P_VALID + j] = 1.0
        for hh in range(2):
            tg = c * TOK + hh * HT + np.arange(16)
            for g, wwin in enumerate((2, 4, 8, 16)):
                o = P_ICNT + hh * 64 + g * 16
                prm[:, o:o + 16] = (1.0 / np.minimum(tg + 1, wwin))[None, :]
        in_maps.append({
            "xall": xall.reshape(NHB * D, HT), "w_in": w_in2, "pool_w": pool_w2, "w_out": w_out2,
            "w_gate": w_gate2, "w_up": w_up2, "w_down": w_down2, "consts": consts, "params": prm,
        })
    return in_maps


_NC_CACHE = {}


def kernel(**inputs):
    in_maps = make_in_maps(**inputs)
    if "nc" not in _NC_CACHE:
        _NC_CACHE["nc"] = build_nc()
    nc = _NC_CACHE["nc"]
    res = run_bass_kernel_spmd(nc, in_maps, core_ids=list(range(NCORES)))
    out = np.empty((SEQ, D), np.float32)
    for c in range(NCORES):
        out[c * TOK:(c + 1) * TOK] = res.results[c]["outT"].T
    return out.reshape(1, SEQ, D)
```

```python
from contextlib import ExitStack
import numpy as np
import concourse.bass as bass
import concourse.mybir as mybir
from concourse.bass_utils import run_bass_kernel_spmd

F32 = mybir.dt.float32
BF16 = mybir.dt.bfloat16
AF = mybir.ActivationFunctionType
ALU = mybir.AluOpType

NCORES = 8
D = 2048
SEQ = 8192
TOK = SEQ // NCORES
HT = 512
NPRE = 14
NHB = NPRE + 2
DFF = 5632
NFF = DFF // 128
EPS = 1e-5
C_Z, C_X, C_B, C_C, C_DT, C_U = 0, 2048, 4096, 4608, 5120, 5152

ENGS = ("pe", "act", "dve", "pool", "sp")

K_ONES, K_LT, K_T0, K_T1, K_CAUS, K_ID = 0, 128, 256, 512, 768, 1152
NCONST = 1280
P_NW1, P_NW2, P_NW3, P_SSDW, P_PSC, P_DSK = 0, 16, 32, 48, 64, 80
P_CW, P_CB, P_DTB, P_ALOG, P_VALID, P_ICNT = 96, 192, 216, 248, 280, 296
NPAR = 296 + 128


class Sched:
    def __init__(self, nc):
        self.nc = nc
        self.ops = []
        self.tok_w = {}
        self.tok_r = {}
        self.last = {e: None for e in ENGS}
        self.pending_barrier = {e: set() for e in ENGS}

    def op(self, eng, fn, r=(), w=(), dma=False):
        if getattr(self, "cap", None) is not None:
            self.cap[-1].append((eng, fn, tuple(r), tuple(w), dma))
            return None
        oid = len(self.ops)
        deps = {}
        for t in r:
            if t in self.tok_w:
                deps[self.tok_w[t]] = True
        for t in w:
            if t in self.tok_w:
                deps.setdefault(self.tok_w[t], False)
            for x in self.tok_r.get(t, ()):
                deps.setdefault(x, False)
        for x in self.pending_barrier[eng]:
            deps[x] = True
        self.pending_barrier[eng] = set()
        deps.pop(oid, None)
        best = {}
        out = {}
        for d, raw in deps.items():
            p = self.ops[d]
            if p["dma"]:
                out[d] = raw
                continue
            if p["eng"] == eng and eng == "pe" and not dma and not raw:
                continue
            pe_ = p["eng"]
            if pe_ not in best or d > best[pe_]:
                best[pe_] = d
        for pe_, d in best.items():
            out[d] = True
        self.ops.append(dict(eng=eng, fn=fn, deps=out, dma=dma, id=oid))
        for t in r:
            lst = self.tok_r.setdefault(t, [])
            if not dma:
                lst[:] = [x for x in lst if self.ops[x]["dma"] or self.ops[x]["eng"] != eng]
            lst.append(oid)
        for t in w:
            self.tok_w[t] = oid
            self.tok_r[t] = []
        self.last[eng] = oid
        return oid

    def begin_capture(self):
        self.cap = [[]]

    def new_block(self):
        if getattr(self, "cap", None) is not None and self.cap[-1]:
            self.cap.append([])

    def end_capture(self):
        blocks = [b for b in self.cap if b]
        self.cap = None
        return blocks

    def replay(self, blocks):
        for b in blocks:
            for (eng, fn, r, w, dma) in b:
                self.op(eng, fn, r=r, w=w, dma=dma)

    def interleave(self, A, B):
        na = sum(len(b) for b in A)
        nb = sum(len(b) for b in B)
        ia = ib = 0
        da = db = 0
        while ia < len(A) or ib < len(B):
            fa = da / na if na else 1.0
            fb = db / nb if nb else 1.0
            if ib >= len(B) or (ia < len(A) and fa <= fb):
                self.replay([A[ia]])
                da += len(A[ia])
                ia += 1
            else:
                self.replay([B[ib]])
                db += len(B[ib])
                ib += 1

    def barrier(self):
        lasts = {self.last[e] for e in ENGS if self.last[e] is not None}
        for e in ENGS:
            self.pending_barrier[e] |= lasts

    def emit(self, stack, final_wait_eng="sp", nds=32):
        nc = self.nc
        ops = self.ops
        esem = {e: stack.enter_context(nc.semaphore("s_" + e)) for e in ENGS}
        dsem = [stack.enter_context(nc.semaphore("d_%d" % i)) for i in range(nds)]
        dcount = [0] * nds
        dma_prev = [None] * nds
        npool = (nds * 2) // 3
        kk = {"pool": 0, "sp": 0}
        for o in ops:
            if o["dma"]:
                if o["eng"] == "pool":
                    s = kk["pool"] % npool
                    kk["pool"] += 1
                else:
                    s = npool + kk["sp"] % (nds - npool)
                    kk["sp"] += 1
                if dma_prev[s] is not None:
                    o["deps"].setdefault(dma_prev[s], True)
                o["dsem"] = s
                dma_prev[s] = o["id"]
        needed = set()
        for o in ops:
            needed |= set(o["deps"])
        ecount = {e: 0 for e in ENGS}
        ref = {}
        for o in ops:
            if o["dma"]:
                s = o["dsem"]
                dcount[s] += 16
                ref[o["id"]] = (dsem[s], dcount[s])
            elif o["id"] in needed:
                ecount[o["eng"]] += 1
                ref[o["id"]] = (esem[o["eng"]], ecount[o["eng"]])
        dma_ids = [o["id"] for o in ops if o["dma"]]
        per = {e: [o for o in ops if o["eng"] == e] for e in ENGS}
        block = stack.enter_context(nc.Block())
        self.nwaits = 0

        def mk(e):
            def body(engine):
                waited = {}
                for o in per[e]:
                    for d in sorted(o["deps"]):
                        sem, val = ref[d]
                        if waited.get(id(sem), 0) >= val:
                            continue
                        engine.wait_ge(sem, val)
                        self.nwaits += 1
                        waited[id(sem)] = val
                    meth, a, kw = o["fn"]
                    ins = getattr(engine, meth)(*a, **kw)
                    if o["id"] in ref:
                        sem, val = ref[o["id"]]
                        ins.then_inc(sem, 16 if o["dma"] else 1)
                if e == final_wait_eng:
                    for d in dma_ids:
                        sem, val = ref[d]
                        if waited.get(id(sem), 0) < val:
                            engine.wait_ge(sem, val)
                            waited[id(sem)] = val
            return body

        block.tensor(mk("pe"))
        block.scalar(mk("act"))
        block.vector(mk("dve"))
        block.gpsimd(mk("pool"))
        block.sync(mk("sp"))


class Arena:
    def __init__(self, tensor, nbytes):
        self.t = tensor
        self.nbytes = nbytes
        self.off = 0
        self.peak = 0

    def alloc(self, shape, dt):
        esz = 4 if dt == F32 else 2
        n = 1
        for s in shape:
            n *= s
        nb = n * esz
        off = (self.off + 3) // 4 * 4
        assert off + nb <= self.nbytes, ("SBUF arena overflow", off + nb, self.nbytes)
        self.off = off + nb
        self.peak = max(self.peak, self.off)
        v = self.t[:, off // 2: (off + nb) // 2]
        if dt == F32:
            v = v.bitcast(F32)
        if len(shape) == 2:
            v = v.rearrange("p (a b) -> p a b", b=shape[1])
        elif len(shape) == 3:
            v = v.rearrange("p (a b c) -> p a b c", b=shape[1], c=shape[2])
        return v

    def mark(self):
        return self.off

    def reset(self, m):
        self.off = m


def build_nc(npre=NPRE, do_own=True, do_ffn=True, dbg=False, lvl=9):
    nc = bass.Bass("TRN2", target_bir_lowering=False)
    dt_in = lambda name, shape: nc.dram_tensor(name, shape, F32, kind="ExternalInput").ap()
    xall = dt_in("xall", [NHB * D, HT])
    w_in = dt_in("w_in", [D, 7200])
    pool_w = dt_in("pool_w", [4 * 512, 512])
    w_out = dt_in("w_out", [4096, D])
    w_gate = dt_in("w_gate", [D, DFF])
    w_up = dt_in("w_up", [D, DFF])
    w_down = dt_in("w_down", [DFF, D])
    consts_d = dt_in("consts", [128, NCONST])
    params_d = dt_in("params", [128, NPAR])
    outT = nc.dram_tensor("outT", [D, TOK], F32, kind="ExternalOutput").ap()
    if dbg:
        dbg_S = nc.dram_tensor("dbg_S", [128, 2048], F32, kind="ExternalOutput").ap()
        dbg_R = nc.dram_tensor("dbg_R", [128, 2 * 16 * HT], F32, kind="ExternalOutput").ap()
        dbg_H = nc.dram_tensor("dbg_H", [128, 16 * HT], F32, kind="ExternalOutput").ap()

    xall_v = xall.rearrange("(h c p) t -> h p c t", c=16, p=128)
    w_in_v = w_in.rearrange("(kc p) n -> p kc n", p=128)
    w_out_v = w_out.rearrange("(kc p) n -> p kc n", p=128)
    pool_w_v = pool_w.rearrange("(g kc p) n -> g p kc n", g=4, p=128)
    w_gate_v = w_gate.rearrange("(kc p) n -> p kc n", p=128)
    w_up_v = w_up.rearrange("(kc p) n -> p kc n", p=128)
    w_down_v = w_down.rearrange("(kc p) n -> p kc n", p=128)
    outT_v = outT.rearrange("(c p) t -> p c t", p=128)

    with ExitStack() as st:
        TOTAL = 212800
        arena_t = st.enter_context(nc.sbuf_tensor("arena", [128, TOTAL // 2], BF16))
        pbank = [st.enter_context(nc.psum_tensor("pb%d" % i, [128, 512], F32)) for i in range(8)]
        A = Arena(arena_t, TOTAL)
        S = Sched(nc)

        def I(eng, meth, r=(), w=(), dma=False, a=(), **kw):
            S.op(eng, (meth, tuple(a), kw), r=r, w=w, dma=dma)

        cst = A.alloc([NCONST], F32)
        prm = A.alloc([NPAR], F32)
        ones_bf = A.alloc([128], BF16)
        ident_bf = A.alloc([128], BF16)
        LT_bf = A.alloc([128], BF16)
        a_b = A.alloc([32], F32)
        carry = A.alloc([24, 3], F32)
        M_S = A.mark()
        Sst = A.alloc([4, 512], F32)
        hn = A.alloc([16, 16 + HT], BF16)
        M_R = A.mark()
        R = [A.alloc([16, HT], F32), A.alloc([16, HT], F32)]
        PH = A.mark()

        onesf = cst[:, K_ONES:K_ONES + 128]
        LTf = cst[:, K_LT:K_LT + 128]
        T0 = cst[:, K_T0:K_T0 + 256]
        T1h = cst[:, K_T1 + 128:K_T1 + 256]
        caus = cst[:, K_CAUS:K_CAUS + 384]
        identf = cst[:, K_ID:K_ID + 128]

        def pcol(off, i=0, n=1):
            return prm[:, off + i: off + i + n]

        def dma_w(dst, src, tok, eng="pool"):
            I(eng, "dma_start", w=[tok], dma=True, out=dst, in_=src)

        mm_banks = [[0, 1, 2, 7]]
        mm_rr = [0]

        def next_mm():
            b = mm_banks[0]
            i = b[mm_rr[0] % len(b)]
            mm_rr[0] += 1
            return i

        pCB = pbank[3][:, 0:384]
        pdt = pbank[3][:, 384:512].rearrange("p (a b) -> p a b", b=32)
        pss = pbank[4][:, :]
        ptr = pbank[5][:, 0:256].bitcast(BF16).rearrange("p (a b) -> p a b", b=128)
        pY = pbank[5][:, 256:512]
        pYo = pbank[6][:, 0:256]
        pra = [pbank[6][:, 256 + q * 96: 256 + (q + 1) * 96].rearrange("p (a b) -> p a b", b=32) for q in range(2)]
        ph = pbank[6][:, 448:464]
        pcc = pbank[6][:, 464:468]
        pAbs = [(pbank[7][:, 0:256], "pb7"), (pbank[2][:, 0:256], "pb2")]
        pYs = [(pbank[5][:, 256:512], "pb5"), (pbank[3][:, 0:256], "pb3")]
        pYos = [(pbank[6][:, 0:256], "pb6"), (pbank[4][:, 256:512], "pb4")]

        I("sp", "dma_start", w=["cst"], dma=True, out=cst, in_=consts_d[:, :])
        I("sp", "dma_start", w=["prm"], dma=True, out=prm, in_=params_d[:, :])
        I("dve", "tensor_copy", r=["cst"], w=["ones_bf"], a=(ones_bf, onesf))
        I("dve", "tensor_copy", r=["cst"], w=["ident_bf"], a=(ident_bf, identf))
        I("dve", "tensor_copy", r=["cst"], w=["LT_bf"], a=(LT_bf, LTf))
        I("act", "activation", r=["prm"], w=["a_b"], out=a_b, in_=prm[:, P_ALOG:P_ALOG + 32], func=AF.Exp)
        I("dve", "tensor_scalar", r=["a_b"], w=["a_b"], out=a_b, in0=a_b, scalar1=-1.0, scalar2=None, op0=ALU.mult)
        I("pool", "memset", w=["carry%d" % m_ for m_ in range(24)], a=(carry, 0.0))
        I("pool", "memset", w=["S0", "S1", "S2", "S3"], a=(Sst, 0.0))
        I("pool", "memset", w=["hn"], a=(hn, 0.0))

        def norm_half(src, nw_off, dst_fn, tmp, dst_tok="hn", src_tok="R"):
            for c in range(16):
                sqb = tmp["sq"][c % 2]
                I("act", "activation", r=[src_tok], w=["sq%d" % (c % 2)], out=sqb, in_=src[:, c, :], func=AF.Square)
                I("pe", "matmul", r=["ones_bf", "sq%d" % (c % 2)], w=["pb4"], a=(pss, ones_bf, sqb),
                  start=(c == 0), stop=(c == 15))
            I("act", "activation", r=["pb4"], w=["sd"], out=tmp["sd"], in_=pss, func=AF.Sqrt, bias=EPS, scale=1.0 / D)
            I("dve", "reciprocal", r=["sd"], w=["rstd"], a=(tmp["rstd"], tmp["sd"]))
            for c in range(16):
                I("dve", "scalar_tensor_tensor", r=[src_tok, "prm", "rstd"], w=[dst_tok], out=dst_fn(c),
                  in0=src[:, c, :], scalar=pcol(nw_off, c), in1=tmp["rstd"], op0=ALU.mult, op1=ALU.mult)

        def mixer_phase_alloc(own):
            t = {}
            t["sq"] = [A.alloc([HT], BF16), A.alloc([HT], BF16)]
            t["sd"] = A.alloc([HT], F32)
            t["rstd"] = A.alloc([HT], F32)
            t["pre"] = [A.alloc([3 + HT + 1], F32) for _ in range(2 if own else 3)]
            t["acc"] = [A.alloc([HT], F32) for _ in range(2 if own else 3)]
            t["xc"] = [A.alloc([HT], BF16) for _ in range(0 if own else 3)]
            t["xtok"] = [A.alloc([4, 640], BF16) for _ in range(1 if own else 2)]
            t["xw"] = [A.alloc([2, 512], BF16) for _ in range(1 if own else 2)]
            for k in ("dtr", "dtabs", "dtl", "dt", "dtA", "lndt"):
                t[k] = A.alloc([4, 32], F32)
            t["RAraw"] = [A.alloc([3, 32], F32) for _ in range(2)]
            t["eR"] = [A.alloc([3, 32], F32) for _ in range(2)]
            t["wgt"] = [A.alloc([2, 32], F32) for _ in range(2)]
            t["negA"] = [A.alloc([2, 32], F32) for _ in range(2)]
            t["stmp"] = A.alloc([512], F32)
            t["d3"] = [A.alloc([4, 32], BF16) for _ in range(3)]
            t["rr"] = [A.alloc([4, 32], F32) for _ in range(2)]
            return t

        def dt_block(t, own, hb, WDT, hsrc=None, htok="hn"):
            hsrc = (lambda kc, tt: hn[:, kc, 16 + tt * 128: 16 + (tt + 1) * 128]) if hsrc is None else hsrc
            for tt in range(4):
                for kc in range(16):
                    I("pe", "matmul", r=[htok, "wdt"], w=["pb3"],
                      a=(pdt[:, tt, :], hsrc(kc, tt), WDT[:, kc, :]),
                      start=(kc == 0), stop=(kc == 15))
            bias_b = prm[:, P_DTB:P_DTB + 32].unsqueeze(1).to_broadcast([128, 4, 32])
            I("dve", "tensor_tensor", r=["pb3", "prm"], w=["dtr"], out=t["dtr"], in0=pdt, in1=bias_b, op=ALU.add)
            I("act", "activation", r=["dtr"], w=["dtabs"], out=t["dtabs"], in_=t["dtr"], func=AF.Abs)
            I("act", "activation", r=["dtabs"], w=["dtabs"], out=t["dtabs"], in_=t["dtabs"], func=AF.Exp, scale=-1.0)
            I("act", "activation", r=["dtabs"], w=["dtl"], out=t["dtl"], in_=t["dtabs"], func=AF.Ln, bias=1.0, scale=1.0)
            I("dve", "scalar_tensor_tensor", r=["dtr", "dtl"], w=["dt"], out=t["dt"], in0=t["dtr"], scalar=0.0,
              in1=t["dtl"], op0=ALU.max, op1=ALU.add)
            if not own:
                I("dve", "tensor_scalar", r=["dt", "prm"], w=["dt"], out=t["dt"], in0=t["dt"],
                  scalar1=pcol(P_VALID, hb), scalar2=None, op0=ALU.mult)
            a_bb = a_b.unsqueeze(1).to_broadcast([128, 4, 32])
            I("dve", "tensor_tensor", r=["dt", "a_b"], w=["dtA"], out=t["dtA"], in0=t["dt"], in1=a_bb, op=ALU.mult)
            if own:
                I("act", "activation", r=["dt"], w=["lndt"], out=t["lndt"], in_=t["dt"], func=AF.Ln)
            d3, rr = t["d3"], t["rr"]
            I("dve", "tensor_copy", r=["dtA"], w=["d3_0"], a=(d3[0], t["dtA"]))
            I("dve", "tensor_tensor", r=["dtA", "d3_0"], w=["rr0"], out=rr[0], in0=t["dtA"], in1=d3[0], op=ALU.subtract)
            I("dve", "tensor_copy", r=["rr0"], w=["d3_1"], a=(d3[1], rr[0]))
            I("dve", "tensor_tensor", r=["rr0", "d3_1"], w=["rr1"], out=rr[1], in0=rr[0], in1=d3[1], op=ALU.subtract)
            I("dve", "tensor_copy", r=["rr1"], w=["d3_2"], a=(d3[2], rr[1]))
            rd = ["LT_bf", "ones_bf", "d3_0", "d3_1", "d3_2"]
            for q in range(2):
                ta, tb = 2 * q, 2 * q + 1
                p_ = pra[q]
                tk = "pb6"
                for k in range(3):
                    I("pe", "matmul", r=rd, w=[tk], a=(p_[:, 0, :], LT_bf, d3[k][:, ta, :]), start=(k == 0), stop=False)
                    I("pe", "matmul", r=rd, w=[tk], a=(p_[:, 0, :], ones_bf, d3[k][:, tb, :]), start=False, stop=(k == 2))
                for k in range(3):
                    I("pe", "matmul", r=rd, w=[tk], a=(p_[:, 1, :], LT_bf, d3[k][:, tb, :]), start=(k == 0), stop=(k == 2))
                for k in range(3):
                    I("pe", "matmul", r=rd, w=[tk], a=(p_[:, 2, :], ones_bf, d3[k][:, ta, :]), start=(k == 0), stop=False)
                    I("pe", "matmul", r=rd, w=[tk], a=(p_[:, 2, :], ones_bf, d3[k][:, tb, :]), start=False, stop=(k == 2))
                I("act", "activation", r=[tk], w=["eR%d" % q], out=t["eR"][q], in_=p_, func=AF.Exp)
                I("dve", "tensor_tensor", r=["eR%d" % q, "dt"], w=["wgt%d" % q], out=t["wgt"][q],
                  in0=t["eR"][q][:, 0:2, :], in1=t["dt"][:, ta:ta + 2, :], op=ALU.mult)
                if own:
                    I("act", "copy", r=[tk], w=["RAraw%d" % q], a=(t["RAraw"][q], p_))
                    aend_b = t["RAraw"][q][:, 2:3, :].to_broadcast([128, 2, 32])
                    I("dve", "tensor_tensor", r=["RAraw%d" % q], w=["negA%d" % q], out=t["negA"][q],
                      in0=aend_b, in1=t["RAraw"][q][:, 0:2, :], op=ALU.subtract)

        def proj_chunk(wsrc, wtok, col0, rhs_fn, n, mmi, out=None, otok=None, htok="hn"):
            pm = pbank[mmi][:, 0:n] if out is None else out
            otok = otok or ("pb%d" % mmi)
            for kc in range(16):
                I("pe", "matmul", r=[wtok, htok], w=[otok], a=(pm, wsrc[:, kc, col0:col0 + 128], rhs_fn(kc)),
                  start=(kc == 0), stop=(kc == 15))
            return pm

        def conv_s1(t, pm, mmi, m, i):
            pre = t["pre"][i % len(t["pre"])]
            acc = t["acc"][i % len(t["acc"])]
            ptk, atk = "pre%d" % (i % len(t["pre"])), "acc%d" % (i % len(t["acc"]))
            I("act", "copy", r=["pb%d" % mmi], w=[ptk], a=(pre[:, 3:3 + HT], pm))
            I("act", "copy", r=["carry%d" % m], w=[ptk], a=(pre[:, 0:3], carry[:, m, :]))
            I("act", "activation", r=[ptk, "prm"], w=[atk], out=acc, in_=pre[:, 0:HT], func=AF.Identity,
              bias=pcol(P_CB, m), scale=pcol(P_CW, m * 4 + 0))

        def conv_s2(t, m, i):
            pre = t["pre"][i % len(t["pre"])]
            acc = t["acc"][i % len(t["acc"])]
            ptk, atk = "pre%d" % (i % len(t["pre"])), "acc%d" % (i % len(t["acc"]))
            for k in (1, 2, 3):
                I("dve", "scalar_tensor_tensor", r=[ptk, "prm", atk], w=[atk], out=acc, in0=pre[:, k:k + HT],
                  scalar=pcol(P_CW, m * 4 + k), in1=acc, op0=ALU.mult, op1=ALU.add)

        def conv_s3(t, m, dst, dst_tok, i):
            pre = t["pre"][i % len(t["pre"])]
            acc = t["acc"][i % len(t["acc"])]
            ptk, atk = "pre%d" % (i % len(t["pre"])), "acc%d" % (i % len(t["acc"]))
            I("act", "copy", r=[ptk], w=["carry%d" % m], a=(carry[:, m, :], pre[:, HT:HT + 3]))
            I("act", "activation", r=[atk], w=[dst_tok], out=dst, in_=acc, func=AF.Silu)

        def conv_silu(t, pm, mmi, m, dst, dst_tok, i):
            conv_s1(t, pm, mmi, m, i)
            conv_s2(t, m, i)
            conv_s3(t, m, dst, dst_tok, i)

        def transpose_pe(src, src_tok):
            for tt in range(4):
                I("pe", "transpose", r=[src_tok, "ident_bf"], w=["pb5"],
                  a=(ptr[:, tt, :], src[:, tt * 128:(tt + 1) * 128], ident_bf))

        def transpose_evac(xtok, xtok_tok, col0):
            I("act", "copy", r=["pb5"], w=[xtok_tok], a=(xtok[:, :, col0:col0 + 128], ptr))

        def transpose_to_tok(src, src_tok, xtok, xtok_tok, col0):
            for tt in range(4):
                I("pe", "transpose", r=[src_tok, "ident_bf"], w=["pb5"],
                  a=(ptr[:, tt, :], src[:, tt * 128:(tt + 1) * 128], ident_bf))
            I("act", "copy", r=["pb5"], w=[xtok_tok], a=(xtok[:, :, col0:col0 + 128], ptr))

        def state_update_a(t, g, q, xtok, xtok_tok, gi):
            ta, tb = 2 * q, 2 * q + 1
            xw = t["xw"][gi % len(t["xw"])]
            xwt = "xw%d" % (gi % len(t["xw"]))
            for j, tt in enumerate((ta, tb)):
                wb = t["wgt"][q][:, j, 8 * g:8 * g + 8].unsqueeze(2).to_broadcast([128, 8, 64])
                I("dve", "tensor_tensor", r=[xtok_tok, "wgt%d" % q], w=[xwt],
                  out=xw[:, j, :].rearrange("p (a b) -> p a b", b=64),
                  in0=xtok[:, tt, 0:512].rearrange("p (a b) -> p a b", b=64), in1=wb, op=ALU.mult)

        def state_update_b(t, g, q, xtok, xtok_tok, gi, pS=None, pStok="pb4"):
            pS = pss if pS is None else pS
            ta, tb = 2 * q, 2 * q + 1
            xw = t["xw"][gi % len(t["xw"])]
            xwt = "xw%d" % (gi % len(t["xw"]))
            for j, tt in enumerate((ta, tb)):
                I("pe", "matmul", r=[xtok_tok, xwt], w=[pStok], a=(pS, xtok[:, tt, 512:640], xw[:, j, :]),
                  start=(j == 0), stop=(j == 1))
            decb = t["eR"][q][:, 2, 8 * g:8 * g + 8].unsqueeze(2).to_broadcast([128, 8, 64])
            Sg = Sst[:, g, :]
            I("dve", "tensor_tensor", r=["S%d" % g, "eR%d" % q], w=["stmp"],
              out=t["stmp"].rearrange("p (a b) -> p a b", b=64),
              in0=Sg.rearrange("p (a b) -> p a b", b=64), in1=decb, op=ALU.mult)
            I("dve", "tensor_tensor", r=["stmp", pStok], w=["S%d" % g], out=Sg, in0=t["stmp"], in1=pS, op=ALU.add)

        def state_update(t, g, q, xtok, xtok_tok, gi, pS=None, pStok="pb4"):
            state_update_a(t, g, q, xtok, xtok_tok, gi)
            state_update_b(t, g, q, xtok, xtok_tok, gi, pS=pS, pStok=pStok)

        def run_pipeline(items, stages, hooks=None):
            nst = len(stages)
            for n in range(len(items) + nst - 1):
                for s in reversed(range(nst)):
                    k = n - s
                    if 0 <= k < len(items):
                        stages[s](items[k])
                if hooks and n in hooks:
                    fs = hooks[n]
                    for f in (fs if isinstance(fs, list) else [fs]):
                        f()

        A.off = PH - 16 * HT * 4
        WRES = A.alloc([16, 2560], BF16)
        WDT_P = A.alloc([16, 32], BF16)
        hnB = A.alloc([16, HT], BF16)
        tP = mixer_phase_alloc(False)
        dma_w(WDT_P, w_in_v[:, :, C_DT:C_DT + 32], "wdt")
        for j in (0, 4, 1, 2, 3):
            dma_w(WRES[:, :, j * 512:(j + 1) * 512], w_in_v[:, :, C_X + j * 512: C_X + (j + 1) * 512], "wres%d" % j)

        ci = [0]
        gi = [0]
        rhs_main = lambda kc: hn[:, kc, 16:16 + HT]

        def norm_stats(src, tmp, src_tok):
            for c in range(16):
                sqb = tmp["sq"][c % 2]
                I("act", "activation", r=[src_tok], w=["sq%d" % (c % 2)], out=sqb, in_=src[:, c, :], func=AF.Square)
                I("pe", "matmul", r=["ones_bf", "sq%d" % (c % 2)], w=["pb4"], a=(pss, ones_bf, sqb),
                  start=(c == 0), stop=(c == 15))
            I("act", "activation", r=["pb4"], w=["sd"], out=tmp["sd"], in_=pss, func=AF.Sqrt, bias=EPS, scale=1.0 / D)
            I("dve", "reciprocal", r=["sd"], w=["rstd"], a=(tmp["rstd"], tmp["sd"]))

        def norm_apply(src, nw_off, dst_fn, tmp, dst_tok, src_tok):
            for c in range(16):
                I("dve", "scalar_tensor_tensor", r=[src_tok, "prm", "rstd"], w=[dst_tok], out=dst_fn(c),
                  in0=src[:, c, :], scalar=pcol(nw_off, c), in1=tmp["rstd"], op0=ALU.mult, op1=ALU.mult)

        pS_P = pbank[3][:, :]
        hbs = list(range(NPRE - npre, NPRE))
        NH = len(hbs)

        def hbuf(h):
            if (NH - 1 - h) % 2 == 0:
                return (lambda kc: hn[:, kc, 16:16 + HT]), "hn", (lambda c: hn[:, c, 16:16 + HT]), \
                       (lambda kc, tt: hn[:, kc, 16 + tt * 128: 16 + (tt + 1) * 128])
            return (lambda kc: hnB[:, kc, :]), "hnB", (lambda c: hnB[:, c, :]), \
                   (lambda kc, tt: hnB[:, kc, tt * 128:(tt + 1) * 128])

        info = []
        for h, hb in enumerate(hbs):
            rhs_fn, htok, _, _ = hbuf(h)
            for g in range(4):
                for i in range(5):
                    if i == 0:
                        cur_x = (tP["xtok"][gi[0] % 2], "xtok%d" % (gi[0] % 2), gi[0])
                        gi[0] += 1
                    if i < 4:
                        col0, m, dcol = g * 512 + i * 128, g * 4 + i, i * 128
                    else:
                        col0, m, dcol = 2048 + g * 128, 16 + g, 512
                    info.append(dict(h=h, g=g, i=i, col0=col0, m=m, dcol=dcol, xt=cur_x, ci=ci[0], xc=tP["xc"][ci[0] % 3],
                                     xct="xc%d" % (ci[0] % 3), rhs=rhs_fn, htok=htok))
                    ci[0] += 1

        def st0(it):
            it["mmi"] = next_mm()
            it["pm"] = proj_chunk(WRES, "wres%d" % (it["col0"] // 512), it["col0"], it["rhs"], HT, it["mmi"], htok=it["htok"])

        def st1(it):
            conv_s1(tP, it["pm"], it["mmi"], it["m"], it["ci"])

        def st2(it):
            conv_s2(tP, it["m"], it["ci"])

        def st3(it):
            conv_s3(tP, it["m"], it["xc"], it["xct"], it["ci"])

        def st4(it):
            transpose_pe(it["xc"], it["xct"])

        def st5(it):
            transpose_evac(it["xt"][0], it["xt"][1], it["dcol"])

        def st6(it):
            if it["i"] == 4 and lvl >= 4:
                xt, xk, gx = it["xt"]
                for q in range(2):
                    state_update_a(tP, it["g"], q, xt, xk, gx * 2 + q)

        def st7(it):
            pass

        def st8(it):
            if it["i"] == 4 and lvl >= 4:
                xt, xk, gx = it["xt"]
                for q in range(2):
                    state_update_b(tP, it["g"], q, xt, xk, gx * 2 + q, pS=pS_P, pStok="pb3")

        def load_x(h):
            I("sp", "dma_start", w=["R"], dma=True, out=R[0], in_=xall_v[hbs[h]])

        def do_stats(h):
            norm_stats(R[0], tP, "R")

        def do_apply(h):
            _, htok, dstf, _ = hbuf(h)
            norm_apply(R[0], P_NW1, dstf, tP, htok, "R")
            if h + 1 < NH:
                load_x(h + 1)

        def do_dt(h):
            _, htok, _, dsrc = hbuf(h)
            dt_block(tP, False, hbs[h], WDT_P, hsrc=dsrc, htok=htok)

        hooks = {}
        if NH:
            load_x(0)
            do_stats(0)
            do_apply(0)
            do_dt(0)
            for h in range(NH - 1):
                bstep = 20 * h
                hooks.setdefault(bstep + 3, []).append(lambda h=h: do_stats(h + 1))
                hooks.setdefault(bstep + 8, []).append(lambda h=h: do_apply(h + 1))
                hooks.setdefault(bstep + 28, []).append(lambda h=h: do_dt(h + 1))
            run_pipeline(info, [st0, st1, st2, st3, st4, st5, st6, st7, st8], hooks=hooks)
        peakP = A.peak
        if dbg:
            I("sp", "dma_start", r=["S0", "S1", "S2", "S3"], dma=True, out=dbg_S[:, :], in_=Sst.rearrange("p a b -> p (a b)"))
            for c_ in range(16):
                I("act", "copy", r=["hn"], w=["R"], a=(R[0][:, c_, :], hn[:, c_, 16:16 + HT]))
            I("sp", "dma_start", r=["R"], dma=True, out=dbg_H[:, :], in_=R[0].rearrange("p a b -> p (a b)"))

        S.barrier()
        mm_banks[0] = [0, 1]
        A.reset(PH)
        WDT_M = A.alloc([16, 32], BF16)
        tM = mixer_phase_alloc(True)
        WS = [A.alloc([16, 512], BF16) for _ in range(2)]
        bt = A.alloc([HT], BF16)
        ctt = A.alloc([HT], BF16)
        xTg = A.alloc([4, HT], BF16)
        szg = A.alloc([4, HT], BF16)
        T0_bf = A.alloc([256], BF16)
        T1h_bf = A.alloc([128], BF16)
        I("dve", "tensor_copy", r=["cst"], w=["T0_bf"], a=(T0_bf, T0))
        I("dve", "tensor_copy", r=["cst"], w=["T0_bf"], a=(T1h_bf, T1h))
        arg0 = [A.alloc([256], F32) for _ in range(2)]
        arg1 = [A.alloc([128], F32) for _ in range(2)]
        E0 = [tM["sd"][:, 0:256], tM["sd"][:, 256:512]]
        E1 = [tM["rstd"][:, 0:128], tM["rstd"][:, 128:256]]
        M1 = [tM["rstd"][:, 256:384].bitcast(BF16)[:, 0:128], tM["rstd"][:, 256:384].bitcast(BF16)[:, 128:256]]
        M0 = [tM["sq"][0][:, 0:256], tM["sq"][0][:, 256:512]]
        CBm0_ = A.alloc([384], F32)
        CBms = [(CBm0_, ["CBm"]), (tM["acc"][0][:, 0:384], ["acc0"])]
        eA = [A.alloc([256], F32) for _ in range(2)]
        Sbf0_ = A.alloc([512], BF16)
        Sbfs = [(Sbf0_, ["Sbf"]), (tM["acc"][1].bitcast(BF16)[:, 0:512], ["acc1"])]
        gbuf0_ = A.alloc([4, 256], F32)
        gb1a = tM["pre"][0][:, 0:512].rearrange("p (a b) -> p a b", b=256)
        gb1b = tM["pre"][1][:, 0:512].rearrange("p (a b) -> p a b", b=256)
        gbufs = [([gbuf0_[:, i, :] for i in range(4)], ["gbuf"]),
                 ([gb1a[:, 0, :], gb1a[:, 1, :], gb1b[:, 0, :], gb1b[:, 1, :]], ["pre0", "pre1"])]
        ytmp = [A.alloc([256], F32) for _ in range(2)]
        gsq = [A.alloc([256], BF16) for _ in range(2)]
        rstd_g = A.alloc([256], F32)
        sd_g = A.alloc([256], F32)
        yss = A.alloc([4, HT], BF16)
        ubuf = A.alloc([16 + HT], F32)
        pp = [A.alloc([16 + HT], F32) for _ in range(2)]
        pooled = A.alloc([4, HT], BF16)
        ypool = A.alloc([4, HT], BF16)
        dma_w(WDT_M, w_in_v[:, :, C_DT:C_DT + 32], "wdt")

        ws_rr = [0]

        def load_ws(srcs):
            i = ws_rr[0] % 2
            ws_rr[0] += 1
            ws = WS[i]
            tok = "ws%d" % i
            for dfn, src in srcs:
                dma_w(dfn(ws), src, tok)
            return ws, tok

        T0b = T0.unsqueeze(1).to_broadcast([128, 2, 256])
        T1b = T1h.unsqueeze(1).to_broadcast([128, 2, 128])
        pG = pbank[4][:, 0:256]

        for hh in (range(2) if do_own else ()):
            hb = NPRE + hh
            if hh == 1:
                S.barrier()
            Rh = R[hh]
            rt = "R%d" % hh
            I("sp", "dma_start", w=[rt], dma=True, out=Rh, in_=xall_v[hb])
            I("dve", "tensor_copy", r=["hn"], w=["hn"], a=(hn[:, :, 0:16], hn[:, :, HT:HT + 16]))
            norm_half(Rh, P_NW1, lambda c: hn[:, c, 16:16 + HT], tM, src_tok=rt)
            dt_block(tM, True, hb, WDT_M)
            xtok = tM["xtok"][0]
            xtk = "xtok0"
            hcnt = [0]

            def o_st0(it):
                it["mmi"] = next_mm()
                it["pm"] = proj_chunk(it["ws"], it["wt"], it["wcol"], rhs_main, HT, it["mmi"])
                if it["kind"] == "u":
                    proj_chunk(it["ws"], it["wt"], it["wcol"], lambda kc: hn[:, kc, 0:16], 16, 6, out=ph, otok="pb6")

            def o_st1(it):
                k_ = it["kind"]
                if k_ in ("x", "B", "C"):
                    conv_s1(tM, it["pm"], it["mmi"], it["m"], it["ci"])
                    conv_s2(tM, it["m"], it["ci"])
                elif k_ == "z":
                    I("act", "activation", r=["pb%d" % it["mmi"]], w=["szg"], out=szg[:, it["i"], :], in_=it["pm"], func=AF.Silu)
                else:
                    I("act", "copy", r=["pb%d" % it["mmi"]], w=["ubuf"], a=(ubuf[:, 16:16 + HT], it["pm"]))
                    I("act", "copy", r=["pb6"], w=["ubuf"], a=(ubuf[:, 0:16], ph))

            def o_st2(it):
                k_ = it["kind"]
                if k_ in ("x", "B", "C"):
                    conv_s3(tM, it["m"], it["dst"], it["dtok"], it["ci"])
                elif k_ == "u":
                    g_, i_ = it["g"], it["i"]
                    wwin = float(1 << (g_ + 1))
                    src, stok, lo = ubuf, "ubuf", 0
                    for k in range(g_ + 1):
                        sh = 1 << k
                        dst, dtok = pp[k % 2], "pp%d" % (k % 2)
                        lo2 = lo + sh
                        I("dve", "tensor_tensor", r=[stok], w=[dtok], out=dst[:, lo2:16 + HT], in0=src[:, lo2:16 + HT],
                          in1=src[:, lo2 - sh:16 + HT - sh], op=ALU.add)
                        src, stok, lo = dst, dtok, lo2
                    I("dve", "scalar_tensor_tensor", r=[stok, "ubuf"], w=["pooled"], out=pooled[:, i_, 16:HT],
                      in0=src[:, 32:16 + HT], scalar=1.0 / wwin, in1=ubuf[:, 32:16 + HT], op0=ALU.mult, op1=ALU.subtract)
                    oth = pp[(g_ + 1) % 2]
                    otk = "pp%d" % ((g_ + 1) % 2)
                    io = P_ICNT + it["hh"] * 64 + g_ * 16
                    I("dve", "tensor_tensor", r=[stok, "prm"], w=[otk], out=oth[:, 0:16], in0=src[:, 16:32],
                      in1=prm[:, io:io + 16], op=ALU.mult)
                    I("dve", "tensor_tensor", r=[otk, "ubuf"], w=["pooled"], out=pooled[:, i_, 0:16], in0=oth[:, 0:16],
                      in1=ubuf[:, 16:32], op=ALU.subtract)

            def o_st3(it):
                if it["kind"] in ("x", "B"):
                    transpose_pe(it["dst"], it["dtok"])

            def o_st4(it):
                if it["kind"] in ("x", "B"):
                    transpose_evac(xtok, xtk, it["dcol"])
                S.new_block()

            o_stages = [o_st0, o_st1, o_st2, o_st3, o_st4]

            def out_proj_part(kc0, src_t, src_tok):
                for dh in range(2):
                    ws, wt = load_ws([(lambda w_: w_[:, 0:8, :].rearrange("p (a b) c -> p a (b c)", b=2),
                                       w_out_v[:, kc0:kc0 + 4, dh * 1024:(dh + 1) * 1024])])
                    wv = ws[:, 0:8, :].rearrange("p (a b) c -> p a (b c)", b=2)
                    for dd in range(8):
                        d = dh * 8 + dd
                        mmi = next_mm()
                        pm = pbank[mmi][:, :]
                        for i in range(4):
                            I("pe", "matmul", r=[wt, src_tok], w=["pb%d" % mmi],
                              a=(pm, wv[:, i, dd * 128:(dd + 1) * 128], src_t[:, i, :]), start=(i == 0), stop=(i == 3))
                        I("dve", "tensor_tensor", r=[rt, "pb%d" % mmi], w=[rt], out=Rh[:, d, :], in0=Rh[:, d, :], in1=pm, op=ALU.add)
                        S.new_block()

            def ssd_group(g):
                for q in range(2):
                    ta, tb = 2 * q, 2 * q + 1
                    Sb, Sbt = Sbfs[q]
                    I("act", "copy", r=["S%d" % g], w=Sbt, a=(Sb, Sst[:, g, :]))
                    state_update(tM, g, q, xtok, xtk, gi[0])
                    gi[0] += 1
                    I("pe", "matmul", r=["bt", "ct"], w=["pb3"],
                      a=(pCB[:, 0:256], bt[:, ta * 128:(ta + 1) * 128], ctt[:, ta * 128: ta * 128 + 256]), start=True, stop=True)
                    I("pe", "matmul", r=["bt", "ct"], w=["pb3"],
                      a=(pCB[:, 256:384], bt[:, tb * 128:(tb + 1) * 128], ctt[:, tb * 128:(tb + 1) * 128]), start=True, stop=True)
                    cb, cbt = CBms[q]
                    I("dve", "tensor_tensor", r=["pb3", "cst"], w=cbt, out=cb, in0=pCB, in1=caus, op=ALU.mult)
                S.new_block()
                heads = []
                for q in range(2):
                    for h8 in range(8):
                        k = hcnt[0] % 2
                        pk = (hcnt[0] // 2) % 2
                        heads.append(dict(q=q, h=8 * g + h8, hp=h8 // 2, hx=h8 % 2, hc=h8 * 64, k=k, pk=pk))
                        hcnt[0] += 1

                def s0(it):
                    pass

                def s1(it):
                    k, h, q = it["k"], it["h"], it["q"]
                    pA, pAt = pAbs[k]
                    rd = ["T0_bf", "d3_0", "d3_1", "d3_2"]
                    for kk in range(3):
                        la = tM["d3"][kk][:, 2 * q, h:h + 1].to_broadcast([128, 128])
                        I("pe", "matmul", r=rd, w=[pAt], a=(pA, la, T0_bf), start=(kk == 0), stop=False)
                    for kk in range(3):
                        lb = tM["d3"][kk][:, 2 * q + 1, h:h + 1].to_broadcast([128, 128])
                        I("pe", "matmul", r=rd, w=[pAt], a=(pA[:, 128:256], lb, T1h_bf), start=False, stop=(kk == 2))

                def s2(it):
                    k, h, q = it["k"], it["h"], it["q"]
                    pA, pAt = pAbs[k]
                    negA = tM["negA"][q]
                    nq = "negA%d" % q
                    I("act", "activation", r=[pAt, nq], w=["arg0_%d" % k], out=arg0[k], in_=pA, func=AF.Relu,
                      bias=negA[:, 0, h:h + 1], scale=-1.0)
                    I("act", "activation", r=[pAt, nq], w=["arg1_%d" % k], out=arg1[k], in_=pA[:, 128:256], func=AF.Relu,
                      bias=negA[:, 1, h:h + 1], scale=-1.0)

                def s3(it):
                    k, h, hx, pk, q = it["k"], it["h"], it["hx"], it["pk"], it["q"]
                    pA, pAt = pAbs[k]
                    I("act", "activation", r=["arg0_%d" % k, "lndt"], w=["E0_%d" % k], out=E0[k], in_=arg0[k], func=AF.Exp,
                      bias=tM["lndt"][:, 2 * q, h:h + 1], scale=-1.0)
                    I("act", "activation", r=["arg1_%d" % k, "lndt"], w=["E1_%d" % k], out=E1[k], in_=arg1[k], func=AF.Exp,
                      bias=tM["lndt"][:, 2 * q + 1, h:h + 1], scale=-1.0)
                    I("act", "activation", r=[pAt], w=["eA%d" % pk], out=eA[pk][hx * 64:(hx + 1) * 64, :],
                      in_=pA[hx * 64:(hx + 1) * 64, :], func=AF.Exp)

                def s4(it):
                    k, q = it["k"], it["q"]
                    cb, cbt = CBms[q]
                    I("dve", "tensor_tensor", r=["E0_%d" % k] + cbt, w=["M0_%d" % k], out=M0[k], in0=E0[k], in1=cb[:, 0:256], op=ALU.mult)
                    I("dve", "tensor_tensor", r=["E1_%d" % k] + cbt, w=["M1_%d" % k], out=M1[k], in0=E1[k], in1=cb[:, 256:384], op=ALU.mult)

                def s5(it):
                    k, hx, hc, pk, hp, q = it["k"], it["hx"], it["hc"], it["pk"], it["hp"], it["q"]
                    pYv, pYt = pYs[pk]
                    I("pe", "matmul", r=[xtk, "M0_%d" % k], w=[pYt],
                      a=(pYv[hx * 64:(hx + 1) * 64, :], xtok[:, 2 * q, hc:hc + 64], M0[k]), start=True, stop=False)
                    I("pe", "matmul", r=[xtk, "M1_%d" % k], w=[pYt],
                      a=(pYv[hx * 64:(hx + 1) * 64, 128:256], xtok[:, 2 * q + 1, hc:hc + 64], M1[k]), start=False, stop=True)
                    if hx == 1:
                        pYov, pYot = pYos[pk]
                        Sb, Sbt = Sbfs[q]
                        I("pe", "matmul", r=Sbt + ["ct"], w=[pYot],
                          a=(pYov, Sb[:, hp * 128:(hp + 1) * 128], ctt[:, q * 256:(q + 1) * 256]), start=True, stop=True)

                def s6(it):
                    if it["hx"] != 1:
                        return
                    pk, hp, q = it["pk"], it["hp"], it["q"]
                    tks = slice(q * 256, (q + 1) * 256)
                    pYv, pYt = pYs[pk]
                    pYov, pYot = pYos[pk]
                    yt, ytk = ytmp[pk], "ytmp%d" % pk
                    gb, gbt = gbufs[q]
                    I("dve", "tensor_tensor", r=[pYot, "eA%d" % pk], w=[ytk], out=yt, in0=pYov, in1=eA[pk], op=ALU.mult)
                    I("dve", "tensor_tensor", r=[ytk, pYt], w=[ytk], out=yt, in0=yt, in1=pYv, op=ALU.add)
                    I("dve", "scalar_tensor_tensor", r=["xTg", "prm", ytk], w=[ytk], out=yt, in0=xTg[:, hp, tks],
                      scalar=pcol(P_DSK, g * 4 + hp), in1=yt, op0=ALU.mult, op1=ALU.add)
                    I("dve", "tensor_tensor", r=[ytk, "szg"], w=gbt, out=gb[hp], in0=yt, in1=szg[:, hp, tks], op=ALU.mult)

                def s7(it):
                    S.new_block()

                def epilogue(q):
                    tks = slice(q * 256, (q + 1) * 256)
                    gb, gbt = gbufs[q]
                    for hp in range(4):
                        gs = gsq[hp % 2]
                        I("act", "activation", r=gbt, w=["gsq%d" % (hp % 2)], out=gs, in_=gb[hp], func=AF.Square)
                        I("pe", "matmul", r=["ones_bf", "gsq%d" % (hp % 2)], w=["pb4"], a=(pG, ones_bf, gs),
                          start=(hp == 0), stop=(hp == 3))
                    I("act", "activation", r=["pb4"], w=["sd_g"], out=sd_g, in_=pG, func=AF.Sqrt, bias=EPS, scale=1.0 / 512)
                    I("dve", "reciprocal", r=["sd_g"], w=["rstd_g"], a=(rstd_g, sd_g))
                    for hp in range(4):
                        I("dve", "scalar_tensor_tensor", r=gbt + ["prm", "rstd_g"], w=["yss"], out=yss[:, hp, tks],
                          in0=gb[hp], scalar=pcol(P_SSDW, g * 4 + hp), in1=rstd_g, op0=ALU.mult, op1=ALU.mult)
                    S.new_block()

                run_pipeline(heads, [s0, s1, s2, s3, s4, s5, s6, s7], hooks={13: (lambda: epilogue(0))})
                epilogue(1)

            def pool_branch(g):
                ws, wt = load_ws([(lambda w_: w_[:, :, :], w_in_v[:, :, C_U + g * 512: C_U + (g + 1) * 512])])
                ws2, wt2 = load_ws([(lambda w_: w_[:, 0:4, :], pool_w_v[g])])
                its = [dict(kind="u", g=g, i=i, hh=hh, ws=ws, wt=wt, wcol=i * 128) for i in range(4)]
                run_pipeline(its, o_stages)
                for j in range(4):
                    mmi = next_mm()
                    pm = pbank[mmi][:, :]
                    for i in range(4):
                        I("pe", "matmul", r=[wt2, "pooled"], w=["pb%d" % mmi],
                          a=(pm, ws2[:, i, j * 128:(j + 1) * 128], pooled[:, i, :]), start=(i == 0), stop=(i == 3))
                    I("act", "mul", r=["pb%d" % mmi, "prm"], w=["ypool"], a=(ypool[:, j, :], pm, pcol(P_PSC, g * 4 + j)))
                    S.new_block()

            for g in range(4):
                wsx, wtx = load_ws([(lambda w_: w_[:, :, :], w_in_v[:, :, C_X + g * 512: C_X + (g + 1) * 512])])
                wsb, wtb = load_ws([(lambda w_: w_[:, :, 0:128], w_in_v[:, :, C_B + g * 128: C_B + (g + 1) * 128]),
                                    (lambda w_: w_[:, :, 128:256], w_in_v[:, :, C_C + g * 128: C_C + (g + 1) * 128])])
                its = []
                for i in range(4):
                    its.append(dict(kind="x", g=g, i=i, ws=wsx, wt=wtx, wcol=i * 128, m=g * 4 + i, dst=xTg[:, i, :], dtok="xTg",
                                    dcol=i * 128, ci=ci[0]))
                    ci[0] += 1
                its.append(dict(kind="B", g=g, i=0, ws=wsb, wt=wtb, wcol=0, m=16 + g, dst=bt, dtok="bt", dcol=512, ci=ci[0]))
                ci[0] += 1
                if hh == 0:
                    proj_chunk(wsb, wtb, 128, lambda kc: hn[:, kc, 12:16], 4, 6, out=pcc, otok="pb6")
                    I("act", "copy", r=["pb6"], w=["carry%d" % (20 + g)], a=(carry[:, 20 + g, :], pcc[:, 1:4]))
                its.append(dict(kind="C", g=g, i=0, ws=wsb, wt=wtb, wcol=128, m=20 + g, dst=ctt, dtok="ct", dcol=0, ci=ci[0]))
                ci[0] += 1
                run_pipeline(its, o_stages)
                wsz, wtz = load_ws([(lambda w_: w_[:, :, :], w_in_v[:, :, C_Z + g * 512: C_Z + (g + 1) * 512])])
                its = [dict(kind="z", g=g, i=i, ws=wsz, wt=wtz, wcol=i * 128) for i in range(4)]
                run_pipeline(its, o_stages)
                S.begin_capture()
                ssd_group(g)
                blkA = S.end_capture()
                S.begin_capture()
                if g > 0:
                    out_proj_part((g - 1) * 4, yss_prev[0], yss_prev[1])
                pool_branch(g)
                out_proj_part(16 + g * 4, ypool, "ypool")
                blkB = S.end_capture()
                nA0 = len(blkA) // 2
                nB1 = 16 if g > 0 else 0
                rest = blkB[nB1:]
                S.interleave(blkA[:nA0], blkB[:nB1] + rest[:len(rest) // 2])
                S.interleave(blkA[nA0:], rest[len(rest) // 2:])
                yss_prev = (yss, "yss")
                if g == 3:
                    out_proj_part(g * 4, yss, "yss")
        peakM = A.peak
        if dbg:
            for hh in range(2):
                I("sp", "dma_start", r=["R%d" % hh], dma=True, out=dbg_R[:, hh * 16 * HT:(hh + 1) * 16 * HT],
                  in_=R[hh].rearrange("p a b -> p (a b)"))

        S.barrier()
        A.reset(PH)
        mm_banks[0] = [0, 1, 2, 3, 5, 6, 7]
        A.off = M_S
        tF = {"sq": [A.alloc([HT], BF16), A.alloc([HT], BF16)], "sd": A.alloc([HT], F32), "rstd": A.alloc([HT], F32)}
        WD = [A.alloc([8, 512], BF16) for _ in range(2)]
        assert A.off <= M_R
        A.reset(PH)
        hn2 = A.alloc([16, TOK], BF16)
        act = [A.alloc([8, TOK], BF16) for _ in range(2)]
        WGU = [A.alloc([16, 512], BF16) for _ in range(2)]
        sg = [A.alloc([HT], F32) for _ in range(2)]
        ost = [A.alloc([HT], F32) for _ in range(2)]

        for hh in (range(2) if do_ffn else ()):
            norm_half(R[hh], P_NW2, lambda c, hh=hh: hn2[:, c, hh * HT:(hh + 1) * HT], tF, dst_tok="hn2", src_tok="R%d" % hh)

        groups = [(0, 8), (8, 8), (16, 8), (24, 8), (32, 8), (40, 4)] if do_ffn else []
        wgu_rr, wd_rr, sg_rr = [0], [0], [0]
        for G, (fc0, nfc) in enumerate(groups):
            ab = act[G % 2]
            at = "act%d" % (G % 2)
            for pr in range(nfc // 2):
                c0 = (fc0 + pr * 2) * 128
                i = wgu_rr[0] % 2
                wgu_rr[0] += 1
                wgu, wgt_ = WGU[i], "wgu%d" % i
                dma_w(wgu[:, :, 0:256], w_gate_v[:, :, c0:c0 + 256], wgt_)
                dma_w(wgu[:, :, 256:512], w_up_v[:, :, c0:c0 + 256], wgt_)
                for cc in range(2):
                    fl = pr * 2 + cc
                    for hh in range(2):
                        mg = next_mm()
                        pg = pbank[mg][:, :]
                        for kc in range(16):
                            I("pe", "matmul", r=[wgt_, "hn2"], w=["pb%d" % mg],
                              a=(pg, wgu[:, kc, cc * 128:(cc + 1) * 128], hn2[:, kc, hh * HT:(hh + 1) * HT]),
                              start=(kc == 0), stop=(kc == 15))
                        mu = next_mm()
                        pu = pbank[mu][:, :]
                        for kc in range(16):
                            I("pe", "matmul", r=[wgt_, "hn2"], w=["pb%d" % mu],
                              a=(pu, wgu[:, kc, 256 + cc * 128:256 + (cc + 1) * 128], hn2[:, kc, hh * HT:(hh + 1) * HT]),
                              start=(kc == 0), stop=(kc == 15))
                        si = sg_rr[0] % 2
                        sg_rr[0] += 1
                        I("act", "activation", r=["pb%d" % mg], w=["sg%d" % si], out=sg[si], in_=pg, func=AF.Silu)
                        I("dve", "tensor_tensor", r=["sg%d" % si, "pb%d" % mu], w=[at], out=ab[:, fl, hh * HT:(hh + 1) * HT],
                          in0=sg[si], in1=pu, op=ALU.mult)
            for db in range(4):
                i = wd_rr[0] % 2
                wd_rr[0] += 1
                wd, wdt_ = WD[i], "wd%d" % i
                dma_w(wd[:, 0:nfc, :], w_down_v[:, fc0:fc0 + nfc, db * 512:(db + 1) * 512], wdt_)
                for dd in range(4):
                    d = db * 4 + dd
                    for hh in range(2):
                        mmi = next_mm()
                        pm = pbank[mmi][:, :]
                        for f in range(nfc):
                            I("pe", "matmul", r=[wdt_, at], w=["pb%d" % mmi],
                              a=(pm, wd[:, f, dd * 128:(dd + 1) * 128], ab[:, f, hh * HT:(hh + 1) * HT]),
                              start=(f == 0), stop=(f == nfc - 1))
                        I("dve", "tensor_tensor", r=["R%d" % hh, "pb%d" % mmi], w=["R%d" % hh], out=R[hh][:, d, :],
                          in0=R[hh][:, d, :], in1=pm, op=ALU.add)
        oi = 0
        for hh in (range(2) if do_ffn else ()):
            src = R[hh]
            rt = "R%d" % hh
            for c in range(16):
                sqb = tF["sq"][c % 2]
                I("act", "activation", r=[rt], w=["sq%d" % (c % 2)], out=sqb, in_=src[:, c, :], func=AF.Square)
                I("pe", "matmul", r=["ones_bf", "sq%d" % (c % 2)], w=["pb4"], a=(pss, ones_bf, sqb),
                  start=(c == 0), stop=(c == 15))
            I("act", "activation", r=["pb4"], w=["sd"], out=tF["sd"], in_=pss, func=AF.Sqrt, bias=EPS, scale=1.0 / D)
            I("dve", "reciprocal", r=["sd"], w=["rstd"], a=(tF["rstd"], tF["sd"]))
            for c in range(16):
                o = ost[oi % 2]
                ot = "ost%d" % (oi % 2)
                oi += 1
                I("dve", "scalar_tensor_tensor", r=[rt, "prm", "rstd"], w=[ot], out=o, in0=src[:, c, :],
                  scalar=pcol(P_NW3, c), in1=tF["rstd"], op0=ALU.mult, op1=ALU.mult)
                I("sp", "dma_start", r=[ot], dma=True, out=outT_v[:, c, hh * HT:(hh + 1) * HT], in_=o)
        peakF = A.peak
        print("SBUF peaks P/M/F:", peakP, peakM, peakF, " ops:", len(S.ops))
        with ExitStack() as st2:
            S.emit(st2)
        print("waits:", S.nwaits)
    return nc


def _pc(v):
    return np.ascontiguousarray(np.asarray(v, np.float32).reshape(16, 128).T)


def _consts():
    c = np.zeros((128, NCONST), np.float32)
    j = np.arange(128)[:, None]
    l = np.arange(128)[None, :]
    tri = (j <= l).astype(np.float32)
    c[:, K_ONES:K_ONES + 128] = 1.0
    c[:, K_LT:K_LT + 128] = (j > l).astype(np.float32)
    c[:, K_T0:K_T0 + 128] = tri
    c[:, K_T0 + 128:K_T0 + 256] = 1.0
    c[:, K_T1 + 128:K_T1 + 256] = tri
    c[:, K_CAUS:K_CAUS + 128] = tri
    c[:, K_CAUS + 128:K_CAUS + 256] = 1.0
    c[:, K_CAUS + 256:K_CAUS + 384] = tri
    c[:, K_ID:K_ID + 128] = np.eye(128, dtype=np.float32)
    return c


def make_in_maps(x, attn_norm_w, w_in, conv_w, conv_b, dt_bias, a_log, d_skip, ssd_norm_w, pool_w,
                 pool_scale, w_out, ffn_norm_w, w_gate, w_up, w_down, final_norm_w):
    x = np.asarray(x, np.float32)
    xs = x.reshape(SEQ, D)
    consts = _consts()
    base = np.zeros((128, NPAR), np.float32)
    base[:, P_NW1:P_NW1 + 16] = _pc(np.asarray(attn_norm_w)[0])
    base[:, P_NW2:P_NW2 + 16] = _pc(np.asarray(ffn_norm_w)[0])
    base[:, P_NW3:P_NW3 + 16] = _pc(np.asarray(final_norm_w))
    base[:, P_SSDW:P_SSDW + 16] = _pc(np.asarray(ssd_norm_w)[0])
    base[:, P_PSC:P_PSC + 16] = _pc(np.asarray(pool_scale)[0])
    base[:, P_DSK:P_DSK + 16] = _pc(np.repeat(np.asarray(d_skip, np.float32)[0], 64))
    cw = np.asarray(conv_w, np.float32)[0]
    base[:, P_CW:P_CW + 96] = cw.reshape(4, 24, 128).transpose(2, 1, 0).reshape(128, 96)
    base[:, P_CB:P_CB + 24] = np.asarray(conv_b, np.float32)[0].reshape(24, 128).T
    base[:, P_DTB:P_DTB + 32] = np.asarray(dt_bias, np.float32)[0][None, :]
    base[:, P_ALOG:P_ALOG + 32] = np.asarray(a_log, np.float32)[0][None, :]

    w_in2 = np.ascontiguousarray(np.asarray(w_in, np.float32)[0])
    pool_w2 = np.ascontiguousarray(np.asarray(pool_w, np.float32)[0].reshape(4 * 512, 512))
    w_out2 = np.ascontiguousarray(np.asarray(w_out, np.float32)[0])
    w_gate2 = np.ascontiguousarray(np.asarray(w_gate, np.float32)[0])
    w_up2 = np.ascontiguousarray(np.asarray(w_up, np.float32)[0])
    w_down2 = np.ascontiguousarray(np.asarray(w_down, np.float32)[0])

    in_maps = []
    for c in range(NCORES):
        xall = np.zeros((NHB, D, HT), np.float32)
        prm = base.copy()
        for j in range(NHB):
            gh = 2 * c - NPRE + j
            if gh >= 0:
                xall[j] = xs[gh * HT:(gh + 1) * HT].T
                prm[:, P_VALID + j] = 1.0
        for hh in range(2):
            tg = c * TOK + hh * HT + np.arange(16)
            for g, wwin in enumerate((2, 4, 8, 16)):
                o = P_ICNT + hh * 64 + g * 16
                prm[:, o:o + 16] = (1.0 / np.minimum(tg + 1, wwin))[None, :]
        in_maps.append({
            "xall": xall.reshape(NHB * D, HT), "w_in": w_in2, "pool_w": pool_w2, "w_out": w_out2,
            "w_gate": w_gate2, "w_up": w_up2, "w_down": w_down2, "consts": consts, "params": prm,
        })
    return in_maps


_NC_CACHE = {}


def kernel(**inputs):
    in_maps = make_in_maps(**inputs)
    if "nc" not in _NC_CACHE:
        _NC_CACHE["nc"] = build_nc()
    nc = _NC_CACHE["nc"]
    res = run_bass_kernel_spmd(nc, in_maps, core_ids=list(range(NCORES)))
    out = np.empty((SEQ, D), np.float32)
    for c in range(NCORES):
        out[c * TOK:(c + 1) * TOK] = res.results[c]["outT"].T
    return out.reshape(1, SEQ, D)
```

```python
from contextlib import ExitStack
import numpy as np
import concourse.bass as bass
import concourse.mybir as mybir
from concourse.bass_utils import run_bass_kernel_spmd

F32 = mybir.dt.float32
BF16 = mybir.dt.bfloat16
AF = mybir.ActivationFunctionType
ALU = mybir.AluOpType

NCORES = 8
D = 2048
SEQ = 8192
TOK = SEQ // NCORES
HT = 512
NPRE = 14
NHB = NPRE + 2
DFF = 5632
NFF = DFF // 128
EPS = 1e-5
C_Z, C_X, C_B, C_C, C_DT, C_U = 0, 2048, 4096, 4608, 5120, 5152

ENGS = ("pe", "act", "dve", "pool", "sp")

K_ONES, K_LT, K_T0, K_T1, K_CAUS, K_ID = 0, 128, 256, 512, 768, 1152
NCONST = 1280
P_NW1, P_NW2, P_NW3, P_SSDW, P_PSC, P_DSK = 0, 16, 32, 48, 64, 80
P_CW, P_CB, P_DTB, P_ALOG, P_VALID, P_ICNT = 96, 192, 216, 248, 280, 296
NPAR = 296 + 128


class Sched:
    def __init__(self, nc):
        self.nc = nc
        self.ops = []
        self.tok_w = {}
        self.tok_r = {}
        self.last = {e: None for e in ENGS}
        self.pending_barrier = {e: set() for e in ENGS}

    def op(self, eng, fn, r=(), w=(), dma=False):
        if getattr(self, "cap", None) is not None:
            self.cap[-1].append((eng, fn, tuple(r), tuple(w), dma))
            return None
        oid = len(self.ops)
        deps = {}
        for t in r:
            if t in self.tok_w:
                deps[self.tok_w[t]] = True
        for t in w:
            if t in self.tok_w:
                deps.setdefault(self.tok_w[t], False)
            for x in self.tok_r.get(t, ()):
                deps.setdefault(x, False)
        for x in self.pending_barrier[eng]:
            deps[x] = True
        self.pending_barrier[eng] = set()
        deps.pop(oid, None)
        best = {}
        out = {}
        for d, raw in deps.items():
            p = self.ops[d]
            if p["dma"]:
                out[d] = raw
                continue
            if p["eng"] == eng and eng == "pe" and not dma and not raw:
                continue
            pe_ = p["eng"]
            if pe_ not in best or d > best[pe_]:
                best[pe_] = d
        for pe_, d in best.items():
            out[d] = True
        self.ops.append(dict(eng=eng, fn=fn, deps=out, dma=dma, id=oid))
        for t in r:
            lst = self.tok_r.setdefault(t, [])
            if not dma:
                lst[:] = [x for x in lst if self.ops[x]["dma"] or self.ops[x]["eng"] != eng]
            lst.append(oid)
        for t in w:
            self.tok_w[t] = oid
            self.tok_r[t] = []
        self.last[eng] = oid
        return oid

    def begin_capture(self):
        self.cap = [[]]

    def new_block(self):
        if getattr(self, "cap", None) is not None and self.cap[-1]:
            self.cap.append([])

    def end_capture(self):
        blocks = [b for b in self.cap if b]
        self.cap = None
        return blocks

    def replay(self, blocks):
        for b in blocks:
            for (eng, fn, r, w, dma) in b:
                self.op(eng, fn, r=r, w=w, dma=dma)

    def interleave(self, A, B):
        na = sum(len(b) for b in A)
        nb = sum(len(b) for b in B)
        ia = ib = 0
        da = db = 0
        while ia < len(A) or ib < len(B):
            fa = da / na if na else 1.0
            fb = db / nb if nb else 1.0
            if ib >= len(B) or (ia < len(A) and fa <= fb):
                self.replay([A[ia]])
                da += len(A[ia])
                ia += 1
            else:
                self.replay([B[ib]])
                db += len(B[ib])
                ib += 1

    def barrier(self):
        lasts = {self.last[e] for e in ENGS if self.last[e] is not None}
        for e in ENGS:
            self.pending_barrier[e] |= lasts

    def emit(self, stack, final_wait_eng="sp", nds=32):
        nc = self.nc
        ops = self.ops
        esem = {e: stack.enter_context(nc.semaphore("s_" + e)) for e in ENGS}
        dsem = [stack.enter_context(nc.semaphore("d_%d" % i)) for i in range(nds)]
        dcount = [0] * nds
        dma_prev = [None] * nds
        npool = (nds * 2) // 3
        kk = {"pool": 0, "sp": 0}
        for o in ops:
            if o["dma"]:
                if o["eng"] == "pool":
                    s = kk["pool"] % npool
                    kk["pool"] += 1
                else:
                    s = npool + kk["sp"] % (nds - npool)
                    kk["sp"] += 1
                if dma_prev[s] is not None:
                    o["deps"].setdefault(dma_prev[s], True)
                o["dsem"] = s
                dma_prev[s] = o["id"]
        needed = set()
        for o in ops:
            needed |= set(o["deps"])
        ecount = {e: 0 for e in ENGS}
        ref = {}
        for o in ops:
            if o["dma"]:
                s = o["dsem"]
                dcount[s] += 16
                ref[o["id"]] = (dsem[s], dcount[s])
            elif o["id"] in needed:
                ecount[o["eng"]] += 1
                ref[o["id"]] = (esem[o["eng"]], ecount[o["eng"]])
        dma_ids = [o["id"] for o in ops if o["dma"]]
        per = {e: [o for o in ops if o["eng"] == e] for e in ENGS}
        block = stack.enter_context(nc.Block())
        self.nwaits = 0

        def mk(e):
            def body(engine):
                waited = {}
                for o in per[e]:
                    for d in sorted(o["deps"]):
                        sem, val = ref[d]
                        if waited.get(id(sem), 0) >= val:
                            continue
                        engine.wait_ge(sem, val)
                        self.nwaits += 1
                        waited[id(sem)] = val
                    meth, a, kw = o["fn"]
                    ins = getattr(engine, meth)(*a, **kw)
                    if o["id"] in ref:
                        sem, val = ref[o["id"]]
                        ins.then_inc(sem, 16 if o["dma"] else 1)
                if e == final_wait_eng:
                    for d in dma_ids:
                        sem, val = ref[d]
                        if waited.get(id(sem), 0) < val:
                            engine.wait_ge(sem, val)
                            waited[id(sem)] = val
            return body

        block.tensor(mk("pe"))
        block.scalar(mk("act"))
        block.vector(mk("dve"))
        block.gpsimd(mk("pool"))
        block.sync(mk("sp"))


class Arena:
    def __init__(self, tensor, nbytes):
        self.t = tensor
        self.nbytes = nbytes
        self.off = 0
        self.peak = 0

    def alloc(self, shape, dt):
        esz = 4 if dt == F32 else 2
        n = 1
        for s in shape:
            n *= s
        nb = n * esz
        off = (self.off + 3) // 4 * 4
        assert off + nb <= self.nbytes, ("SBUF arena overflow", off + nb, self.nbytes)
        self.off = off + nb
        self.peak = max(self.peak, self.off)
        v = self.t[:, off // 2: (off + nb) // 2]
        if dt == F32:
            v = v.bitcast(F32)
        if len(shape) == 2:
            v = v.rearrange("p (a b) -> p a b", b=shape[1])
        elif len(shape) == 3:
            v = v.rearrange("p (a b c) -> p a b c", b=shape[1], c=shape[2])
        return v

    def mark(self):
        return self.off

    def reset(self, m):
        self.off = m


def build_nc(npre=NPRE, do_own=True, do_ffn=True, dbg=False, lvl=9):
    nc = bass.Bass("TRN2", target_bir_lowering=False)
    dt_in = lambda name, shape: nc.dram_tensor(name, shape, F32, kind="ExternalInput").ap()
    xall = dt_in("xall", [NHB * D, HT])
    w_in = dt_in("w_in", [D, 7200])
    pool_w = dt_in("pool_w", [4 * 512, 512])
    w_out = dt_in("w_out", [4096, D])
    w_gate = dt_in("w_gate", [D, DFF])
    w_up = dt_in("w_up", [D, DFF])
    w_down = dt_in("w_down", [DFF, D])
    consts_d = dt_in("consts", [128, NCONST])
    params_d = dt_in("params", [128, NPAR])
    outT = nc.dram_tensor("outT", [D, TOK], F32, kind="ExternalOutput").ap()
    if dbg:
        dbg_S = nc.dram_tensor("dbg_S", [128, 2048], F32, kind="ExternalOutput").ap()
        dbg_R = nc.dram_tensor("dbg_R", [128, 2 * 16 * HT], F32, kind="ExternalOutput").ap()
        dbg_H = nc.dram_tensor("dbg_H", [128, 16 * HT], F32, kind="ExternalOutput").ap()

    xall_v = xall.rearrange("(h c p) t -> h p c t", c=16, p=128)
    w_in_v = w_in.rearrange("(kc p) n -> p kc n", p=128)
    w_out_v = w_out.rearrange("(kc p) n -> p kc n", p=128)
    pool_w_v = pool_w.rearrange("(g kc p) n -> g p kc n", g=4, p=128)
    w_gate_v = w_gate.rearrange("(kc p) n -> p kc n", p=128)
    w_up_v = w_up.rearrange("(kc p) n -> p kc n", p=128)
    w_down_v = w_down.rearrange("(kc p) n -> p kc n", p=128)
    outT_v = outT.rearrange("(c p) t -> p c t", p=128)

    with ExitStack() as st:
        TOTAL = 212800
        arena_t = st.enter_context(nc.sbuf_tensor("arena", [128, TOTAL // 2], BF16))
        pbank = [st.enter_context(nc.psum_tensor("pb%d" % i, [128, 512], F32)) for i in range(8)]
        A = Arena(arena_t, TOTAL)
        S = Sched(nc)

        def I(eng, meth, r=(), w=(), dma=False, a=(), **kw):
            S.op(eng, (meth, tuple(a), kw), r=r, w=w, dma=dma)

        cst = A.alloc([NCONST], F32)
        prm = A.alloc([NPAR], F32)
        ones_bf = A.alloc([128], BF16)
        ident_bf = A.alloc([128], BF16)
        LT_bf = A.alloc([128], BF16)
        a_b = A.alloc([32], F32)
        carry = A.alloc([24, 3], F32)
        M_S = A.mark()
        Sst = A.alloc([4, 512], F32)
        hn = A.alloc([16, 16 + HT], BF16)
        M_R = A.mark()
        R = [A.alloc([16, HT], F32), A.alloc([16, HT], F32)]
        PH = A.mark()

        onesf = cst[:, K_ONES:K_ONES + 128]
        LTf = cst[:, K_LT:K_LT + 128]
        T0 = cst[:, K_T0:K_T0 + 256]
        T1h = cst[:, K_T1 + 128:K_T1 + 256]
        caus = cst[:, K_CAUS:K_CAUS + 384]
        identf = cst[:, K_ID:K_ID + 128]

        def pcol(off, i=0, n=1):
            return prm[:, off + i: off + i + n]

        def dma_w(dst, src, tok, eng="pool"):
            I(eng, "dma_start", w=[tok], dma=True, out=dst, in_=src)

        mm_banks = [[0, 1, 2, 7]]
        mm_rr = [0]

        def next_mm():
            b = mm_banks[0]
            i = b[mm_rr[0] % len(b)]
            mm_rr[0] += 1
            return i

        pCB = pbank[3][:, 0:384]
        pdt = pbank[3][:, 384:512].rearrange("p (a b) -> p a b", b=32)
        pss = pbank[4][:, :]
        ptr = pbank[5][:, 0:256].bitcast(BF16).rearrange("p (a b) -> p a b", b=128)
        pY = pbank[5][:, 256:512]
        pYo = pbank[6][:, 0:256]
        pra = [pbank[6][:, 256 + q * 96: 256 + (q + 1) * 96].rearrange("p (a b) -> p a b", b=32) for q in range(2)]
        ph = pbank[6][:, 448:464]
        pcc = pbank[6][:, 464:468]
        pAbs = [(pbank[7][:, 0:256], "pb7"), (pbank[2][:, 0:256], "pb2")]
        pYs = [(pbank[5][:, 256:512], "pb5"), (pbank[3][:, 0:256], "pb3")]
        pYos = [(pbank[6][:, 0:256], "pb6"), (pbank[4][:, 256:512], "pb4")]

        I("sp", "dma_start", w=["cst"], dma=True, out=cst, in_=consts_d[:, :])
        I("sp", "dma_start", w=["prm"], dma=True, out=prm, in_=params_d[:, :])
        I("dve", "tensor_copy", r=["cst"], w=["ones_bf"], a=(ones_bf, onesf))
        I("dve", "tensor_copy", r=["cst"], w=["ident_bf"], a=(ident_bf, identf))
        I("dve", "tensor_copy", r=["cst"], w=["LT_bf"], a=(LT_bf, LTf))
        I("act", "activation", r=["prm"], w=["a_b"], out=a_b, in_=prm[:, P_ALOG:P_ALOG + 32], func=AF.Exp)
        I("dve", "tensor_scalar", r=["a_b"], w=["a_b"], out=a_b, in0=a_b, scalar1=-1.0, scalar2=None, op0=ALU.mult)
        I("pool", "memset", w=["carry%d" % m_ for m_ in range(24)], a=(carry, 0.0))
        I("pool", "memset", w=["S0", "S1", "S2", "S3"], a=(Sst, 0.0))
        I("pool", "memset", w=["hn"], a=(hn, 0.0))

        def norm_half(src, nw_off, dst_fn, tmp, dst_tok="hn", src_tok="R"):
            for c in range(16):
                sqb = tmp["sq"][c % 2]
                I("act", "activation", r=[src_tok], w=["sq%d" % (c % 2)], out=sqb, in_=src[:, c, :], func=AF.Square)
                I("pe", "matmul", r=["ones_bf", "sq%d" % (c % 2)], w=["pb4"], a=(pss, ones_bf, sqb),
                  start=(c == 0), stop=(c == 15))
            I("act", "activation", r=["pb4"], w=["sd"], out=tmp["sd"], in_=pss, func=AF.Sqrt, bias=EPS, scale=1.0 / D)
            I("dve", "reciprocal", r=["sd"], w=["rstd"], a=(tmp["rstd"], tmp["sd"]))
            for c in range(16):
                I("dve", "scalar_tensor_tensor", r=[src_tok, "prm", "rstd"], w=[dst_tok], out=dst_fn(c),
                  in0=src[:, c, :], scalar=pcol(nw_off, c), in1=tmp["rstd"], op0=ALU.mult, op1=ALU.mult)

        def mixer_phase_alloc(own):
            t = {}
            t["sq"] = [A.alloc([HT], BF16), A.alloc([HT], BF16)]
            t["sd"] = A.alloc([HT], F32)
            t["rstd"] = A.alloc([HT], F32)
            t["pre"] = [A.alloc([3 + HT + 1], F32) for _ in range(2 if own else 3)]
            t["acc"] = [A.alloc([HT], F32) for _ in range(2 if own else 3)]
            t["xc"] = [A.alloc([HT], BF16) for _ in range(0 if own else 3)]
            t["xtok"] = [A.alloc([4, 640], BF16) for _ in range(1 if own else 2)]
            t["xw"] = [A.alloc([2, 512], BF16) for _ in range(1 if own else 2)]
            for k in ("dtr", "dtabs", "dtl", "dt", "dtA", "lndt"):
                t[k] = A.alloc([4, 32], F32)
            t["RAraw"] = [A.alloc([3, 32], F32) for _ in range(2)]
            t["eR"] = [A.alloc([3, 32], F32) for _ in range(2)]
            t["wgt"] = [A.alloc([2, 32], F32) for _ in range(2)]
            t["negA"] = [A.alloc([2, 32], F32) for _ in range(2)]
            t["stmp"] = A.alloc([512], F32)
            t["d3"] = [A.alloc([4, 32], BF16) for _ in range(3)]
            t["rr"] = [A.alloc([4, 32], F32) for _ in range(2)]
            return t

        def dt_block(t, own, hb, WDT, hsrc=None, htok="hn"):
            hsrc = (lambda kc, tt: hn[:, kc, 16 + tt * 128: 16 + (tt + 1) * 128]) if hsrc is None else hsrc
            for tt in range(4):
                for kc in range(16):
                    I("pe", "matmul", r=[htok, "wdt"], w=["pb3"],
                      a=(pdt[:, tt, :], hsrc(kc, tt), WDT[:, kc, :]),
                      start=(kc == 0), stop=(kc == 15))
            bias_b = prm[:, P_DTB:P_DTB + 32].unsqueeze(1).to_broadcast([128, 4, 32])
            I("dve", "tensor_tensor", r=["pb3", "prm"], w=["dtr"], out=t["dtr"], in0=pdt, in1=bias_b, op=ALU.add)
            I("act", "activation", r=["dtr"], w=["dtabs"], out=t["dtabs"], in_=t["dtr"], func=AF.Abs)
            I("act", "activation", r=["dtabs"], w=["dtabs"], out=t["dtabs"], in_=t["dtabs"], func=AF.Exp, scale=-1.0)
            I("act", "activation", r=["dtabs"], w=["dtl"], out=t["dtl"], in_=t["dtabs"], func=AF.Ln, bias=1.0, scale=1.0)
            I("dve", "scalar_tensor_tensor", r=["dtr", "dtl"], w=["dt"], out=t["dt"], in0=t["dtr"], scalar=0.0,
              in1=t["dtl"], op0=ALU.max, op1=ALU.add)
            if not own:
                I("dve", "tensor_scalar", r=["dt", "prm"], w=["dt"], out=t["dt"], in0=t["dt"],
                  scalar1=pcol(P_VALID, hb), scalar2=None, op0=ALU.mult)
            a_bb = a_b.unsqueeze(1).to_broadcast([128, 4, 32])
            I("dve", "tensor_tensor", r=["dt", "a_b"], w=["dtA"], out=t["dtA"], in0=t["dt"], in1=a_bb, op=ALU.mult)
            if own:
                I("act", "activation", r=["dt"], w=["lndt"], out=t["lndt"], in_=t["dt"], func=AF.Ln)
            d3, rr = t["d3"], t["rr"]
            I("dve", "tensor_copy", r=["dtA"], w=["d3_0"], a=(d3[0], t["dtA"]))
            I("dve", "tensor_tensor", r=["dtA", "d3_0"], w=["rr0"], out=rr[0], in0=t["dtA"], in1=d3[0], op=ALU.subtract)
            I("dve", "tensor_copy", r=["rr0"], w=["d3_1"], a=(d3[1], rr[0]))
            I("dve", "tensor_tensor", r=["rr0", "d3_1"], w=["rr1"], out=rr[1], in0=rr[0], in1=d3[1], op=ALU.subtract)
            I("dve", "tensor_copy", r=["rr1"], w=["d3_2"], a=(d3[2], rr[1]))
            rd = ["LT_bf", "ones_bf", "d3_0", "d3_1", "d3_2"]
            for q in range(2):
                ta, tb = 2 * q, 2 * q + 1
                p_ = pra[q]
                tk = "pb6"
                for k in range(3):
                    I("pe", "matmul", r=rd, w=[tk], a=(p_[:, 0, :], LT_bf, d3[k][:, ta, :]), start=(k == 0), stop=False)
                    I("pe", "matmul", r=rd, w=[tk], a=(p_[:, 0, :], ones_bf, d3[k][:, tb, :]), start=False, stop=(k == 2))
                for k in range(3):
                    I("pe", "matmul", r=rd, w=[tk], a=(p_[:, 1, :], LT_bf, d3[k][:, tb, :]), start=(k == 0), stop=(k == 2))
                for k in range(3):
                    I("pe", "matmul", r=rd, w=[tk], a=(p_[:, 2, :], ones_bf, d3[k][:, ta, :]), start=(k == 0), stop=False)
                    I("pe", "matmul", r=rd, w=[tk], a=(p_[:, 2, :], ones_bf, d3[k][:, tb, :]), start=False, stop=(k == 2))
                I("act", "activation", r=[tk], w=["eR%d" % q], out=t["eR"][q], in_=p_, func=AF.Exp)
                I("dve", "tensor_tensor", r=["eR%d" % q, "dt"], w=["wgt%d" % q], out=t["wgt"][q],
                  in0=t["eR"][q][:, 0:2, :], in1=t["dt"][:, ta:ta + 2, :], op=ALU.mult)
                if own:
                    I("act", "copy", r=[tk], w=["RAraw%d" % q], a=(t["RAraw"][q], p_))
                    aend_b = t["RAraw"][q][:, 2:3, :].to_broadcast([128, 2, 32])
                    I("dve", "tensor_tensor", r=["RAraw%d" % q], w=["negA%d" % q], out=t["negA"][q],
                      in0=aend_b, in1=t["RAraw"][q][:, 0:2, :], op=ALU.subtract)

        def proj_chunk(wsrc, wtok, col0, rhs_fn, n, mmi, out=None, otok=None, htok="hn"):
            pm = pbank[mmi][:, 0:n] if out is None else out
            otok = otok or ("pb%d" % mmi)
            for kc in range(16):
                I("pe", "matmul", r=[wtok, htok], w=[otok], a=(pm, wsrc[:, kc, col0:col0 + 128], rhs_fn(kc)),
                  start=(kc == 0), stop=(kc == 15))
            return pm

        def conv_s1(t, pm, mmi, m, i):
            pre = t["pre"][i % len(t["pre"])]
            acc = t["acc"][i % len(t["acc"])]
            ptk, atk = "pre%d" % (i % len(t["pre"])), "acc%d" % (i % len(t["acc"]))
            I("act", "copy", r=["pb%d" % mmi], w=[ptk], a=(pre[:, 3:3 + HT], pm))
            I("act", "copy", r=["carry%d" % m], w=[ptk], a=(pre[:, 0:3], carry[:, m, :]))
            I("act", "activation", r=[ptk, "prm"], w=[atk], out=acc, in_=pre[:, 0:HT], func=AF.Identity,
              bias=pcol(P_CB, m), scale=pcol(P_CW, m * 4 + 0))

        def conv_s2(t, m, i):
            pre = t["pre"][i % len(t["pre"])]
            acc = t["acc"][i % len(t["acc"])]
            ptk, atk = "pre%d" % (i % len(t["pre"])), "acc%d" % (i % len(t["acc"]))
            for k in (1, 2, 3):
                I("dve", "scalar_tensor_tensor", r=[ptk, "prm", atk], w=[atk], out=acc, in0=pre[:, k:k + HT],
                  scalar=pcol(P_CW, m * 4 + k), in1=acc, op0=ALU.mult, op1=ALU.add)

        def conv_s3(t, m, dst, dst_tok, i):
            pre = t["pre"][i % len(t["pre"])]
            acc = t["acc"][i % len(t["acc"])]
            ptk, atk = "pre%d" % (i % len(t["pre"])), "acc%d" % (i % len(t["acc"]))
            I("act", "copy", r=[ptk], w=["carry%d" % m], a=(carry[:, m, :], pre[:, HT:HT + 3]))
            I("act", "activation", r=[atk], w=[dst_tok], out=dst, in_=acc, func=AF.Silu)

        def conv_silu(t, pm, mmi, m, dst, dst_tok, i):
            conv_s1(t, pm, mmi, m, i)
            conv_s2(t, m, i)
            conv_s3(t, m, dst, dst_tok, i)

        def transpose_pe(src, src_tok):
            for tt in range(4):
                I("pe", "transpose", r=[src_tok, "ident_bf"], w=["pb5"],
                  a=(ptr[:, tt, :], src[:, tt * 128:(tt + 1) * 128], ident_bf))

        def transpose_evac(xtok, xtok_tok, col0):
            I("act", "copy", r=["pb5"], w=[xtok_tok], a=(xtok[:, :, col0:col0 + 128], ptr))

        def transpose_to_tok(src, src_tok, xtok, xtok_tok, col0):
            for tt in range(4):
                I("pe", "transpose", r=[src_tok, "ident_bf"], w=["pb5"],
                  a=(ptr[:, tt, :], src[:, tt * 128:(tt + 1) * 128], ident_bf))
            I("act", "copy", r=["pb5"], w=[xtok_tok], a=(xtok[:, :, col0:col0 + 128], ptr))

        def state_update_a(t, g, q, xtok, xtok_tok, gi):
            ta, tb = 2 * q, 2 * q + 1
            xw = t["xw"][gi % len(t["xw"])]
            xwt = "xw%d" % (gi % len(t["xw"]))
            for j, tt in enumerate((ta, tb)):
                wb = t["wgt"][q][:, j, 8 * g:8 * g + 8].unsqueeze(2).to_broadcast([128, 8, 64])
                I("dve", "tensor_tensor", r=[xtok_tok, "wgt%d" % q], w=[xwt],
                  out=xw[:, j, :].rearrange("p (a b) -> p a b", b=64),
                  in0=xtok[:, tt, 0:512].rearrange("p (a b) -> p a b", b=64), in1=wb, op=ALU.mult)

        def state_update_b(t, g, q, xtok, xtok_tok, gi, pS=None, pStok="pb4"):
            pS = pss if pS is None else pS
            ta, tb = 2 * q, 2 * q + 1
            xw = t["xw"][gi % len(t["xw"])]
            xwt = "xw%d" % (gi % len(t["xw"]))
            for j, tt in enumerate((ta, tb)):
                I("pe", "matmul", r=[xtok_tok, xwt], w=[pStok], a=(pS, xtok[:, tt, 512:640], xw[:, j, :]),
                  start=(j == 0), stop=(j == 1))
            decb = t["eR"][q][:, 2, 8 * g:8 * g + 8].unsqueeze(2).to_broadcast([128, 8, 64])
            Sg = Sst[:, g, :]
            I("dve", "tensor_tensor", r=["S%d" % g, "eR%d" % q], w=["stmp"],
              out=t["stmp"].rearrange("p (a b) -> p a b", b=64),
              in0=Sg.rearrange("p (a b) -> p a b", b=64), in1=decb, op=ALU.mult)
            I("dve", "tensor_tensor", r=["stmp", pStok], w=["S%d" % g], out=Sg, in0=t["stmp"], in1=pS, op=ALU.add)

        def state_update(t, g, q, xtok, xtok_tok, gi, pS=None, pStok="pb4"):
            state_update_a(t, g, q, xtok, xtok_tok, gi)
            state_update_b(t, g, q, xtok, xtok_tok, gi, pS=pS, pStok=pStok)

        def run_pipeline(items, stages, hooks=None):
            nst = len(stages)
            for n in range(len(items) + nst - 1):
                for s in reversed(range(nst)):
                    k = n - s
                    if 0 <= k < len(items):
                        stages[s](items[k])
                if hooks and n in hooks:
                    fs = hooks[n]
                    for f in (fs if isinstance(fs, list) else [fs]):
                        f()

        A.off = PH - 16 * HT * 4
        WRES = A.alloc([16, 2560], BF16)
        WDT_P = A.alloc([16, 32], BF16)
        hnB = A.alloc([16, HT], BF16)
        tP = mixer_phase_alloc(False)
        dma_w(WDT_P, w_in_v[:, :, C_DT:C_DT + 32], "wdt")
        for j in (0, 4, 1, 2, 3):
            dma_w(WRES[:, :, j * 512:(j + 1) * 512], w_in_v[:, :, C_X + j * 512: C_X + (j + 1) * 512], "wres%d" % j)

        ci = [0]
        gi = [0]
        rhs_main = lambda kc: hn[:, kc, 16:16 + HT]

        def norm_stats(src, tmp, src_tok):
            for c in range(16):
                sqb = tmp["sq"][c % 2]
                I("act", "activation", r=[src_tok], w=["sq%d" % (c % 2)], out=sqb, in_=src[:, c, :], func=AF.Square)
                I("pe", "matmul", r=["ones_bf", "sq%d" % (c % 2)], w=["pb4"], a=(pss, ones_bf, sqb),
                  start=(c == 0), stop=(c == 15))
            I("act", "activation", r=["pb4"], w=["sd"], out=tmp["sd"], in_=pss, func=AF.Sqrt, bias=EPS, scale=1.0 / D)
            I("dve", "reciprocal", r=["sd"], w=["rstd"], a=(tmp["rstd"], tmp["sd"]))

        def norm_apply(src, nw_off, dst_fn, tmp, dst_tok, src_tok):
            for c in range(16):
                I("dve", "scalar_tensor_tensor", r=[src_tok, "prm", "rstd"], w=[dst_tok], out=dst_fn(c),
                  in0=src[:, c, :], scalar=pcol(nw_off, c), in1=tmp["rstd"], op0=ALU.mult, op1=ALU.mult)

        pS_P = pbank[3][:, :]
        hbs = list(range(NPRE - npre, NPRE))
        NH = len(hbs)

        def hbuf(h):
            if (NH - 1 - h) % 2 == 0:
                return (lambda kc: hn[:, kc, 16:16 + HT]), "hn", (lambda c: hn[:, c, 16:16 + HT]), \
                       (lambda kc, tt: hn[:, kc, 16 + tt * 128: 16 + (tt + 1) * 128])
            return (lambda kc: hnB[:, kc, :]), "hnB", (lambda c: hnB[:, c, :]), \
                   (lambda kc, tt: hnB[:, kc, tt * 128:(tt + 1) * 128])

        info = []
        for h, hb in enumerate(hbs):
            rhs_fn, htok, _, _ = hbuf(h)
            for g in range(4):
                for i in range(5):
                    if i == 0:
                        cur_x = (tP["xtok"][gi[0] % 2], "xtok%d" % (gi[0] % 2), gi[0])
                        gi[0] += 1
                    if i < 4:
                        col0, m, dcol = g * 512 + i * 128, g * 4 + i, i * 128
                    else:
                        col0, m, dcol = 2048 + g * 128, 16 + g, 512
                    info.append(dict(h=h, g=g, i=i, col0=col0, m=m, dcol=dcol, xt=cur_x, ci=ci[0], xc=tP["xc"][ci[0] % 3],
                                     xct="xc%d" % (ci[0] % 3), rhs=rhs_fn, htok=htok))
                    ci[0] += 1

        def st0(it):
            it["mmi"] = next_mm()
            it["pm"] = proj_chunk(WRES, "wres%d" % (it["col0"] // 512), it["col0"], it["rhs"], HT, it["mmi"], htok=it["htok"])

        def st1(it):
            conv_s1(tP, it["pm"], it["mmi"], it["m"], it["ci"])

        def st2(it):
            conv_s2(tP, it["m"], it["ci"])

        def st3(it):
            conv_s3(tP, it["m"], it["xc"], it["xct"], it["ci"])

        def st4(it):
            transpose_pe(it["xc"], it["xct"])

        def st5(it):
            transpose_evac(it["xt"][0], it["xt"][1], it["dcol"])

        def st6(it):
            if it["i"] == 4 and lvl >= 4:
                xt, xk, gx = it["xt"]
                for q in range(2):
                    state_update_a(tP, it["g"], q, xt, xk, gx * 2 + q)

        def st7(it):
            pass

        def st8(it):
            if it["i"] == 4 and lvl >= 4:
                xt, xk, gx = it["xt"]
                for q in range(2):
                    state_update_b(tP, it["g"], q, xt, xk, gx * 2 + q, pS=pS_P, pStok="pb3")

        def load_x(h):
            I("sp", "dma_start", w=["R"], dma=True, out=R[0], in_=xall_v[hbs[h]])

        def do_stats(h):
            norm_stats(R[0], tP, "R")

        def do_apply(h):
            _, htok, dstf, _ = hbuf(h)
            norm_apply(R[0], P_NW1, dstf, tP, htok, "R")
            if h + 1 < NH:
                load_x(h + 1)

        def do_dt(h):
            _, htok, _, dsrc = hbuf(h)
            dt_block(tP, False, hbs[h], WDT_P, hsrc=dsrc, htok=htok)

        hooks = {}
        if NH:
            load_x(0)
            do_stats(0)
            do_apply(0)
            do_dt(0)
            for h in range(NH - 1):
                bstep = 20 * h
                hooks.setdefault(bstep + 3, []).append(lambda h=h: do_stats(h + 1))
                hooks.setdefault(bstep + 8, []).append(lambda h=h: do_apply(h + 1))
                hooks.setdefault(bstep + 28, []).append(lambda h=h: do_dt(h + 1))
            run_pipeline(info, [st0, st1, st2, st3, st4, st5, st6, st7, st8], hooks=hooks)
        peakP = A.peak
        if dbg:
            I("sp", "dma_start", r=["S0", "S1", "S2", "S3"], dma=True, out=dbg_S[:, :], in_=Sst.rearrange("p a b -> p (a b)"))
            for c_ in range(16):
                I("act", "copy", r=["hn"], w=["R"], a=(R[0][:, c_, :], hn[:, c_, 16:16 + HT]))
            I("sp", "dma_start", r=["R"], dma=True, out=dbg_H[:, :], in_=R[0].rearrange("p a b -> p (a b)"))

        S.barrier()
        mm_banks[0] = [0, 1]
        A.reset(PH)
        WDT_M = A.alloc([16, 32], BF16)
        tM = mixer_phase_alloc(True)
        WS = [A.alloc([16, 512], BF16) for _ in range(2)]
        bt = A.alloc([HT], BF16)
        ctt = A.alloc([HT], BF16)
        xTg = A.alloc([4, HT], BF16)
        szg = A.alloc([4, HT], BF16)
        T0_bf = A.alloc([256], BF16)
        T1h_bf = A.alloc([128], BF16)
        I("dve", "tensor_copy", r=["cst"], w=["T0_bf"], a=(T0_bf, T0))
        I("dve", "tensor_copy", r=["cst"], w=["T0_bf"], a=(T1h_bf, T1h))
        arg0 = [A.alloc([256], F32) for _ in range(2)]
        arg1 = [A.alloc([128], F32) for _ in range(2)]
        E0 = [tM["sd"][:, 0:256], tM["sd"][:, 256:512]]
        E1 = [tM["rstd"][:, 0:128], tM["rstd"][:, 128:256]]
        M1 = [tM["rstd"][:, 256:384].bitcast(BF16)[:, 0:128], tM["rstd"][:, 256:384].bitcast(BF16)[:, 128:256]]
        M0 = [tM["sq"][0][:, 0:256], tM["sq"][0][:, 256:512]]
        CBm0_ = A.alloc([384], F32)
        CBms = [(CBm0_, ["CBm"]), (tM["acc"][0][:, 0:384], ["acc0"])]
        eA = [A.alloc([256], F32) for _ in range(2)]
        Sbf0_ = A.alloc([512], BF16)
        Sbfs = [(Sbf0_, ["Sbf"]), (tM["acc"][1].bitcast(BF16)[:, 0:512], ["acc1"])]
        gbuf0_ = A.alloc([4, 256], F32)
        gb1a = tM["pre"][0][:, 0:512].rearrange("p (a b) -> p a b", b=256)
        gb1b = tM["pre"][1][:, 0:512].rearrange("p (a b) -> p a b", b=256)
        gbufs = [([gbuf0_[:, i, :] for i in range(4)], ["gbuf"]),
                 ([gb1a[:, 0, :], gb1a[:, 1, :], gb1b[:, 0, :], gb1b[:, 1, :]], ["pre0", "pre1"])]
        ytmp = [A.alloc([256], F32) for _ in range(2)]
        gsq = [A.alloc([256], BF16) for _ in range(2)]
        rstd_g = A.alloc([256], F32)
        sd_g = A.alloc([256], F32)
        yss = A.alloc([4, HT], BF16)
        ubuf = A.alloc([16 + HT], F32)
        pp = [A.alloc([16 + HT], F32) for _ in range(2)]
        pooled = A.alloc([4, HT], BF16)
        ypool = A.alloc([4, HT], BF16)
        dma_w(WDT_M, w_in_v[:, :, C_DT:C_DT + 32], "wdt")

        ws_rr = [0]

        def load_ws(srcs):
            i = ws_rr[0] % 2
            ws_rr[0] += 1
            ws = WS[i]
            tok = "ws%d" % i
            for dfn, src in srcs:
                dma_w(dfn(ws), src, tok)
            return ws, tok

        T0b = T0.unsqueeze(1).to_broadcast([128, 2, 256])
        T1b = T1h.unsqueeze(1).to_broadcast([128, 2, 128])
        pG = pbank[4][:, 0:256]

        for hh in (range(2) if do_own else ()):
            hb = NPRE + hh
            if hh == 1:
                S.barrier()
            Rh = R[hh]
            rt = "R%d" % hh
            if hh == 0:
                I("sp", "dma_start", w=["R0"], dma=True, out=R[0], in_=xall_v[NPRE])
                I("sp", "dma_start", w=["R1"], dma=True, out=R[1], in_=xall_v[NPRE + 1])
            I("dve", "tensor_copy", r=["hn"], w=["hn"], a=(hn[:, :, 0:16], hn[:, :, HT:HT + 16]))
            norm_half(Rh, P_NW1, lambda c: hn[:, c, 16:16 + HT], tM, src_tok=rt)
            dt_block(tM, True, hb, WDT_M)
            xtok = tM["xtok"][0]
            xtk = "xtok0"
            hcnt = [0]

            def o_st0(it):
                it["mmi"] = next_mm()
                it["pm"] = proj_chunk(it["ws"], it["wt"], it["wcol"], rhs_main, HT, it["mmi"])
                if it["kind"] == "u":
                    proj_chunk(it["ws"], it["wt"], it["wcol"], lambda kc: hn[:, kc, 0:16], 16, 6, out=ph, otok="pb6")

            def o_st1(it):
                k_ = it["kind"]
                if k_ in ("x", "B", "C"):
                    conv_s1(tM, it["pm"], it["mmi"], it["m"], it["ci"])
                    conv_s2(tM, it["m"], it["ci"])
                elif k_ == "z":
                    I("act", "activation", r=["pb%d" % it["mmi"]], w=["szg"], out=szg[:, it["i"], :], in_=it["pm"], func=AF.Silu)
                else:
                    I("act", "copy", r=["pb%d" % it["mmi"]], w=["ubuf"], a=(ubuf[:, 16:16 + HT], it["pm"]))
                    I("act", "copy", r=["pb6"], w=["ubuf"], a=(ubuf[:, 0:16], ph))

            def o_st2(it):
                k_ = it["kind"]
                if k_ in ("x", "B", "C"):
                    conv_s3(tM, it["m"], it["dst"], it["dtok"], it["ci"])
                elif k_ == "u":
                    g_, i_ = it["g"], it["i"]
                    wwin = float(1 << (g_ + 1))
                    src, stok, lo = ubuf, "ubuf", 0
                    for k in range(g_ + 1):
                        sh = 1 << k
                        dst, dtok = pp[k % 2], "pp%d" % (k % 2)
                        lo2 = lo + sh
                        I("dve", "tensor_tensor", r=[stok], w=[dtok], out=dst[:, lo2:16 + HT], in0=src[:, lo2:16 + HT],
                          in1=src[:, lo2 - sh:16 + HT - sh], op=ALU.add)
                        src, stok, lo = dst, dtok, lo2
                    I("dve", "scalar_tensor_tensor", r=[stok, "ubuf"], w=["pooled"], out=pooled[:, i_, 16:HT],
                      in0=src[:, 32:16 + HT], scalar=1.0 / wwin, in1=ubuf[:, 32:16 + HT], op0=ALU.mult, op1=ALU.subtract)
                    oth = pp[(g_ + 1) % 2]
                    otk = "pp%d" % ((g_ + 1) % 2)
                    io = P_ICNT + it["hh"] * 64 + g_ * 16
                    I("dve", "tensor_tensor", r=[stok, "prm"], w=[otk], out=oth[:, 0:16], in0=src[:, 16:32],
                      in1=prm[:, io:io + 16], op=ALU.mult)
                    I("dve", "tensor_tensor", r=[otk, "ubuf"], w=["pooled"], out=pooled[:, i_, 0:16], in0=oth[:, 0:16],
                      in1=ubuf[:, 16:32], op=ALU.subtract)

            def o_st3(it):
                if it["kind"] in ("x", "B"):
                    transpose_pe(it["dst"], it["dtok"])

            def o_st4(it):
                if it["kind"] in ("x", "B"):
                    transpose_evac(xtok, xtk, it["dcol"])
                S.new_block()

            o_stages = [o_st0, o_st1, o_st2, o_st3, o_st4]

            def out_proj_part(kc0, src_t, src_tok):
                for dh in range(2):
                    ws, wt = load_ws([(lambda w_: w_[:, 0:8, :].rearrange("p (a b) c -> p a (b c)", b=2),
                                       w_out_v[:, kc0:kc0 + 4, dh * 1024:(dh + 1) * 1024])])
                    wv = ws[:, 0:8, :].rearrange("p (a b) c -> p a (b c)", b=2)
                    for dd in range(8):
                        d = dh * 8 + dd
                        mmi = next_mm()
                        pm = pbank[mmi][:, :]
                        for i in range(4):
                            I("pe", "matmul", r=[wt, src_tok], w=["pb%d" % mmi],
                              a=(pm, wv[:, i, dd * 128:(dd + 1) * 128], src_t[:, i, :]), start=(i == 0), stop=(i == 3))
                        I("dve", "tensor_tensor", r=[rt, "pb%d" % mmi], w=[rt], out=Rh[:, d, :], in0=Rh[:, d, :], in1=pm, op=ALU.add)
                        S.new_block()

            def ssd_group(g):
                for q in range(2):
                    ta, tb = 2 * q, 2 * q + 1
                    Sb, Sbt = Sbfs[q]
                    I("act", "copy", r=["S%d" % g], w=Sbt, a=(Sb, Sst[:, g, :]))
                    state_update(tM, g, q, xtok, xtk, gi[0])
                    gi[0] += 1
                    I("pe", "matmul", r=["bt", "ct"], w=["pb3"],
                      a=(pCB[:, 0:256], bt[:, ta * 128:(ta + 1) * 128], ctt[:, ta * 128: ta * 128 + 256]), start=True, stop=True)
                    I("pe", "matmul", r=["bt", "ct"], w=["pb3"],
                      a=(pCB[:, 256:384], bt[:, tb * 128:(tb + 1) * 128], ctt[:, tb * 128:(tb + 1) * 128]), start=True, stop=True)
                    cb, cbt = CBms[q]
                    I("dve", "tensor_tensor", r=["pb3", "cst"], w=cbt, out=cb, in0=pCB, in1=caus, op=ALU.mult)
                S.new_block()
                heads = []
                for q in range(2):
                    for h8 in range(8):
                        k = hcnt[0] % 2
                        pk = (hcnt[0] // 2) % 2
                        heads.append(dict(q=q, h=8 * g + h8, hp=h8 // 2, hx=h8 % 2, hc=h8 * 64, k=k, pk=pk))
                        hcnt[0] += 1

                def s0(it):
                    pass

                def s1(it):
                    k, h, q = it["k"], it["h"], it["q"]
                    pA, pAt = pAbs[k]
                    rd = ["T0_bf", "d3_0", "d3_1", "d3_2"]
                    for kk in range(3):
                        la = tM["d3"][kk][:, 2 * q, h:h + 1].to_broadcast([128, 128])
                        I("pe", "matmul", r=rd, w=[pAt], a=(pA, la, T0_bf), start=(kk == 0), stop=False)
                    for kk in range(3):
                        lb = tM["d3"][kk][:, 2 * q + 1, h:h + 1].to_broadcast([128, 128])
                        I("pe", "matmul", r=rd, w=[pAt], a=(pA[:, 128:256], lb, T1h_bf), start=False, stop=(kk == 2))

                def s2(it):
                    k, h, q = it["k"], it["h"], it["q"]
                    pA, pAt = pAbs[k]
                    negA = tM["negA"][q]
                    nq = "negA%d" % q
                    I("act", "activation", r=[pAt, nq], w=["arg0_%d" % k], out=arg0[k], in_=pA, func=AF.Relu,
                      bias=negA[:, 0, h:h + 1], scale=-1.0)
                    I("act", "activation", r=[pAt, nq], w=["arg1_%d" % k], out=arg1[k], in_=pA[:, 128:256], func=AF.Relu,
                      bias=negA[:, 1, h:h + 1], scale=-1.0)

                def s3(it):
                    k, h, hx, pk, q = it["k"], it["h"], it["hx"], it["pk"], it["q"]
                    pA, pAt = pAbs[k]
                    I("act", "activation", r=["arg0_%d" % k, "lndt"], w=["E0_%d" % k], out=E0[k], in_=arg0[k], func=AF.Exp,
                      bias=tM["lndt"][:, 2 * q, h:h + 1], scale=-1.0)
                    I("act", "activation", r=["arg1_%d" % k, "lndt"], w=["E1_%d" % k], out=E1[k], in_=arg1[k], func=AF.Exp,
                      bias=tM["lndt"][:, 2 * q + 1, h:h + 1], scale=-1.0)
                    I("act", "activation", r=[pAt], w=["eA%d" % pk], out=eA[pk][hx * 64:(hx + 1) * 64, :],
                      in_=pA[hx * 64:(hx + 1) * 64, :], func=AF.Exp)

                def s4(it):
                    k, q = it["k"], it["q"]
                    cb, cbt = CBms[q]
                    I("dve", "tensor_tensor", r=["E0_%d" % k] + cbt, w=["M0_%d" % k], out=M0[k], in0=E0[k], in1=cb[:, 0:256], op=ALU.mult)
                    I("dve", "tensor_tensor", r=["E1_%d" % k] + cbt, w=["M1_%d" % k], out=M1[k], in0=E1[k], in1=cb[:, 256:384], op=ALU.mult)

                def s5(it):
                    k, hx, hc, pk, hp, q = it["k"], it["hx"], it["hc"], it["pk"], it["hp"], it["q"]
                    pYv, pYt = pYs[pk]
                    I("pe", "matmul", r=[xtk, "M0_%d" % k], w=[pYt],
                      a=(pYv[hx * 64:(hx + 1) * 64, :], xtok[:, 2 * q, hc:hc + 64], M0[k]), start=True, stop=False)
                    I("pe", "matmul", r=[xtk, "M1_%d" % k], w=[pYt],
                      a=(pYv[hx * 64:(hx + 1) * 64, 128:256], xtok[:, 2 * q + 1, hc:hc + 64], M1[k]), start=False, stop=True)
                    if hx == 1:
                        pYov, pYot = pYos[pk]
                        Sb, Sbt = Sbfs[q]
                        I("pe", "matmul", r=Sbt + ["ct"], w=[pYot],
                          a=(pYov, Sb[:, hp * 128:(hp + 1) * 128], ctt[:, q * 256:(q + 1) * 256]), start=True, stop=True)

                def s6(it):
                    if it["hx"] != 1:
                        return
                    pk, hp, q = it["pk"], it["hp"], it["q"]
                    tks = slice(q * 256, (q + 1) * 256)
                    pYv, pYt = pYs[pk]
                    pYov, pYot = pYos[pk]
                    yt, ytk = ytmp[pk], "ytmp%d" % pk
                    gb, gbt = gbufs[q]
                    I("dve", "tensor_tensor", r=[pYot, "eA%d" % pk], w=[ytk], out=yt, in0=pYov, in1=eA[pk], op=ALU.mult)
                    I("dve", "tensor_tensor", r=[ytk, pYt], w=[ytk], out=yt, in0=yt, in1=pYv, op=ALU.add)
                    I("dve", "scalar_tensor_tensor", r=["xTg", "prm", ytk], w=[ytk], out=yt, in0=xTg[:, hp, tks],
                      scalar=pcol(P_DSK, g * 4 + hp), in1=yt, op0=ALU.mult, op1=ALU.add)
                    I("dve", "tensor_tensor", r=[ytk, "szg"], w=gbt, out=gb[hp], in0=yt, in1=szg[:, hp, tks], op=ALU.mult)

                def s7(it):
                    S.new_block()

                def epilogue(q):
                    tks = slice(q * 256, (q + 1) * 256)
                    gb, gbt = gbufs[q]
                    for hp in range(4):
                        gs = gsq[hp % 2]
                        I("act", "activation", r=gbt, w=["gsq%d" % (hp % 2)], out=gs, in_=gb[hp], func=AF.Square)
                        I("pe", "matmul", r=["ones_bf", "gsq%d" % (hp % 2)], w=["pb4"], a=(pG, ones_bf, gs),
                          start=(hp == 0), stop=(hp == 3))
                    I("act", "activation", r=["pb4"], w=["sd_g"], out=sd_g, in_=pG, func=AF.Sqrt, bias=EPS, scale=1.0 / 512)
                    I("dve", "reciprocal", r=["sd_g"], w=["rstd_g"], a=(rstd_g, sd_g))
                    for hp in range(4):
                        I("dve", "scalar_tensor_tensor", r=gbt + ["prm", "rstd_g"], w=["yss"], out=yss[:, hp, tks],
                          in0=gb[hp], scalar=pcol(P_SSDW, g * 4 + hp), in1=rstd_g, op0=ALU.mult, op1=ALU.mult)
                    S.new_block()

                run_pipeline(heads, [s0, s1, s2, s3, s4, s5, s6, s7], hooks={13: (lambda: epilogue(0))})
                epilogue(1)

            def pool_branch(g):
                ws, wt = load_ws([(lambda w_: w_[:, :, :], w_in_v[:, :, C_U + g * 512: C_U + (g + 1) * 512])])
                ws2, wt2 = load_ws([(lambda w_: w_[:, 0:4, :], pool_w_v[g])])
                its = [dict(kind="u", g=g, i=i, hh=hh, ws=ws, wt=wt, wcol=i * 128) for i in range(4)]
                run_pipeline(its, o_stages)
                for j in range(4):
                    mmi = next_mm()
                    pm = pbank[mmi][:, :]
                    for i in range(4):
                        I("pe", "matmul", r=[wt2, "pooled"], w=["pb%d" % mmi],
                          a=(pm, ws2[:, i, j * 128:(j + 1) * 128], pooled[:, i, :]), start=(i == 0), stop=(i == 3))
                    I("act", "mul", r=["pb%d" % mmi, "prm"], w=["ypool"], a=(ypool[:, j, :], pm, pcol(P_PSC, g * 4 + j)))
                    S.new_block()

            for g in range(4):
                wsx, wtx = load_ws([(lambda w_: w_[:, :, :], w_in_v[:, :, C_X + g * 512: C_X + (g + 1) * 512])])
                wsb, wtb = load_ws([(lambda w_: w_[:, :, 0:128], w_in_v[:, :, C_B + g * 128: C_B + (g + 1) * 128]),
                                    (lambda w_: w_[:, :, 128:256], w_in_v[:, :, C_C + g * 128: C_C + (g + 1) * 128])])
                its = []
                for i in range(4):
                    its.append(dict(kind="x", g=g, i=i, ws=wsx, wt=wtx, wcol=i * 128, m=g * 4 + i, dst=xTg[:, i, :], dtok="xTg",
                                    dcol=i * 128, ci=ci[0]))
                    ci[0] += 1
                its.append(dict(kind="B", g=g, i=0, ws=wsb, wt=wtb, wcol=0, m=16 + g, dst=bt, dtok="bt", dcol=512, ci=ci[0]))
                ci[0] += 1
                if hh == 0:
                    proj_chunk(wsb, wtb, 128, lambda kc: hn[:, kc, 12:16], 4, 6, out=pcc, otok="pb6")
                    I("act", "copy", r=["pb6"], w=["carry%d" % (20 + g)], a=(carry[:, 20 + g, :], pcc[:, 1:4]))
                its.append(dict(kind="C", g=g, i=0, ws=wsb, wt=wtb, wcol=128, m=20 + g, dst=ctt, dtok="ct", dcol=0, ci=ci[0]))
                ci[0] += 1
                run_pipeline(its, o_stages)
                wsz, wtz = load_ws([(lambda w_: w_[:, :, :], w_in_v[:, :, C_Z + g * 512: C_Z + (g + 1) * 512])])
                its = [dict(kind="z", g=g, i=i, ws=wsz, wt=wtz, wcol=i * 128) for i in range(4)]
                run_pipeline(its, o_stages)
                S.begin_capture()
                ssd_group(g)
                blkA = S.end_capture()
                S.begin_capture()
                if g > 0:
                    out_proj_part((g - 1) * 4, yss_prev[0], yss_prev[1])
                pool_branch(g)
                out_proj_part(16 + g * 4, ypool, "ypool")
                blkB = S.end_capture()
                nA0 = len(blkA) // 2
                nB1 = 16 if g > 0 else 0
                rest = blkB[nB1:]
                S.interleave(blkA[:nA0], blkB[:nB1] + rest[:len(rest) // 2])
                S.interleave(blkA[nA0:], rest[len(rest) // 2:])
                yss_prev = (yss, "yss")
                if g == 3:
                    out_proj_part(g * 4, yss, "yss")
        peakM = A.peak
        if dbg:
            for hh in range(2):
                I("sp", "dma_start", r=["R%d" % hh], dma=True, out=dbg_R[:, hh * 16 * HT:(hh + 1) * 16 * HT],
                  in_=R[hh].rearrange("p a b -> p (a b)"))

        S.barrier()
        A.reset(PH)
        mm_banks[0] = [0, 1, 2, 3, 5, 6, 7]
        A.off = M_S
        tF = {"sq": [A.alloc([HT], BF16), A.alloc([HT], BF16)], "sd": A.alloc([HT], F32), "rstd": A.alloc([HT], F32)}
        WD = [A.alloc([8, 512], BF16) for _ in range(2)]
        assert A.off <= M_R
        A.reset(PH)
        hn2 = A.alloc([16, TOK], BF16)
        act = [A.alloc([8, TOK], BF16) for _ in range(2)]
        WGU = [A.alloc([16, 512], BF16) for _ in range(2)]
        sg = [A.alloc([HT], F32) for _ in range(2)]
        ost = [A.alloc([HT], F32) for _ in range(2)]

        for hh in (range(2) if do_ffn else ()):
            norm_half(R[hh], P_NW2, lambda c, hh=hh: hn2[:, c, hh * HT:(hh + 1) * HT], tF, dst_tok="hn2", src_tok="R%d" % hh)

        groups = [(0, 8), (8, 8), (16, 8), (24, 8), (32, 8), (40, 4)] if do_ffn else []
        wgu_rr, wd_rr, sg_rr = [0], [0], [0]
        for G, (fc0, nfc) in enumerate(groups):
            ab = act[G % 2]
            at = "act%d" % (G % 2)
            for pr in range(nfc // 2):
                c0 = (fc0 + pr * 2) * 128
                i = wgu_rr[0] % 2
                wgu_rr[0] += 1
                wgu, wgt_ = WGU[i], "wgu%d" % i
                dma_w(wgu[:, :, 0:256], w_gate_v[:, :, c0:c0 + 256], wgt_)
                dma_w(wgu[:, :, 256:512], w_up_v[:, :, c0:c0 + 256], wgt_)
                for cc in range(2):
                    fl = pr * 2 + cc
                    for hh in range(2):
                        mg = next_mm()
                        pg = pbank[mg][:, :]
                        for kc in range(16):
                            I("pe", "matmul", r=[wgt_, "hn2"], w=["pb%d" % mg],
                              a=(pg, wgu[:, kc, cc * 128:(cc + 1) * 128], hn2[:, kc, hh * HT:(hh + 1) * HT]),
                              start=(kc == 0), stop=(kc == 15))
                        mu = next_mm()
                        pu = pbank[mu][:, :]
                        for kc in range(16):
                            I("pe", "matmul", r=[wgt_, "hn2"], w=["pb%d" % mu],
                              a=(pu, wgu[:, kc, 256 + cc * 128:256 + (cc + 1) * 128], hn2[:, kc, hh * HT:(hh + 1) * HT]),
                              start=(kc == 0), stop=(kc == 15))
                        si = sg_rr[0] % 2
                        sg_rr[0] += 1
                        I("act", "activation", r=["pb%d" % mg], w=["sg%d" % si], out=sg[si], in_=pg, func=AF.Silu)
                        I("dve", "tensor_tensor", r=["sg%d" % si, "pb%d" % mu], w=[at], out=ab[:, fl, hh * HT:(hh + 1) * HT],
                          in0=sg[si], in1=pu, op=ALU.mult)
            for db in range(4):
                i = wd_rr[0] % 2
                wd_rr[0] += 1
                wd, wdt_ = WD[i], "wd%d" % i
                dma_w(wd[:, 0:nfc, :], w_down_v[:, fc0:fc0 + nfc, db * 512:(db + 1) * 512], wdt_)
                for dd in range(4):
                    d = db * 4 + dd
                    for hh in range(2):
                        mmi = next_mm()
                        pm = pbank[mmi][:, :]
                        for f in range(nfc):
                            I("pe", "matmul", r=[wdt_, at], w=["pb%d" % mmi],
                              a=(pm, wd[:, f, dd * 128:(dd + 1) * 128], ab[:, f, hh * HT:(hh + 1) * HT]),
                              start=(f == 0), stop=(f == nfc - 1))
                        I("dve", "tensor_tensor", r=["R%d" % hh, "pb%d" % mmi], w=["R%d" % hh], out=R[hh][:, d, :],
                          in0=R[hh][:, d, :], in1=pm, op=ALU.add)
        oi = 0
        for hh in (range(2) if do_ffn else ()):
            src = R[hh]
            rt = "R%d" % hh
            for c in range(16):
                sqb = tF["sq"][c % 2]
                I("act", "activation", r=[rt], w=["sq%d" % (c % 2)], out=sqb, in_=src[:, c, :], func=AF.Square)
                I("pe", "matmul", r=["ones_bf", "sq%d" % (c % 2)], w=["pb4"], a=(pss, ones_bf, sqb),
                  start=(c == 0), stop=(c == 15))
            I("act", "activation", r=["pb4"], w=["sd"], out=tF["sd"], in_=pss, func=AF.Sqrt, bias=EPS, scale=1.0 / D)
            I("dve", "reciprocal", r=["sd"], w=["rstd"], a=(tF["rstd"], tF["sd"]))
            for c in range(16):
                o = ost[oi % 2]
                ot = "ost%d" % (oi % 2)
                oi += 1
                I("dve", "scalar_tensor_tensor", r=[rt, "prm", "rstd"], w=[ot], out=o, in0=src[:, c, :],
                  scalar=pcol(P_NW3, c), in1=tF["rstd"], op0=ALU.mult, op1=ALU.mult)
                I("sp", "dma_start", r=[ot], dma=True, out=outT_v[:, c, hh * HT:(hh + 1) * HT], in_=o)
        peakF = A.peak
        print("SBUF peaks P/M/F:", peakP, peakM, peakF, " ops:", len(S.ops))
        with ExitStack() as st2:
            S.emit(st2)
        print("waits:", S.nwaits)
    return nc


def _pc(v):
    return np.ascontiguousarray(np.asarray(v, np.float32).reshape(16, 128).T)


def _consts():
    c = np.zeros((128, NCONST), np.float32)
    j = np.arange(128)[:, None]
    l = np.arange(128)[None, :]
    tri = (j <= l).astype(np.float32)
    c[:, K_ONES:K_ONES + 128] = 1.0
    c[:, K_LT:K_LT + 128] = (j > l).astype(np.float32)
    c[:, K_T0:K_T0 + 128] = tri
    c[:, K_T0 + 128:K_T0 + 256] = 1.0
    c[:, K_T1 + 128:K_T1 + 256] = tri
    c[:, K_CAUS:K_CAUS + 128] = tri
    c[:, K_CAUS + 128:K_CAUS + 256] = 1.0
    c[:, K_CAUS + 256:K_CAUS + 384] = tri
    c[:, K_ID:K_ID + 128] = np.eye(128, dtype=np.float32)
    return c


def make_in_maps(x, attn_norm_w, w_in, conv_w, conv_b, dt_bias, a_log, d_skip, ssd_norm_w, pool_w,
                 pool_scale, w_out, ffn_norm_w, w_gate, w_up, w_down, final_norm_w):
    x = np.asarray(x, np.float32)
    xs = x.reshape(SEQ, D)
    consts = _consts()
    base = np.zeros((128, NPAR), np.float32)
    base[:, P_NW1:P_NW1 + 16] = _pc(np.asarray(attn_norm_w)[0])
    base[:, P_NW2:P_NW2 + 16] = _pc(np.asarray(ffn_norm_w)[0])
    base[:, P_NW3:P_NW3 + 16] = _pc(np.asarray(final_norm_w))
    base[:, P_SSDW:P_SSDW + 16] = _pc(np.asarray(ssd_norm_w)[0])
    base[:, P_PSC:P_PSC + 16] = _pc(np.asarray(pool_scale)[0])
    base[:, P_DSK:P_DSK + 16] = _pc(np.repeat(np.asarray(d_skip, np.float32)[0], 64))
    cw = np.asarray(conv_w, np.float32)[0]
    base[:, P_CW:P_CW + 96] = cw.reshape(4, 24, 128).transpose(2, 1, 0).reshape(128, 96)
    base[:, P_CB:P_CB + 24] = np.asarray(conv_b, np.float32)[0].reshape(24, 128).T
    base[:, P_DTB:P_DTB + 32] = np.asarray(dt_bias, np.float32)[0][None, :]
    base[:, P_ALOG:P_ALOG + 32] = np.asarray(a_log, np.float32)[0][None, :]

    w_in2 = np.ascontiguousarray(np.asarray(w_in, np.float32)[0])
    pool_w2 = np.ascontiguousarray(np.asarray(pool_w, np.float32)[0].reshape(4 * 512, 512))
    w_out2 = np.ascontiguousarray(np.asarray(w_out, np.float32)[0])
    w_gate2 = np.ascontiguousarray(np.asarray(w_gate, np.float32)[0])
    w_up2 = np.ascontiguousarray(np.asarray(w_up, np.float32)[0])
    w_down2 = np.ascontiguousarray(np.asarray(w_down, np.float32)[0])

    in_maps = []
    for c in range(NCORES):
        xall = np.zeros((NHB, D, HT), np.float32)
        prm = base.copy()
        for j in range(NHB):
            gh = 2 * c - NPRE + j
            if gh >= 0:
                xall[j] = xs[gh * HT:(gh + 1) * HT].T
                prm[:, P_VALID + j] = 1.0
        for hh in range(2):
            tg = c * TOK + hh * HT + np.arange(16)
            for g, wwin in enumerate((2, 4, 8, 16)):
                o = P_ICNT + hh * 64 + g * 16
                prm[:, o:o + 16] = (1.0 / np.minimum(tg + 1, wwin))[None, :]
        in_maps.append({
            "xall": xall.reshape(NHB * D, HT), "w_in": w_in2, "pool_w": pool_w2, "w_out": w_out2,
            "w_gate": w_gate2, "w_up": w_up2, "w_down": w_down2, "consts": consts, "params": prm,
        })
    return in_maps


_NC_CACHE = {}


def kernel(**inputs):
    in_maps = make_in_maps(**inputs)
    if "nc" not in _NC_CACHE:
        _NC_CACHE["nc"] = build_nc()
    nc = _NC_CACHE["nc"]
    res = run_bass_kernel_spmd(nc, in_maps, core_ids=list(range(NCORES)))
    out = np.empty((SEQ, D), np.float32)
    for c in range(NCORES):
        out[c * TOK:(c + 1) * TOK] = res.results[c]["outT"].T
    return out.reshape(1, SEQ, D)
```

```python
from contextlib import ExitStack
import numpy as np
import concourse.bass as bass
import concourse.mybir as mybir
from concourse.bass_utils import run_bass_kernel_spmd

F32 = mybir.dt.float32
BF16 = mybir.dt.bfloat16
AF = mybir.ActivationFunctionType
ALU = mybir.AluOpType

NCORES = 8
D = 2048
SEQ = 8192
TOK = SEQ // NCORES
HT = 512
NPRE = 14
NHB = NPRE + 2
DFF = 5632
NFF = DFF // 128
EPS = 1e-5
C_Z, C_X, C_B, C_C, C_DT, C_U = 0, 2048, 4096, 4608, 5120, 5152

ENGS = ("pe", "act", "dve", "pool", "sp")

K_ONES, K_LT, K_T0, K_T1, K_CAUS, K_ID = 0, 128, 256, 512, 768, 1152
NCONST = 1280
P_NW1, P_NW2, P_NW3, P_SSDW, P_PSC, P_DSK = 0, 16, 32, 48, 64, 80
P_CW, P_CB, P_DTB, P_ALOG, P_VALID, P_ICNT = 96, 192, 216, 248, 280, 296
NPAR = 296 + 128


class Sched:
    def __init__(self, nc):
        self.nc = nc
        self.ops = []
        self.tok_w = {}
        self.tok_r = {}
        self.last = {e: None for e in ENGS}
        self.pending_barrier = {e: set() for e in ENGS}

    def op(self, eng, fn, r=(), w=(), dma=False):
        if getattr(self, "cap", None) is not None:
            self.cap[-1].append((eng, fn, tuple(r), tuple(w), dma))
            return None
        oid = len(self.ops)
        deps = {}
        for t in r:
            if t in self.tok_w:
                deps[self.tok_w[t]] = True
        for t in w:
            if t in self.tok_w:
                deps.setdefault(self.tok_w[t], False)
            for x in self.tok_r.get(t, ()):
                deps.setdefault(x, False)
        for x in self.pending_barrier[eng]:
            deps[x] = True
        self.pending_barrier[eng] = set()
        deps.pop(oid, None)
        best = {}
        out = {}
        for d, raw in deps.items():
            p = self.ops[d]
            if p["dma"]:
                out[d] = raw
                continue
            if p["eng"] == eng and eng == "pe" and not dma and not raw:
                continue
            pe_ = p["eng"]
            if pe_ not in best or d > best[pe_]:
                best[pe_] = d
        for pe_, d in best.items():
            out[d] = True
        self.ops.append(dict(eng=eng, fn=fn, deps=out, dma=dma, id=oid))
        for t in r:
            lst = self.tok_r.setdefault(t, [])
            if not dma:
                lst[:] = [x for x in lst if self.ops[x]["dma"] or self.ops[x]["eng"] != eng]
            lst.append(oid)
        for t in w:
            self.tok_w[t] = oid
            self.tok_r[t] = []
        self.last[eng] = oid
        return oid

    def begin_capture(self):
        self.cap = [[]]

    def new_block(self):
        if getattr(self, "cap", None) is not None and self.cap[-1]:
            self.cap.append([])

    def end_capture(self):
        blocks = [b for b in self.cap if b]
        self.cap = None
        return blocks

    def replay(self, blocks):
        for b in blocks:
            for (eng, fn, r, w, dma) in b:
                self.op(eng, fn, r=r, w=w, dma=dma)

    def interleave(self, A, B):
        na = sum(len(b) for b in A)
        nb = sum(len(b) for b in B)
        ia = ib = 0
        da = db = 0
        while ia < len(A) or ib < len(B):
            fa = da / na if na else 1.0
            fb = db / nb if nb else 1.0
            if ib >= len(B) or (ia < len(A) and fa <= fb):
                self.replay([A[ia]])
                da += len(A[ia])
                ia += 1
            else:
                self.replay([B[ib]])
                db += len(B[ib])
                ib += 1

    def barrier(self):
        lasts = {self.last[e] for e in ENGS if self.last[e] is not None}
        for e in ENGS:
            self.pending_barrier[e] |= lasts

    def emit(self, stack, final_wait_eng="sp", nds=32):
        nc = self.nc
        ops = self.ops
        esem = {e: stack.enter_context(nc.semaphore("s_" + e)) for e in ENGS}
        dsem = [stack.enter_context(nc.semaphore("d_%d" % i)) for i in range(nds)]
        dcount = [0] * nds
        dma_prev = [None] * nds
        npool = (nds * 2) // 3
        kk = {"pool": 0, "sp": 0}
        for o in ops:
            if o["dma"]:
                if o["eng"] == "pool":
                    s = kk["pool"] % npool
                    kk["pool"] += 1
                else:
                    s = npool + kk["sp"] % (nds - npool)
                    kk["sp"] += 1
                if dma_prev[s] is not None:
                    o["deps"].setdefault(dma_prev[s], True)
                o["dsem"] = s
                dma_prev[s] = o["id"]
        needed = set()
        for o in ops:
            needed |= set(o["deps"])
        ecount = {e: 0 for e in ENGS}
        ref = {}
        for o in ops:
            if o["dma"]:
                s = o["dsem"]
                dcount[s] += 16
                ref[o["id"]] = (dsem[s], dcount[s])
            elif o["id"] in needed:
                ecount[o["eng"]] += 1
                ref[o["id"]] = (esem[o["eng"]], ecount[o["eng"]])
        dma_ids = [o["id"] for o in ops if o["dma"]]
        per = {e: [o for o in ops if o["eng"] == e] for e in ENGS}
        block = stack.enter_context(nc.Block())
        self.nwaits = 0

        def mk(e):
            def body(engine):
                waited = {}
                for o in per[e]:
                    for d in sorted(o["deps"]):
                        sem, val = ref[d]
                        if waited.get(id(sem), 0) >= val:
                            continue
                        engine.wait_ge(sem, val)
                        self.nwaits += 1
                        waited[id(sem)] = val
                    meth, a, kw = o["fn"]
                    ins = getattr(engine, meth)(*a, **kw)
                    if o["id"] in ref:
                        sem, val = ref[o["id"]]
                        ins.then_inc(sem, 16 if o["dma"] else 1)
                if e == final_wait_eng:
                    for d in dma_ids:
                        sem, val = ref[d]
                        if waited.get(id(sem), 0) < val:
                            engine.wait_ge(sem, val)
                            waited[id(sem)] = val
            return body

        block.tensor(mk("pe"))
        block.scalar(mk("act"))
        block.vector(mk("dve"))
        block.gpsimd(mk("pool"))
        block.sync(mk("sp"))


class Arena:
    def __init__(self, tensor, nbytes):
        self.t = tensor
        self.nbytes = nbytes
        self.off = 0
        self.peak = 0

    def alloc(self, shape, dt):
        esz = 4 if dt == F32 else 2
        n = 1
        for s in shape:
            n *= s
        nb = n * esz
        off = (self.off + 3) // 4 * 4
        assert off + nb <= self.nbytes, ("SBUF arena overflow", off + nb, self.nbytes)
        self.off = off + nb
        self.peak = max(self.peak, self.off)
        v = self.t[:, off // 2: (off + nb) // 2]
        if dt == F32:
            v = v.bitcast(F32)
        if len(shape) == 2:
            v = v.rearrange("p (a b) -> p a b", b=shape[1])
        elif len(shape) == 3:
            v = v.rearrange("p (a b c) -> p a b c", b=shape[1], c=shape[2])
        return v

    def mark(self):
        return self.off

    def reset(self, m):
        self.off = m


def build_nc(npre=NPRE, do_own=True, do_ffn=True, dbg=False, lvl=9):
    nc = bass.Bass("TRN2", target_bir_lowering=False)
    dt_in = lambda name, shape: nc.dram_tensor(name, shape, F32, kind="ExternalInput").ap()
    xall = dt_in("xall", [NHB * D, HT])
    w_in = dt_in("w_in", [D, 7200])
    pool_w = dt_in("pool_w", [4 * 512, 512])
    w_out = dt_in("w_out", [4096, D])
    w_gate = dt_in("w_gate", [D, DFF])
    w_up = dt_in("w_up", [D, DFF])
    w_down = dt_in("w_down", [DFF, D])
    consts_d = dt_in("consts", [128, NCONST])
    params_d = dt_in("params", [128, NPAR])
    outT = nc.dram_tensor("outT", [D, TOK], F32, kind="ExternalOutput").ap()
    if dbg:
        dbg_S = nc.dram_tensor("dbg_S", [128, 2048], F32, kind="ExternalOutput").ap()
        dbg_R = nc.dram_tensor("dbg_R", [128, 2 * 16 * HT], F32, kind="ExternalOutput").ap()
        dbg_H = nc.dram_tensor("dbg_H", [128, 16 * HT], F32, kind="ExternalOutput").ap()

    xall_v = xall.rearrange("(h c p) t -> h p c t", c=16, p=128)
    w_in_v = w_in.rearrange("(kc p) n -> p kc n", p=128)
    w_out_v = w_out.rearrange("(kc p) n -> p kc n", p=128)
    pool_w_v = pool_w.rearrange("(g kc p) n -> g p kc n", g=4, p=128)
    w_gate_v = w_gate.rearrange("(kc p) n -> p kc n", p=128)
    w_up_v = w_up.rearrange("(kc p) n -> p kc n", p=128)
    w_down_v = w_down.rearrange("(kc p) n -> p kc n", p=128)
    outT_v = outT.rearrange("(c p) t -> p c t", p=128)

    with ExitStack() as st:
        TOTAL = 212800
        arena_t = st.enter_context(nc.sbuf_tensor("arena", [128, TOTAL // 2], BF16))
        pbank = [st.enter_context(nc.psum_tensor("pb%d" % i, [128, 512], F32)) for i in range(8)]
        A = Arena(arena_t, TOTAL)
        S = Sched(nc)

        def I(eng, meth, r=(), w=(), dma=False, a=(), **kw):
            S.op(eng, (meth, tuple(a), kw), r=r, w=w, dma=dma)

        cst = A.alloc([NCONST], F32)
        prm = A.alloc([NPAR], F32)
        ones_bf = A.alloc([128], BF16)
        ident_bf = A.alloc([128], BF16)
        LT_bf = A.alloc([128], BF16)
        a_b = A.alloc([32], F32)
        carry = A.alloc([24, 3], F32)
        M_S = A.mark()
        Sst = A.alloc([4, 512], F32)
        hn = A.alloc([16, 16 + HT], BF16)
        M_R = A.mark()
        R = [A.alloc([16, HT], F32), A.alloc([16, HT], F32)]
        PH = A.mark()

        onesf = cst[:, K_ONES:K_ONES + 128]
        LTf = cst[:, K_LT:K_LT + 128]
        T0 = cst[:, K_T0:K_T0 + 256]
        T1h = cst[:, K_T1 + 128:K_T1 + 256]
        caus = cst[:, K_CAUS:K_CAUS + 384]
        identf = cst[:, K_ID:K_ID + 128]

        def pcol(off, i=0, n=1):
            return prm[:, off + i: off + i + n]

        def dma_w(dst, src, tok, eng="pool"):
            I(eng, "dma_start", w=[tok], dma=True, out=dst, in_=src)

        mm_banks = [[0, 1, 2, 7]]
        mm_rr = [0]

        def next_mm():
            b = mm_banks[0]
            i = b[mm_rr[0] % len(b)]
            mm_rr[0] += 1
            return i

        pCB = pbank[3][:, 0:384]
        pdt = pbank[3][:, 384:512].rearrange("p (a b) -> p a b", b=32)
        pss = pbank[4][:, :]
        ptr = pbank[5][:, 0:256].bitcast(BF16).rearrange("p (a b) -> p a b", b=128)
        pY = pbank[5][:, 256:512]
        pYo = pbank[6][:, 0:256]
        pra = [pbank[6][:, 256 + q * 96: 256 + (q + 1) * 96].rearrange("p (a b) -> p a b", b=32) for q in range(2)]
        ph = pbank[6][:, 448:464]
        pcc = pbank[6][:, 464:468]
        pAbs = [(pbank[7][:, 0:256], "pb7"), (pbank[2][:, 0:256], "pb2")]
        pYs = [(pbank[5][:, 256:512], "pb5"), (pbank[3][:, 0:256], "pb3")]
        pYos = [(pbank[6][:, 0:256], "pb6"), (pbank[4][:, 256:512], "pb4")]

        I("sp", "dma_start", w=["cst"], dma=True, out=cst, in_=consts_d[:, :])
        I("sp", "dma_start", w=["prm"], dma=True, out=prm, in_=params_d[:, :])
        I("dve", "tensor_copy", r=["cst"], w=["ones_bf"], a=(ones_bf, onesf))
        I("dve", "tensor_copy", r=["cst"], w=["ident_bf"], a=(ident_bf, identf))
        I("dve", "tensor_copy", r=["cst"], w=["LT_bf"], a=(LT_bf, LTf))
        I("act", "activation", r=["prm"], w=["a_b"], out=a_b, in_=prm[:, P_ALOG:P_ALOG + 32], func=AF.Exp)
        I("dve", "tensor_scalar", r=["a_b"], w=["a_b"], out=a_b, in0=a_b, scalar1=-1.0, scalar2=None, op0=ALU.mult)
        I("pool", "memset", w=["carry%d" % m_ for m_ in range(24)], a=(carry, 0.0))
        I("pool", "memset", w=["S0", "S1", "S2", "S3"], a=(Sst, 0.0))
        I("pool", "memset", w=["hn"], a=(hn, 0.0))

        def norm_half(src, nw_off, dst_fn, tmp, dst_tok="hn", src_tok="R"):
            for c in range(16):
                sqb = tmp["sq"][c % 2]
                I("act", "activation", r=[src_tok], w=["sq%d" % (c % 2)], out=sqb, in_=src[:, c, :], func=AF.Square)
                I("pe", "matmul", r=["ones_bf", "sq%d" % (c % 2)], w=["pb4"], a=(pss, ones_bf, sqb),
                  start=(c == 0), stop=(c == 15))
            I("act", "activation", r=["pb4"], w=["sd"], out=tmp["sd"], in_=pss, func=AF.Sqrt, bias=EPS, scale=1.0 / D)
            I("dve", "reciprocal", r=["sd"], w=["rstd"], a=(tmp["rstd"], tmp["sd"]))
            for c in range(16):
                I("dve", "scalar_tensor_tensor", r=[src_tok, "prm", "rstd"], w=[dst_tok], out=dst_fn(c),
                  in0=src[:, c, :], scalar=pcol(nw_off, c), in1=tmp["rstd"], op0=ALU.mult, op1=ALU.mult)

        def mixer_phase_alloc(own):
            t = {}
            t["sq"] = [A.alloc([HT], BF16), A.alloc([HT], BF16)]
            t["sd"] = A.alloc([HT], F32)
            t["rstd"] = A.alloc([HT], F32)
            t["pre"] = [A.alloc([3 + HT + 1], F32) for _ in range(2 if own else 3)]
            t["acc"] = [A.alloc([HT], F32) for _ in range(2 if own else 3)]
            t["xc"] = [A.alloc([HT], BF16) for _ in range(0 if own else 3)]
            t["xtok"] = [A.alloc([4, 640], BF16) for _ in range(1 if own else 2)]
            t["xw"] = [A.alloc([2, 512], BF16) for _ in range(1 if own else 2)]
            for k in ("dtr", "dtabs", "dtl", "dt", "dtA", "lndt"):
                t[k] = A.alloc([4, 32], F32)
            t["RAraw"] = [A.alloc([3, 32], F32) for _ in range(2)]
            t["eR"] = [A.alloc([3, 32], F32) for _ in range(2)]
            t["wgt"] = [A.alloc([2, 32], F32) for _ in range(2)]
            t["negA"] = [A.alloc([2, 32], F32) for _ in range(2)]
            t["stmp"] = A.alloc([512], F32)
            t["d3"] = [A.alloc([4, 32], BF16) for _ in range(3)]
            t["rr"] = [A.alloc([4, 32], F32) for _ in range(2)]
            return t

        def dt_block(t, own, hb, WDT, hsrc=None, htok="hn"):
            hsrc = (lambda kc, tt: hn[:, kc, 16 + tt * 128: 16 + (tt + 1) * 128]) if hsrc is None else hsrc
            for tt in range(4):
                for kc in range(16):
                    I("pe", "matmul", r=[htok, "wdt"], w=["pb3"],
                      a=(pdt[:, tt, :], hsrc(kc, tt), WDT[:, kc, :]),
                      start=(kc == 0), stop=(kc == 15))
            bias_b = prm[:, P_DTB:P_DTB + 32].unsqueeze(1).to_broadcast([128, 4, 32])
            I("dve", "tensor_tensor", r=["pb3", "prm"], w=["dtr"], out=t["dtr"], in0=pdt, in1=bias_b, op=ALU.add)
            I("act", "activation", r=["dtr"], w=["dtabs"], out=t["dtabs"], in_=t["dtr"], func=AF.Abs)
            I("act", "activation", r=["dtabs"], w=["dtabs"], out=t["dtabs"], in_=t["dtabs"], func=AF.Exp, scale=-1.0)
            I("act", "activation", r=["dtabs"], w=["dtl"], out=t["dtl"], in_=t["dtabs"], func=AF.Ln, bias=1.0, scale=1.0)
            I("dve", "scalar_tensor_tensor", r=["dtr", "dtl"], w=["dt"], out=t["dt"], in0=t["dtr"], scalar=0.0,
              in1=t["dtl"], op0=ALU.max, op1=ALU.add)
            if not own:
                I("dve", "tensor_scalar", r=["dt", "prm"], w=["dt"], out=t["dt"], in0=t["dt"],
                  scalar1=pcol(P_VALID, hb), scalar2=None, op0=ALU.mult)
            a_bb = a_b.unsqueeze(1).to_broadcast([128, 4, 32])
            I("dve", "tensor_tensor", r=["dt", "a_b"], w=["dtA"], out=t["dtA"], in0=t["dt"], in1=a_bb, op=ALU.mult)
            if own:
                I("act", "activation", r=["dt"], w=["lndt"], out=t["lndt"], in_=t["dt"], func=AF.Ln)
            d3, rr = t["d3"], t["rr"]
            I("dve", "tensor_copy", r=["dtA"], w=["d3_0"], a=(d3[0], t["dtA"]))
            I("dve", "tensor_tensor", r=["dtA", "d3_0"], w=["rr0"], out=rr[0], in0=t["dtA"], in1=d3[0], op=ALU.subtract)
            I("dve", "tensor_copy", r=["rr0"], w=["d3_1"], a=(d3[1], rr[0]))
            I("dve", "tensor_tensor", r=["rr0", "d3_1"], w=["rr1"], out=rr[1], in0=rr[0], in1=d3[1], op=ALU.subtract)
            I("dve", "tensor_copy", r=["rr1"], w=["d3_2"], a=(d3[2], rr[1]))
            rd = ["LT_bf", "ones_bf", "d3_0", "d3_1", "d3_2"]
            for q in range(2):
                ta, tb = 2 * q, 2 * q + 1
                p_ = pra[q]
                tk = "pb6"
                for k in range(3):
                    I("pe", "matmul", r=rd, w=[tk], a=(p_[:, 0, :], LT_bf, d3[k][:, ta, :]), start=(k == 0), stop=False)
                    I("pe", "matmul", r=rd, w=[tk], a=(p_[:, 0, :], ones_bf, d3[k][:, tb, :]), start=False, stop=(k == 2))
                for k in range(3):
                    I("pe", "matmul", r=rd, w=[tk], a=(p_[:, 1, :], LT_bf, d3[k][:, tb, :]), start=(k == 0), stop=(k == 2))
                for k in range(3):
                    I("pe", "matmul", r=rd, w=[tk], a=(p_[:, 2, :], ones_bf, d3[k][:, ta, :]), start=(k == 0), stop=False)
                    I("pe", "matmul", r=rd, w=[tk], a=(p_[:, 2, :], ones_bf, d3[k][:, tb, :]), start=False, stop=(k == 2))
                I("act", "activation", r=[tk], w=["eR%d" % q], out=t["eR"][q], in_=p_, func=AF.Exp)
                I("dve", "tensor_tensor", r=["eR%d" % q, "dt"], w=["wgt%d" % q], out=t["wgt"][q],
                  in0=t["eR"][q][:, 0:2, :], in1=t["dt"][:, ta:ta + 2, :], op=ALU.mult)
                if own:
                    I("act", "copy", r=[tk], w=["RAraw%d" % q], a=(t["RAraw"][q], p_))
                    aend_b = t["RAraw"][q][:, 2:3, :].to_broadcast([128, 2, 32])
                    I("dve", "tensor_tensor", r=["RAraw%d" % q], w=["negA%d" % q], out=t["negA"][q],
                      in0=aend_b, in1=t["RAraw"][q][:, 0:2, :], op=ALU.subtract)

        def proj_chunk(wsrc, wtok, col0, rhs_fn, n, mmi, out=None, otok=None, htok="hn"):
            pm = pbank[mmi][:, 0:n] if out is None else out
            otok = otok or ("pb%d" % mmi)
            for kc in range(16):
                I("pe", "matmul", r=[wtok, htok], w=[otok], a=(pm, wsrc[:, kc, col0:col0 + 128], rhs_fn(kc)),
                  start=(kc == 0), stop=(kc == 15))
            return pm

        def conv_s1(t, pm, mmi, m, i):
            pre = t["pre"][i % len(t["pre"])]
            acc = t["acc"][i % len(t["acc"])]
            ptk, atk = "pre%d" % (i % len(t["pre"])), "acc%d" % (i % len(t["acc"]))
            I("act", "copy", r=["pb%d" % mmi], w=[ptk], a=(pre[:, 3:3 + HT], pm))
            I("act", "copy", r=["carry%d" % m], w=[ptk], a=(pre[:, 0:3], carry[:, m, :]))
            I("act", "activation", r=[ptk, "prm"], w=[atk], out=acc, in_=pre[:, 0:HT], func=AF.Identity,
              bias=pcol(P_CB, m), scale=pcol(P_CW, m * 4 + 0))

        def conv_s2(t, m, i):
            pre = t["pre"][i % len(t["pre"])]
            acc = t["acc"][i % len(t["acc"])]
            ptk, atk = "pre%d" % (i % len(t["pre"])), "acc%d" % (i % len(t["acc"]))
            for k in (1, 2, 3):
                I("dve", "scalar_tensor_tensor", r=[ptk, "prm", atk], w=[atk], out=acc, in0=pre[:, k:k + HT],
                  scalar=pcol(P_CW, m * 4 + k), in1=acc, op0=ALU.mult, op1=ALU.add)

        def conv_s3(t, m, dst, dst_tok, i):
            pre = t["pre"][i % len(t["pre"])]
            acc = t["acc"][i % len(t["acc"])]
            ptk, atk = "pre%d" % (i % len(t["pre"])), "acc%d" % (i % len(t["acc"]))
            I("act", "copy", r=[ptk], w=["carry%d" % m], a=(carry[:, m, :], pre[:, HT:HT + 3]))
            I("act", "activation", r=[atk], w=[dst_tok], out=dst, in_=acc, func=AF.Silu)

        def conv_silu(t, pm, mmi, m, dst, dst_tok, i):
            conv_s1(t, pm, mmi, m, i)
            conv_s2(t, m, i)
            conv_s3(t, m, dst, dst_tok, i)

        def transpose_pe(src, src_tok):
            for tt in range(4):
                I("pe", "transpose", r=[src_tok, "ident_bf"], w=["pb5"],
                  a=(ptr[:, tt, :], src[:, tt * 128:(tt + 1) * 128], ident_bf))

        def transpose_evac(xtok, xtok_tok, col0):
            I("act", "copy", r=["pb5"], w=[xtok_tok], a=(xtok[:, :, col0:col0 + 128], ptr))

        def transpose_to_tok(src, src_tok, xtok, xtok_tok, col0):
            for tt in range(4):
                I("pe", "transpose", r=[src_tok, "ident_bf"], w=["pb5"],
                  a=(ptr[:, tt, :], src[:, tt * 128:(tt + 1) * 128], ident_bf))
            I("act", "copy", r=["pb5"], w=[xtok_tok], a=(xtok[:, :, col0:col0 + 128], ptr))

        def state_update_a(t, g, q, xtok, xtok_tok, gi):
            ta, tb = 2 * q, 2 * q + 1
            xw = t["xw"][gi % len(t["xw"])]
            xwt = "xw%d" % (gi % len(t["xw"]))
            for j, tt in enumerate((ta, tb)):
                wb = t["wgt"][q][:, j, 8 * g:8 * g + 8].unsqueeze(2).to_broadcast([128, 8, 64])
                I("dve", "tensor_tensor", r=[xtok_tok, "wgt%d" % q], w=[xwt],
                  out=xw[:, j, :].rearrange("p (a b) -> p a b", b=64),
                  in0=xtok[:, tt, 0:512].rearrange("p (a b) -> p a b", b=64), in1=wb, op=ALU.mult)

        def state_update_b(t, g, q, xtok, xtok_tok, gi, pS=None, pStok="pb4"):
            pS = pss if pS is None else pS
            ta, tb = 2 * q, 2 * q + 1
            xw = t["xw"][gi % len(t["xw"])]
            xwt = "xw%d" % (gi % len(t["xw"]))
            for j, tt in enumerate((ta, tb)):
                I("pe", "matmul", r=[xtok_tok, xwt], w=[pStok], a=(pS, xtok[:, tt, 512:640], xw[:, j, :]),
                  start=(j == 0), stop=(j == 1))
            decb = t["eR"][q][:, 2, 8 * g:8 * g + 8].unsqueeze(2).to_broadcast([128, 8, 64])
            Sg = Sst[:, g, :]
            I("dve", "tensor_tensor", r=["S%d" % g, "eR%d" % q], w=["stmp"],
              out=t["stmp"].rearrange("p (a b) -> p a b", b=64),
              in0=Sg.rearrange("p (a b) -> p a b", b=64), in1=decb, op=ALU.mult)
            I("dve", "tensor_tensor", r=["stmp", pStok], w=["S%d" % g], out=Sg, in0=t["stmp"], in1=pS, op=ALU.add)

        def state_update(t, g, q, xtok, xtok_tok, gi, pS=None, pStok="pb4"):
            state_update_a(t, g, q, xtok, xtok_tok, gi)
            state_update_b(t, g, q, xtok, xtok_tok, gi, pS=pS, pStok=pStok)

        def run_pipeline(items, stages, hooks=None):
            nst = len(stages)
            for n in range(len(items) + nst - 1):
                for s in reversed(range(nst)):
                    k = n - s
                    if 0 <= k < len(items):
                        stages[s](items[k])
                if hooks and n in hooks:
                    fs = hooks[n]
                    for f in (fs if isinstance(fs, list) else [fs]):
                        f()

        A.off = PH - 16 * HT * 4
        WRES = A.alloc([16, 2560], BF16)
        WDT_P = A.alloc([16, 32], BF16)
        hnB = A.alloc([16, HT], BF16)
        tP = mixer_phase_alloc(False)
        dma_w(WDT_P, w_in_v[:, :, C_DT:C_DT + 32], "wdt")
        for j in (0, 4, 1, 2, 3):
            dma_w(WRES[:, :, j * 512:(j + 1) * 512], w_in_v[:, :, C_X + j * 512: C_X + (j + 1) * 512], "wres%d" % j)

        ci = [0]
        gi = [0]
        rhs_main = lambda kc: hn[:, kc, 16:16 + HT]

        def norm_stats(src, tmp, src_tok):
            for c in range(16):
                sqb = tmp["sq"][c % 2]
                I("act", "activation", r=[src_tok], w=["sq%d" % (c % 2)], out=sqb, in_=src[:, c, :], func=AF.Square)
                I("pe", "matmul", r=["ones_bf", "sq%d" % (c % 2)], w=["pb4"], a=(pss, ones_bf, sqb),
                  start=(c == 0), stop=(c == 15))
            I("act", "activation", r=["pb4"], w=["sd"], out=tmp["sd"], in_=pss, func=AF.Sqrt, bias=EPS, scale=1.0 / D)
            I("dve", "reciprocal", r=["sd"], w=["rstd"], a=(tmp["rstd"], tmp["sd"]))

        def norm_apply(src, nw_off, dst_fn, tmp, dst_tok, src_tok):
            for c in range(16):
                I("dve", "scalar_tensor_tensor", r=[src_tok, "prm", "rstd"], w=[dst_tok], out=dst_fn(c),
                  in0=src[:, c, :], scalar=pcol(nw_off, c), in1=tmp["rstd"], op0=ALU.mult, op1=ALU.mult)

        pS_P = pbank[3][:, :]
        hbs = list(range(NPRE - npre, NPRE))
        NH = len(hbs)

        def hbuf(h):
            if (NH - 1 - h) % 2 == 0:
                return (lambda kc: hn[:, kc, 16:16 + HT]), "hn", (lambda c: hn[:, c, 16:16 + HT]), \
                       (lambda kc, tt: hn[:, kc, 16 + tt * 128: 16 + (tt + 1) * 128])
            return (lambda kc: hnB[:, kc, :]), "hnB", (lambda c: hnB[:, c, :]), \
                   (lambda kc, tt: hnB[:, kc, tt * 128:(tt + 1) * 128])

        info = []
        for h, hb in enumerate(hbs):
            rhs_fn, htok, _, _ = hbuf(h)
            for g in range(4):
                for i in range(5):
                    if i == 0:
                        cur_x = (tP["xtok"][gi[0] % 2], "xtok%d" % (gi[0] % 2), gi[0])
                        gi[0] += 1
                    if i < 4:
                        col0, m, dcol = g * 512 + i * 128, g * 4 + i, i * 128
                    else:
                        col0, m, dcol = 2048 + g * 128, 16 + g, 512
                    info.append(dict(h=h, g=g, i=i, col0=col0, m=m, dcol=dcol, xt=cur_x, ci=ci[0], xc=tP["xc"][ci[0] % 3],
                                     xct="xc%d" % (ci[0] % 3), rhs=rhs_fn, htok=htok))
                    ci[0] += 1

        def st0(it):
            it["mmi"] = next_mm()
            it["pm"] = proj_chunk(WRES, "wres%d" % (it["col0"] // 512), it["col0"], it["rhs"], HT, it["mmi"], htok=it["htok"])

        def st1(it):
            conv_s1(tP, it["pm"], it["mmi"], it["m"], it["ci"])

        def st2(it):
            conv_s2(tP, it["m"], it["ci"])

        def st3(it):
            conv_s3(tP, it["m"], it["xc"], it["xct"], it["ci"])

        def st4(it):
            transpose_pe(it["xc"], it["xct"])

        def st5(it):
            transpose_evac(it["xt"][0], it["xt"][1], it["dcol"])

        def st6(it):
            if it["i"] == 4 and lvl >= 4:
                xt, xk, gx = it["xt"]
                for q in range(2):
                    state_update_a(tP, it["g"], q, xt, xk, gx * 2 + q)

        def st7(it):
            pass

        def st8(it):
            if it["i"] == 4 and lvl >= 4:
                xt, xk, gx = it["xt"]
                for q in range(2):
                    state_update_b(tP, it["g"], q, xt, xk, gx * 2 + q, pS=pS_P, pStok="pb3")

        def load_x(h):
            I("sp", "dma_start", w=["R"], dma=True, out=R[0], in_=xall_v[hbs[h]])

        def do_stats(h):
            norm_stats(R[0], tP, "R")

        def do_apply(h):
            _, htok, dstf, _ = hbuf(h)
            norm_apply(R[0], P_NW1, dstf, tP, htok, "R")
            if h + 1 < NH:
                load_x(h + 1)
            elif do_own:
                I("sp", "dma_start", w=["R"], dma=True, out=R[0], in_=xall_v[NPRE])

        def do_dt(h):
            _, htok, _, dsrc = hbuf(h)
            dt_block(tP, False, hbs[h], WDT_P, hsrc=dsrc, htok=htok)

        hooks = {}
        if NH:
            load_x(0)
            do_stats(0)
            do_apply(0)
            do_dt(0)
            for h in range(NH - 1):
                bstep = 20 * h
                hooks.setdefault(bstep + 3, []).append(lambda h=h: do_stats(h + 1))
                hooks.setdefault(bstep + 8, []).append(lambda h=h: do_apply(h + 1))
                hooks.setdefault(bstep + 28, []).append(lambda h=h: do_dt(h + 1))
            run_pipeline(info, [st0, st1, st2, st3, st4, st5, st6, st7, st8], hooks=hooks)
        peakP = A.peak
        if dbg:
            I("sp", "dma_start", r=["S0", "S1", "S2", "S3"], dma=True, out=dbg_S[:, :], in_=Sst.rearrange("p a b -> p (a b)"))
            if not do_own:
                for c_ in range(16):
                    I("act", "copy", r=["hn"], w=["R"], a=(R[0][:, c_, :], hn[:, c_, 16:16 + HT]))
                I("sp", "dma_start", r=["R"], dma=True, out=dbg_H[:, :], in_=R[0].rearrange("p a b -> p (a b)"))

        S.barrier()
        mm_banks[0] = [0, 1]
        A.reset(PH)
        WDT_M = A.alloc([16, 32], BF16)
        tM = mixer_phase_alloc(True)
        WS = [A.alloc([16, 512], BF16) for _ in range(2)]
        bt = A.alloc([HT], BF16)
        ctt = A.alloc([HT], BF16)
        xTg = A.alloc([4, HT], BF16)
        szg = A.alloc([4, HT], BF16)
        T0_bf = A.alloc([256], BF16)
        T1h_bf = A.alloc([128], BF16)
        I("dve", "tensor_copy", r=["cst"], w=["T0_bf"], a=(T0_bf, T0))
        I("dve", "tensor_copy", r=["cst"], w=["T0_bf"], a=(T1h_bf, T1h))
        arg0 = [A.alloc([256], F32) for _ in range(2)]
        arg1 = [A.alloc([128], F32) for _ in range(2)]
        E0 = [tM["sd"][:, 0:256], tM["sd"][:, 256:512]]
        E1 = [tM["rstd"][:, 0:128], tM["rstd"][:, 128:256]]
        M1 = [tM["rstd"][:, 256:384].bitcast(BF16)[:, 0:128], tM["rstd"][:, 256:384].bitcast(BF16)[:, 128:256]]
        M0 = [tM["sq"][0][:, 0:256], tM["sq"][0][:, 256:512]]
        CBm0_ = A.alloc([384], F32)
        CBms = [(CBm0_, ["CBm"]), (tM["acc"][0][:, 0:384], ["acc0"])]
        eA = [A.alloc([256], F32) for _ in range(2)]
        Sbf0_ = A.alloc([512], BF16)
        Sbfs = [(Sbf0_, ["Sbf"]), (tM["acc"][1].bitcast(BF16)[:, 0:512], ["acc1"])]
        gbuf0_ = A.alloc([4, 256], F32)
        gb1a = tM["pre"][0][:, 0:512].rearrange("p (a b) -> p a b", b=256)
        gb1b = tM["pre"][1][:, 0:512].rearrange("p (a b) -> p a b", b=256)
        gbufs = [([gbuf0_[:, i, :] for i in range(4)], ["gbuf"]),
                 ([gb1a[:, 0, :], gb1a[:, 1, :], gb1b[:, 0, :], gb1b[:, 1, :]], ["pre0", "pre1"])]
        ytmp = [A.alloc([256], F32) for _ in range(2)]
        gsq = [A.alloc([256], BF16) for _ in range(2)]
        rstd_g = A.alloc([256], F32)
        sd_g = A.alloc([256], F32)
        yss = A.alloc([4, HT], BF16)
        ubuf = A.alloc([16 + HT], F32)
        pp = [A.alloc([16 + HT], F32) for _ in range(2)]
        pooled = A.alloc([4, HT], BF16)
        ypool = A.alloc([4, HT], BF16)
        dma_w(WDT_M, w_in_v[:, :, C_DT:C_DT + 32], "wdt")

        ws_rr = [0]

        def load_ws(srcs):
            i = ws_rr[0] % 2
            ws_rr[0] += 1
            ws = WS[i]
            tok = "ws%d" % i
            for dfn, src in srcs:
                dma_w(dfn(ws), src, tok)
            return ws, tok

        T0b = T0.unsqueeze(1).to_broadcast([128, 2, 256])
        T1b = T1h.unsqueeze(1).to_broadcast([128, 2, 128])
        pG = pbank[4][:, 0:256]

        for hh in (range(2) if do_own else ()):
            hb = NPRE + hh
            if hh == 1:
                S.barrier()
            Rh = R[hh]
            rt = "R%d" % hh
            if hh == 0 and NH > 0:
                rt = "R"
            if hh == 0:
                if NH == 0:
                    I("sp", "dma_start", w=["R0"], dma=True, out=R[0], in_=xall_v[NPRE])
                I("sp", "dma_start", w=["R1"], dma=True, out=R[1], in_=xall_v[NPRE + 1])
            I("dve", "tensor_copy", r=["hn"], w=["hn"], a=(hn[:, :, 0:16], hn[:, :, HT:HT + 16]))
            norm_half(Rh, P_NW1, lambda c: hn[:, c, 16:16 + HT], tM, src_tok=rt)
            dt_block(tM, True, hb, WDT_M)
            xtok = tM["xtok"][0]
            xtk = "xtok0"
            hcnt = [0]

            def o_st0(it):
                it["mmi"] = next_mm()
                it["pm"] = proj_chunk(it["ws"], it["wt"], it["wcol"], rhs_main, HT, it["mmi"])
                if it["kind"] == "u":
                    proj_chunk(it["ws"], it["wt"], it["wcol"], lambda kc: hn[:, kc, 0:16], 16, 6, out=ph, otok="pb6")

            def o_st1(it):
                k_ = it["kind"]
                if k_ in ("x", "B", "C"):
                    conv_s1(tM, it["pm"], it["mmi"], it["m"], it["ci"])
                    conv_s2(tM, it["m"], it["ci"])
                elif k_ == "z":
                    I("act", "activation", r=["pb%d" % it["mmi"]], w=["szg"], out=szg[:, it["i"], :], in_=it["pm"], func=AF.Silu)
                else:
                    I("act", "copy", r=["pb%d" % it["mmi"]], w=["ubuf"], a=(ubuf[:, 16:16 + HT], it["pm"]))
                    I("act", "copy", r=["pb6"], w=["ubuf"], a=(ubuf[:, 0:16], ph))

            def o_st2(it):
                k_ = it["kind"]
                if k_ in ("x", "B", "C"):
                    conv_s3(tM, it["m"], it["dst"], it["dtok"], it["ci"])
                elif k_ == "u":
                    g_, i_ = it["g"], it["i"]
                    wwin = float(1 << (g_ + 1))
                    src, stok, lo = ubuf, "ubuf", 0
                    for k in range(g_ + 1):
                        sh = 1 << k
                        dst, dtok = pp[k % 2], "pp%d" % (k % 2)
                        lo2 = lo + sh
                        I("dve", "tensor_tensor", r=[stok], w=[dtok], out=dst[:, lo2:16 + HT], in0=src[:, lo2:16 + HT],
                          in1=src[:, lo2 - sh:16 + HT - sh], op=ALU.add)
                        src, stok, lo = dst, dtok, lo2
                    I("dve", "scalar_tensor_tensor", r=[stok, "ubuf"], w=["pooled"], out=pooled[:, i_, 16:HT],
                      in0=src[:, 32:16 + HT], scalar=1.0 / wwin, in1=ubuf[:, 32:16 + HT], op0=ALU.mult, op1=ALU.subtract)
                    oth = pp[(g_ + 1) % 2]
                    otk = "pp%d" % ((g_ + 1) % 2)
                    io = P_ICNT + it["hh"] * 64 + g_ * 16
                    I("dve", "tensor_tensor", r=[stok, "prm"], w=[otk], out=oth[:, 0:16], in0=src[:, 16:32],
                      in1=prm[:, io:io + 16], op=ALU.mult)
                    I("dve", "tensor_tensor", r=[otk, "ubuf"], w=["pooled"], out=pooled[:, i_, 0:16], in0=oth[:, 0:16],
                      in1=ubuf[:, 16:32], op=ALU.subtract)

            def o_st3(it):
                if it["kind"] in ("x", "B"):
                    transpose_pe(it["dst"], it["dtok"])

            def o_st4(it):
                if it["kind"] in ("x", "B"):
                    transpose_evac(xtok, xtk, it["dcol"])
                S.new_block()

            o_stages = [o_st0, o_st1, o_st2, o_st3, o_st4]

            def out_proj_part(kc0, src_t, src_tok):
                for dh in range(2):
                    ws, wt = load_ws([(lambda w_: w_[:, 0:8, :].rearrange("p (a b) c -> p a (b c)", b=2),
                                       w_out_v[:, kc0:kc0 + 4, dh * 1024:(dh + 1) * 1024])])
                    wv = ws[:, 0:8, :].rearrange("p (a b) c -> p a (b c)", b=2)
                    for dd in range(8):
                        d = dh * 8 + dd
                        mmi = next_mm()
                        pm = pbank[mmi][:, :]
                        for i in range(4):
                            I("pe", "matmul", r=[wt, src_tok], w=["pb%d" % mmi],
                              a=(pm, wv[:, i, dd * 128:(dd + 1) * 128], src_t[:, i, :]), start=(i == 0), stop=(i == 3))
                        I("dve", "tensor_tensor", r=[rt, "pb%d" % mmi], w=[rt], out=Rh[:, d, :], in0=Rh[:, d, :], in1=pm, op=ALU.add)
                        S.new_block()

            def ssd_group(g):
                for q in range(2):
                    ta, tb = 2 * q, 2 * q + 1
                    Sb, Sbt = Sbfs[q]
                    I("act", "copy", r=["S%d" % g], w=Sbt, a=(Sb, Sst[:, g, :]))
                    state_update(tM, g, q, xtok, xtk, gi[0])
                    gi[0] += 1
                    I("pe", "matmul", r=["bt", "ct"], w=["pb3"],
                      a=(pCB[:, 0:256], bt[:, ta * 128:(ta + 1) * 128], ctt[:, ta * 128: ta * 128 + 256]), start=True, stop=True)
                    I("pe", "matmul", r=["bt", "ct"], w=["pb3"],
                      a=(pCB[:, 256:384], bt[:, tb * 128:(tb + 1) * 128], ctt[:, tb * 128:(tb + 1) * 128]), start=True, stop=True)
                    cb, cbt = CBms[q]
                    I("dve", "tensor_tensor", r=["pb3", "cst"], w=cbt, out=cb, in0=pCB, in1=caus, op=ALU.mult)
                S.new_block()
                heads = []
                for q in range(2):
                    for h8 in range(8):
                        k = hcnt[0] % 2
                        pk = (hcnt[0] // 2) % 2
                        heads.append(dict(q=q, h=8 * g + h8, hp=h8 // 2, hx=h8 % 2, hc=h8 * 64, k=k, pk=pk))
                        hcnt[0] += 1

                def s0(it):
                    pass

                def s1(it):
                    k, h, q = it["k"], it["h"], it["q"]
                    pA, pAt = pAbs[k]
                    rd = ["T0_bf", "d3_0", "d3_1", "d3_2"]
                    for kk in range(3):
                        la = tM["d3"][kk][:, 2 * q, h:h + 1].to_broadcast([128, 128])
                        I("pe", "matmul", r=rd, w=[pAt], a=(pA, la, T0_bf), start=(kk == 0), stop=False)
                    for kk in range(3):
                        lb = tM["d3"][kk][:, 2 * q + 1, h:h + 1].to_broadcast([128, 128])
                        I("pe", "matmul", r=rd, w=[pAt], a=(pA[:, 128:256], lb, T1h_bf), start=False, stop=(kk == 2))

                def s2(it):
                    k, h, q = it["k"], it["h"], it["q"]
                    pA, pAt = pAbs[k]
                    negA = tM["negA"][q]
                    nq = "negA%d" % q
                    I("act", "activation", r=[pAt, nq], w=["arg0_%d" % k], out=arg0[k], in_=pA, func=AF.Relu,
                      bias=negA[:, 0, h:h + 1], scale=-1.0)
                    I("act", "activation", r=[pAt, nq], w=["arg1_%d" % k], out=arg1[k], in_=pA[:, 128:256], func=AF.Relu,
                      bias=negA[:, 1, h:h + 1], scale=-1.0)

                def s3(it):
                    k, h, hx, pk, q = it["k"], it["h"], it["hx"], it["pk"], it["q"]
                    pA, pAt = pAbs[k]
                    I("act", "activation", r=["arg0_%d" % k, "lndt"], w=["E0_%d" % k], out=E0[k], in_=arg0[k], func=AF.Exp,
                      bias=tM["lndt"][:, 2 * q, h:h + 1], scale=-1.0)
                    I("act", "activation", r=["arg1_%d" % k, "lndt"], w=["E1_%d" % k], out=E1[k], in_=arg1[k], func=AF.Exp,
                      bias=tM["lndt"][:, 2 * q + 1, h:h + 1], scale=-1.0)
                    I("act", "activation", r=[pAt], w=["eA%d" % pk], out=eA[pk][hx * 64:(hx + 1) * 64, :],
                      in_=pA[hx * 64:(hx + 1) * 64, :], func=AF.Exp)

                def s4(it):
                    k, q = it["k"], it["q"]
                    cb, cbt = CBms[q]
                    I("dve", "tensor_tensor", r=["E0_%d" % k] + cbt, w=["M0_%d" % k], out=M0[k], in0=E0[k], in1=cb[:, 0:256], op=ALU.mult)
                    I("dve", "tensor_tensor", r=["E1_%d" % k] + cbt, w=["M1_%d" % k], out=M1[k], in0=E1[k], in1=cb[:, 256:384], op=ALU.mult)

                def s5(it):
                    k, hx, hc, pk, hp, q = it["k"], it["hx"], it["hc"], it["pk"], it["hp"], it["q"]
                    pYv, pYt = pYs[pk]
                    I("pe", "matmul", r=[xtk, "M0_%d" % k], w=[pYt],
                      a=(pYv[hx * 64:(hx + 1) * 64, :], xtok[:, 2 * q, hc:hc + 64], M0[k]), start=True, stop=False)
                    I("pe", "matmul", r=[xtk, "M1_%d" % k], w=[pYt],
                      a=(pYv[hx * 64:(hx + 1) * 64, 128:256], xtok[:, 2 * q + 1, hc:hc + 64], M1[k]), start=False, stop=True)
                    if hx == 1:
                        pYov, pYot = pYos[pk]
                        Sb, Sbt = Sbfs[q]
                        I("pe", "matmul", r=Sbt + ["ct"], w=[pYot],
                          a=(pYov, Sb[:, hp * 128:(hp + 1) * 128], ctt[:, q * 256:(q + 1) * 256]), start=True, stop=True)

                def s6(it):
                    if it["hx"] != 1:
                        return
                    pk, hp, q = it["pk"], it["hp"], it["q"]
                    tks = slice(q * 256, (q + 1) * 256)
                    pYv, pYt = pYs[pk]
                    pYov, pYot = pYos[pk]
                    yt, ytk = ytmp[pk], "ytmp%d" % pk
                    gb, gbt = gbufs[q]
                    I("dve", "tensor_tensor", r=[pYot, "eA%d" % pk], w=[ytk], out=yt, in0=pYov, in1=eA[pk], op=ALU.mult)
                    I("dve", "tensor_tensor", r=[ytk, pYt], w=[ytk], out=yt, in0=yt, in1=pYv, op=ALU.add)
                    I("dve", "scalar_tensor_tensor", r=["xTg", "prm", ytk], w=[ytk], out=yt, in0=xTg[:, hp, tks],
                      scalar=pcol(P_DSK, g * 4 + hp), in1=yt, op0=ALU.mult, op1=ALU.add)
                    I("dve", "tensor_tensor", r=[ytk, "szg"], w=gbt, out=gb[hp], in0=yt, in1=szg[:, hp, tks], op=ALU.mult)

                def s7(it):
                    S.new_block()

                def epilogue(q):
                    tks = slice(q * 256, (q + 1) * 256)
                    gb, gbt = gbufs[q]
                    for hp in range(4):
                        gs = gsq[hp % 2]
                        I("act", "activation", r=gbt, w=["gsq%d" % (hp % 2)], out=gs, in_=gb[hp], func=AF.Square)
                        I("pe", "matmul", r=["ones_bf", "gsq%d" % (hp % 2)], w=["pb4"], a=(pG, ones_bf, gs),
                          start=(hp == 0), stop=(hp == 3))
                    I("act", "activation", r=["pb4"], w=["sd_g"], out=sd_g, in_=pG, func=AF.Sqrt, bias=EPS, scale=1.0 / 512)
                    I("dve", "reciprocal", r=["sd_g"], w=["rstd_g"], a=(rstd_g, sd_g))
                    for hp in range(4):
                        I("dve", "scalar_tensor_tensor", r=gbt + ["prm", "rstd_g"], w=["yss"], out=yss[:, hp, tks],
                          in0=gb[hp], scalar=pcol(P_SSDW, g * 4 + hp), in1=rstd_g, op0=ALU.mult, op1=ALU.mult)
                    S.new_block()

                run_pipeline(heads, [s0, s1, s2, s3, s4, s5, s6, s7], hooks={13: (lambda: epilogue(0))})
                epilogue(1)

            def pool_branch(g):
                ws, wt = load_ws([(lambda w_: w_[:, :, :], w_in_v[:, :, C_U + g * 512: C_U + (g + 1) * 512])])
                ws2, wt2 = load_ws([(lambda w_: w_[:, 0:4, :], pool_w_v[g])])
                its = [dict(kind="u", g=g, i=i, hh=hh, ws=ws, wt=wt, wcol=i * 128) for i in range(4)]
                run_pipeline(its, o_stages)
                for j in range(4):
                    mmi = next_mm()
                    pm = pbank[mmi][:, :]
                    for i in range(4):
                        I("pe", "matmul", r=[wt2, "pooled"], w=["pb%d" % mmi],
                          a=(pm, ws2[:, i, j * 128:(j + 1) * 128], pooled[:, i, :]), start=(i == 0), stop=(i == 3))
                    I("act", "mul", r=["pb%d" % mmi, "prm"], w=["ypool"], a=(ypool[:, j, :], pm, pcol(P_PSC, g * 4 + j)))
                    S.new_block()

            for g in range(4):
                wsx, wtx = load_ws([(lambda w_: w_[:, :, :], w_in_v[:, :, C_X + g * 512: C_X + (g + 1) * 512])])
                wsb, wtb = load_ws([(lambda w_: w_[:, :, 0:128], w_in_v[:, :, C_B + g * 128: C_B + (g + 1) * 128]),
                                    (lambda w_: w_[:, :, 128:256], w_in_v[:, :, C_C + g * 128: C_C + (g + 1) * 128])])
                its = []
                for i in range(4):
                    its.append(dict(kind="x", g=g, i=i, ws=wsx, wt=wtx, wcol=i * 128, m=g * 4 + i, dst=xTg[:, i, :], dtok="xTg",
                                    dcol=i * 128, ci=ci[0]))
                    ci[0] += 1
                its.append(dict(kind="B", g=g, i=0, ws=wsb, wt=wtb, wcol=0, m=16 + g, dst=bt, dtok="bt", dcol=512, ci=ci[0]))
                ci[0] += 1
                if hh == 0:
                    proj_chunk(wsb, wtb, 128, lambda kc: hn[:, kc, 12:16], 4, 6, out=pcc, otok="pb6")
                    I("act", "copy", r=["pb6"], w=["carry%d" % (20 + g)], a=(carry[:, 20 + g, :], pcc[:, 1:4]))
                its.append(dict(kind="C", g=g, i=0, ws=wsb, wt=wtb, wcol=128, m=20 + g, dst=ctt, dtok="ct", dcol=0, ci=ci[0]))
                ci[0] += 1
                run_pipeline(its, o_stages)
                wsz, wtz = load_ws([(lambda w_: w_[:, :, :], w_in_v[:, :, C_Z + g * 512: C_Z + (g + 1) * 512])])
                its = [dict(kind="z", g=g, i=i, ws=wsz, wt=wtz, wcol=i * 128) for i in range(4)]
                run_pipeline(its, o_stages)
                S.begin_capture()
                ssd_group(g)
                blkA = S.end_capture()
                S.begin_capture()
                if g > 0:
                    out_proj_part((g - 1) * 4, yss_prev[0], yss_prev[1])
                pool_branch(g)
                out_proj_part(16 + g * 4, ypool, "ypool")
                blkB = S.end_capture()
                nA0 = len(blkA) // 2
                nB1 = 16 if g > 0 else 0
                rest = blkB[nB1:]
                S.interleave(blkA[:nA0], blkB[:nB1] + rest[:len(rest) // 2])
                S.interleave(blkA[nA0:], rest[len(rest) // 2:])
                yss_prev = (yss, "yss")
                if g == 3:
                    out_proj_part(g * 4, yss, "yss")
        peakM = A.peak
        if dbg:
            for hh in range(2):
                I("sp", "dma_start", r=[("R" if (hh == 0 and NH > 0) else "R%d" % hh)], dma=True, out=dbg_R[:, hh * 16 * HT:(hh + 1) * 16 * HT],
                  in_=R[hh].rearrange("p a b -> p (a b)"))

        S.barrier()
        A.reset(PH)
        mm_banks[0] = [0, 1, 2, 3, 5, 6, 7]
        A.off = M_S
        tF = {"sq": [A.alloc([HT], BF16), A.alloc([HT], BF16)], "sd": A.alloc([HT], F32), "rstd": A.alloc([HT], F32)}
        WD = [A.alloc([8, 512], BF16) for _ in range(2)]
        assert A.off <= M_R
        A.reset(PH)
        hn2 = A.alloc([16, TOK], BF16)
        act = [A.alloc([8, TOK], BF16) for _ in range(2)]
        WGU = [A.alloc([16, 512], BF16) for _ in range(2)]
        sg = [A.alloc([HT], F32) for _ in range(2)]
        ost = [A.alloc([HT], F32) for _ in range(2)]

        RT = [("R" if (NH > 0 and do_own) else "R0"), "R1"]
        for hh in (range(2) if do_ffn else ()):
            norm_half(R[hh], P_NW2, lambda c, hh=hh: hn2[:, c, hh * HT:(hh + 1) * HT], tF, dst_tok="hn2", src_tok="R%d" % hh)

        groups = [(0, 8), (8, 8), (16, 8), (24, 8), (32, 8), (40, 4)] if do_ffn else []
        wgu_rr, wd_rr, sg_rr = [0], [0], [0]
        for G, (fc0, nfc) in enumerate(groups):
            ab = act[G % 2]
            at = "act%d" % (G % 2)
            for pr in range(nfc // 2):
                c0 = (fc0 + pr * 2) * 128
                i = wgu_rr[0] % 2
                wgu_rr[0] += 1
                wgu, wgt_ = WGU[i], "wgu%d" % i
                dma_w(wgu[:, :, 0:256], w_gate_v[:, :, c0:c0 + 256], wgt_)
                dma_w(wgu[:, :, 256:512], w_up_v[:, :, c0:c0 + 256], wgt_)
                for cc in range(2):
                    fl = pr * 2 + cc
                    for hh in range(2):
                        mg = next_mm()
                        pg = pbank[mg][:, :]
                        for kc in range(16):
                            I("pe", "matmul", r=[wgt_, "hn2"], w=["pb%d" % mg],
                              a=(pg, wgu[:, kc, cc * 128:(cc + 1) * 128], hn2[:, kc, hh * HT:(hh + 1) * HT]),
                              start=(kc == 0), stop=(kc == 15))
                        mu = next_mm()
                        pu = pbank[mu][:, :]
                        for kc in range(16):
                            I("pe", "matmul", r=[wgt_, "hn2"], w=["pb%d" % mu],
                              a=(pu, wgu[:, kc, 256 + cc * 128:256 + (cc + 1) * 128], hn2[:, kc, hh * HT:(hh + 1) * HT]),
                              start=(kc == 0), stop=(kc == 15))
                        si = sg_rr[0] % 2
                        sg_rr[0] += 1
                        I("act", "activation", r=["pb%d" % mg], w=["sg%d" % si], out=sg[si], in_=pg, func=AF.Silu)
                        I("dve", "tensor_tensor", r=["sg%d" % si, "pb%d" % mu], w=[at], out=ab[:, fl, hh * HT:(hh + 1) * HT],
                          in0=sg[si], in1=pu, op=ALU.mult)
            for db in range(4):
                i = wd_rr[0] % 2
                wd_rr[0] += 1
                wd, wdt_ = WD[i], "wd%d" % i
                dma_w(wd[:, 0:nfc, :], w_down_v[:, fc0:fc0 + nfc, db * 512:(db + 1) * 512], wdt_)
                for dd in range(4):
                    d = db * 4 + dd
                    for hh in range(2):
                        mmi = next_mm()
                        pm = pbank[mmi][:, :]
                        for f in range(nfc):
                            I("pe", "matmul", r=[wdt_, at], w=["pb%d" % mmi],
                              a=(pm, wd[:, f, dd * 128:(dd + 1) * 128], ab[:, f, hh * HT:(hh + 1) * HT]),
                              start=(f == 0), stop=(f == nfc - 1))
                        I("dve", "tensor_tensor", r=["R%d" % hh, "pb%d" % mmi], w=["R%d" % hh], out=R[hh][:, d, :],
                          in0=R[hh][:, d, :], in1=pm, op=ALU.add)
        oi = 0
        for hh in (range(2) if do_ffn else ()):
            src = R[hh]
            rt = "R%d" % hh
            for c in range(16):
                sqb = tF["sq"][c % 2]
                I("act", "activation", r=[rt], w=["sq%d" % (c % 2)], out=sqb, in_=src[:, c, :], func=AF.Square)
                I("pe", "matmul", r=["ones_bf", "sq%d" % (c % 2)], w=["pb4"], a=(pss, ones_bf, sqb),
                  start=(c == 0), stop=(c == 15))
            I("act", "activation", r=["pb4"], w=["sd"], out=tF["sd"], in_=pss, func=AF.Sqrt, bias=EPS, scale=1.0 / D)
            I("dve", "reciprocal", r=["sd"], w=["rstd"], a=(tF["rstd"], tF["sd"]))
            for c in range(16):
                o = ost[oi % 2]
                ot = "ost%d" % (oi % 2)
                oi += 1
                I("dve", "scalar_tensor_tensor", r=[rt, "prm", "rstd"], w=[ot], out=o, in0=src[:, c, :],
                  scalar=pcol(P_NW3, c), in1=tF["rstd"], op0=ALU.mult, op1=ALU.mult)
                I("sp", "dma_start", r=[ot], dma=True, out=outT_v[:, c, hh * HT:(hh + 1) * HT], in_=o)
        peakF = A.peak
        print("SBUF peaks P/M/F:", peakP, peakM, peakF, " ops:", len(S.ops))
        with ExitStack() as st2:
            S.emit(st2)
        print("waits:", S.nwaits)
    return nc


def _pc(v):
    return np.ascontiguousarray(np.asarray(v, np.float32).reshape(16, 128).T)


def _consts():
    c = np.zeros((128, NCONST), np.float32)
    j = np.arange(128)[:, None]
    l = np.arange(128)[None, :]
    tri = (j <= l).astype(np.float32)
    c[:, K_ONES:K_ONES + 128] = 1.0
    c[:, K_LT:K_LT + 128] = (j > l).astype(np.float32)
    c[:, K_T0:K_T0 + 128] = tri
    c[:, K_T0 + 128:K_T0 + 256] = 1.0
    c[:, K_T1 + 128:K_T1 + 256] = tri
    c[:, K_CAUS:K_CAUS + 128] = tri
    c[:, K_CAUS + 128:K_CAUS + 256] = 1.0
    c[:, K_CAUS + 256:K_CAUS + 384] = tri
    c[:, K_ID:K_ID + 128] = np.eye(128, dtype=np.float32)
    return c


def make_in_maps(x, attn_norm_w, w_in, conv_w, conv_b, dt_bias, a_log, d_skip, ssd_norm_w, pool_w,
                 pool_scale, w_out, ffn_norm_w, w_gate, w_up, w_down, final_norm_w):
    x = np.asarray(x, np.float32)
    xs = x.reshape(SEQ, D)
    consts = _consts()
    base = np.zeros((128, NPAR), np.float32)
    base[:, P_NW1:P_NW1 + 16] = _pc(np.asarray(attn_norm_w)[0])
    base[:, P_NW2:P_NW2 + 16] = _pc(np.asarray(ffn_norm_w)[0])
    base[:, P_NW3:P_NW3 + 16] = _pc(np.asarray(final_norm_w))
    base[:, P_SSDW:P_SSDW + 16] = _pc(np.asarray(ssd_norm_w)[0])
    base[:, P_PSC:P_PSC + 16] = _pc(np.asarray(pool_scale)[0])
    base[:, P_DSK:P_DSK + 16] = _pc(np.repeat(np.asarray(d_skip, np.float32)[0], 64))
    cw = np.asarray(conv_w, np.float32)[0]
    base[:, P_CW:P_CW + 96] = cw.reshape(4, 24, 128).transpose(2, 1, 0).reshape(128, 96)
    base[:, P_CB:P_CB + 24] = np.asarray(conv_b, np.float32)[0].reshape(24, 128).T
    base[:, P_DTB:P_DTB + 32] = np.asarray(dt_bias, np.float32)[0][None, :]
    base[:, P_ALOG:P_ALOG + 32] = np.asarray(a_log, np.float32)[0][None, :]

    w_in2 = np.ascontiguousarray(np.asarray(w_in, np.float32)[0])
    pool_w2 = np.ascontiguousarray(np.asarray(pool_w, np.float32)[0].reshape(4 * 512, 512))
    w_out2 = np.ascontiguousarray(np.asarray(w_out, np.float32)[0])
    w_gate2 = np.ascontiguousarray(np.asarray(w_gate, np.float32)[0])
    w_up2 = np.ascontiguousarray(np.asarray(w_up, np.float32)[0])
    w_down2 = np.ascontiguousarray(np.asarray(w_down, np.float32)[0])

    in_maps = []
    for c in range(NCORES):
        xall = np.zeros((NHB, D, HT), np.float32)
        prm = base.copy()
        for j in range(NHB):
            gh = 2 * c - NPRE + j
            if gh >= 0:
                xall[j] = xs[gh * HT:(gh + 1) * HT].T
                prm[:, P_VALID + j] = 1.0
        for hh in range(2):
            tg = c * TOK + hh * HT + np.arange(16)
            for g, wwin in enumerate((2, 4, 8, 16)):
                o = P_ICNT + hh * 64 + g * 16
                prm[:, o:o + 16] = (1.0 / np.minimum(tg + 1, wwin))[None, :]
        in_maps.append({
            "xall": xall.reshape(NHB * D, HT), "w_in": w_in2, "pool_w": pool_w2, "w_out": w_out2,
            "w_gate": w_gate2, "w_up": w_up2, "w_down": w_down2, "consts": consts, "params": prm,
        })
    return in_maps


_NC_CACHE = {}


def kernel(**inputs):
    in_maps = make_in_maps(**inputs)
    if "nc" not in _NC_CACHE:
        _NC_CACHE["nc"] = build_nc()
    nc = _NC_CACHE["nc"]
    res = run_bass_kernel_spmd(nc, in_maps, core_ids=list(range(NCORES)))
    out = np.empty((SEQ, D), np.float32)
    for c in range(NCORES):
        out[c * TOK:(c + 1) * TOK] = res.results[c]["outT"].T
    return out.reshape(1, SEQ, D)
```
